# Optimizing a Trainium2 kernel written in Bass

```python
import jax, jax.numpy as jnp
from jax import lax
import numpy as np

D_MODEL = 1024
BATCH = 4
SEQ = 8192
DEPTH = 1

CHUNK = 64
N_LEFT_CHUNKS = 8
BAND = (N_LEFT_CHUNKS + 1) * CHUNK
REL_CLIP = 256
D_MIX = D_MODEL
HEAD_DIM = 64
D_RWKV = D_MIX // 2
D_ATTN = D_MIX - D_RWKV
H_RWKV = D_RWKV // HEAD_DIM
H_ATTN = D_ATTN // HEAD_DIM
DECAY_LORA = 64
AAA_LORA = 64
GATE_LORA = 128
D_FF = ((8 * D_MODEL // 3 + 127) // 128) * 128
N_RWKV_COLS = 3 * D_RWKV + DECAY_LORA + AAA_LORA + GATE_LORA
N_IN_COLS = N_RWKV_COLS + 3 * D_ATTN
N_MOD = 9
NORM_EPS = 1e-6
QK_EPS = 1e-6
LNX_EPS = 64e-5

kernel_name = "hybrid_rwkv7_chunkattn_macaron_adaln"


def rms_norm(x, g, eps):
    xf = x.astype(jnp.float32)
    y = xf * lax.rsqrt(jnp.mean(xf * xf, axis=-1, keepdims=True) + eps)
    return (y * g.astype(jnp.float32)).astype(x.dtype)


def modulate(n, shift, scale):
    return n * (1 + scale[:, None, :]) + shift[:, None, :]


def swiglu(n, w1, w3, w2):
    return (jax.nn.silu(n @ w1) * (n @ w3)) @ w2


def rwkv7_scan(r, w, k, v, kk, b):
    def step(state, inp):
        r_t, w_t, k_t, v_t, kk_t, b_t = inp
        sa = jnp.einsum('bhvk,bhk->bhv', state, -kk_t)
        state = (state * w_t[:, :, None, :] + sa[..., None] * b_t[:, :, None, :]
                 + v_t[..., None] * k_t[:, :, None, :])
        y_t = jnp.einsum('bhvk,bhk->bhv', state, r_t)
        return state, y_t

    bsz, _, nh, hd = r.shape
    seq_first = tuple(jnp.moveaxis(a, 1, 0) for a in (r, w, k, v, kk, b))
    state0 = jnp.zeros((bsz, nh, hd, hd), jnp.float32)
    _, ys = lax.scan(step, state0, seq_first)
    return jnp.moveaxis(ys, 0, 1)


def rwkv7_mixer(p, mu, w0, w_decay_up, a0, w_a_up, w_g_up, k_k, k_a, r_k, lnx_g, lnx_b):
    bsz, seq, _ = p.shape
    p = p.astype(jnp.float32)
    p_prev = jnp.pad(p[:, :-1], ((0, 0), (1, 0), (0, 0)))
    p = p + (p_prev - p) * mu.astype(jnp.float32)
    o = 3 * D_RWKV
    r, k, v = p[..., :D_RWKV], p[..., D_RWKV:2 * D_RWKV], p[..., 2 * D_RWKV:o]
    wd = p[..., o:o + DECAY_LORA]
    ad = p[..., o + DECAY_LORA:o + DECAY_LORA + AAA_LORA]
    gd = p[..., o + DECAY_LORA + AAA_LORA:]
    f32 = lambda t: t.astype(jnp.float32)
    w_pre = -jax.nn.softplus(-(f32(w0) + jnp.tanh(wd) @ f32(w_decay_up))) - 0.5
    decay = jnp.exp(-jnp.exp(w_pre))
    a = jax.nn.sigmoid(f32(a0) + ad @ f32(w_a_up))
    g = jax.nn.sigmoid(gd) @ f32(w_g_up)
    heads = lambda t: t.reshape(bsz, seq, H_RWKV, HEAD_DIM)
    kk = heads(k * f32(k_k))
    kk = kk / jnp.maximum(jnp.linalg.norm(kk, axis=-1, keepdims=True), 1e-12)
    k = k * (1 + (a - 1) * f32(k_a))
    r_h, k_h, v_h, a_h, w_h = heads(r), heads(k), heads(v), heads(a), heads(decay)
    y = rwkv7_scan(r_h, w_h, k_h, v_h, kk, kk * a_h)
    mean = jnp.mean(y, axis=-1, keepdims=True)
    var = jnp.mean(jnp.square(y - mean), axis=-1, keepdims=True)
    gn_g = f32(lnx_g).reshape(H_RWKV, HEAD_DIM)
    gn_b = f32(lnx_b).reshape(H_RWKV, HEAD_DIM)
    y = (y - mean) * lax.rsqrt(var + LNX_EPS) * gn_g + gn_b
    bonus = jnp.sum(r_h * k_h * f32(r_k), axis=-1, keepdims=True) * v_h
    return ((y + bonus).reshape(bsz, seq, D_RWKV) * g)


def chunk_band_attention(q, k, v, q_norm_g, k_norm_g, rel_bias):
    bsz, seq, _ = q.shape
    nc = seq // CHUNK
    to_chunks = lambda t: t.reshape(bsz, nc, CHUNK, H_ATTN, HEAD_DIM).transpose(0, 3, 1, 2, 4)
    q = rms_norm(to_chunks(q), q_norm_g, QK_EPS)
    k = rms_norm(to_chunks(k), k_norm_g, QK_EPS)
    v = to_chunks(v)
    padw = ((0, 0), (0, 0), (N_LEFT_CHUNKS, 0), (0, 0), (0, 0))
    k_pad, v_pad = jnp.pad(k, padw), jnp.pad(v, padw)
    k_band = jnp.concatenate([k_pad[:, :, j:j + nc] for j in range(N_LEFT_CHUNKS + 1)], axis=3)
    v_band = jnp.concatenate([v_pad[:, :, j:j + nc] for j in range(N_LEFT_CHUNKS + 1)], axis=3)
    band_j = np.repeat(np.arange(N_LEFT_CHUNKS + 1), CHUNK)
    kj = np.tile(np.arange(CHUNK), N_LEFT_CHUNKS + 1)
    qi = np.arange(CHUNK)[:, None]
    dist = (N_LEFT_CHUNKS - band_j)[None, :] * CHUNK + qi - kj[None, :]
    rel_idx = np.clip(dist, -REL_CLIP, REL_CLIP) + REL_CLIP
    bias = rel_bias[:, rel_idx].astype(jnp.float32)
    valid = (np.arange(nc)[:, None] - N_LEFT_CHUNKS + band_j[None, :]) >= 0
    scores = jnp.einsum('bhcqd,bhckd->bhcqk', q, k_band).astype(jnp.float32) * (HEAD_DIM ** -0.5)
    scores = scores + bias[None, :, None]
    scores = jnp.where(valid[None, None, :, None, :], scores, jnp.finfo(jnp.float32).min)
    probs = jax.nn.softmax(scores, axis=-1).astype(v.dtype)
    out = jnp.einsum('bhcqk,bhckd->bhcqd', probs, v_band)
    return out.transpose(0, 2, 3, 1, 4).reshape(bsz, seq, D_ATTN)


def setup_inputs(seed: int = 0) -> dict:
    key = jax.random.key(seed)
    ks = iter(jax.random.split(key, 32))
    L = DEPTH
    nrm = lambda shape, s: jax.random.normal(next(ks), shape, jnp.float32) * s
    gain = lambda shape: 1.0 + nrm(shape, 0.02)
    return {
        "x": nrm((BATCH, SEQ, D_MODEL), 1.0),
        "c": nrm((BATCH, D_MODEL), 1.0),
        "w_ada": nrm((L, D_MODEL, N_MOD * D_MODEL), D_MODEL ** -0.5),
        "b_ada": nrm((L, N_MOD * D_MODEL), 0.02),
        "norm1_g": gain((L, D_MODEL)),
        "ffn1_w1": nrm((L, D_MODEL, D_FF), D_MODEL ** -0.5),
        "ffn1_w3": nrm((L, D_MODEL, D_FF), D_MODEL ** -0.5),
        "ffn1_w2": nrm((L, D_FF, D_MODEL), D_FF ** -0.5),
        "norm2_g": gain((L, D_MODEL)),
        "w_in": nrm((L, D_MODEL, N_IN_COLS), D_MODEL ** -0.5),
        "mu_shift": jax.random.uniform(next(ks), (L, N_RWKV_COLS), jnp.float32),
        "w0": -2.0 + nrm((L, D_RWKV), 1.0),
        "w_decay_up": nrm((L, DECAY_LORA, D_RWKV), 0.5 * DECAY_LORA ** -0.5),
        "a0": nrm((L, D_RWKV), 0.1),
        "w_a_up": nrm((L, AAA_LORA, D_RWKV), AAA_LORA ** -0.5),
        "w_g_up": nrm((L, GATE_LORA, D_RWKV), GATE_LORA ** -0.5),
        "k_k": 0.85 + nrm((L, D_RWKV), 0.05),
        "k_a": 1.0 + nrm((L, D_RWKV), 0.05),
        "r_k": nrm((L, H_RWKV, HEAD_DIM), 0.1),
        "lnx_g": gain((L, D_RWKV)),
        "lnx_b": nrm((L, D_RWKV), 0.01),
        "q_norm_g": gain((L, HEAD_DIM)),
        "k_norm_g": gain((L, HEAD_DIM)),
        "rel_bias": nrm((L, H_ATTN, 2 * REL_CLIP + 1), 0.1),
        "w_out": nrm((L, D_MIX, D_MODEL), D_MIX ** -0.5),
        "norm3_g": gain((L, D_MODEL)),
        "ffn2_w1": nrm((L, D_MODEL, D_FF), D_MODEL ** -0.5),
        "ffn2_w3": nrm((L, D_MODEL, D_FF), D_MODEL ** -0.5),
        "ffn2_w2": nrm((L, D_FF, D_MODEL), D_FF ** -0.5),
    }


def reference(x, c, w_ada, b_ada, norm1_g, ffn1_w1, ffn1_w3, ffn1_w2, norm2_g, w_in,
              mu_shift, w0, w_decay_up, a0, w_a_up, w_g_up, k_k, k_a, r_k, lnx_g, lnx_b,
              q_norm_g, k_norm_g, rel_bias, w_out, norm3_g, ffn2_w1, ffn2_w3, ffn2_w2):
    bsz = x.shape[0]
    h = x
    for l in range(DEPTH):
        mod = (jax.nn.silu(c) @ w_ada[l] + b_ada[l]).reshape(bsz, N_MOD, D_MODEL)
        sh1, sc1, g1, sh2, sc2, g2, sh3, sc3, g3 = [mod[:, i] for i in range(N_MOD)]
        n1 = modulate(rms_norm(h, norm1_g[l], NORM_EPS), sh1, sc1)
        h = h + 0.5 * g1[:, None, :] * swiglu(n1, ffn1_w1[l], ffn1_w3[l], ffn1_w2[l])
        n2 = modulate(rms_norm(h, norm2_g[l], NORM_EPS), sh2, sc2)
        proj = n2 @ w_in[l]
        y_rwkv = rwkv7_mixer(proj[..., :N_RWKV_COLS], mu_shift[l], w0[l], w_decay_up[l],
                             a0[l], w_a_up[l], w_g_up[l], k_k[l], k_a[l], r_k[l],
                             lnx_g[l], lnx_b[l]).astype(h.dtype)
        o = N_RWKV_COLS
        y_attn = chunk_band_attention(proj[..., o:o + D_ATTN], proj[..., o + D_ATTN:o + 2 * D_ATTN],
                                      proj[..., o + 2 * D_ATTN:], q_norm_g[l], k_norm_g[l],
                                      rel_bias[l]).astype(h.dtype)
        mixed = jnp.concatenate([y_rwkv, y_attn], axis=-1) @ w_out[l]
        h = h + g2[:, None, :] * mixed
        n3 = modulate(rms_norm(h, norm3_g[l], NORM_EPS), sh3, sc3)
        h = h + 0.5 * g3[:, None, :] * swiglu(n3, ffn2_w1[l], ffn2_w3[l], ffn2_w2[l])
    return h
```

```python
import contextlib
import numpy as np
import ml_dtypes
import concourse.bass as bass
import concourse.mybir as mybir
from concourse.bass_utils import run_bass_kernel_spmd

F32 = mybir.dt.float32
BF16 = mybir.dt.bfloat16
AF = mybir.ActivationFunctionType
ALU = mybir.AluOpType
AX = mybir.AxisListType

ENGS = ["pe", "act", "dve", "pool", "sp"]

D = 1024
DFF = 2816
NFC = DFF // 128
NIN = 3328
EPS = 1e-6


class Res:
    __slots__ = ("name", "w", "r")

    def __init__(self, name):
        self.name = name
        self.w = None
        self.r = {}


class Prog:
    def __init__(self, nc, n_dma_sems=8, self_sync=True):
        self.nc = nc
        self.ops = {e: [] for e in ENGS}
        self.cnt = {e: 0 for e in ENGS}
        self.seen = {e: {} for e in ENGS}
        self.pending = {e: [] for e in ENGS}
        self.self_sync = self_sync
        self.n_dma_sems = n_dma_sems
        self.dma_cnt = {}
        self.dma_rr = {e: 0 for e in ENGS}
        self.nops = 0
        self.raw = []

    def _need(self, eng, waits, tok):
        if tok is None:
            return
        key, val = tok
        if key == eng and (eng == "pe" or not self.self_sync):
            return
        if self.seen[eng].get(key, 0) >= val:
            return
        if waits.get(key, 0) < val:
            waits[key] = val

    COST = {"pe": 0.13, "act": 0.35, "dve": 0.40, "pool": 0.60, "sp": 0.10}

    def op(self, eng, fn, reads=(), writes=(), dma=False, inc=True, cost=None):
        self.raw.append((eng, fn, tuple(reads), tuple(writes), dma, inc, cost if cost is not None else self.COST[eng]))

    def barrier(self):
        self.raw.append(None)

    def _schedule_segment(self, seg):
        import heapq
        n = len(seg)
        deps = [set() for _ in range(n)]
        lastw, readers = {}, {}
        for i, (eng, fn, reads, writes, dma, inc, cost) in enumerate(seg):
            for r in reads:
                if id(r) in lastw:
                    deps[i].add(lastw[id(r)])
            for r in writes:
                if id(r) in lastw:
                    deps[i].add(lastw[id(r)])
                for j in readers.get(id(r), ()):
                    deps[i].add(j)
            for r in reads:
                readers.setdefault(id(r), []).append(i)
            for r in writes:
                lastw[id(r)] = i
                readers[id(r)] = []
            deps[i].discard(i)
        succ = [[] for _ in range(n)]
        ndep = [len(d) for d in deps]
        for i, d in enumerate(deps):
            for j in d:
                succ[j].append(i)
        LAT_X, LAT_S, DMA_LAT = 0.9, 0.15, 2.5
        ready_t = [0.0] * n
        finish = [0.0] * n
        start = [0.0] * n
        heaps = {e: [] for e in ENGS}
        free = {e: 0.0 for e in ENGS}
        for i in range(n):
            if ndep[i] == 0:
                heapq.heappush(heaps[seg[i][0]], (0.0, i))
        done = 0
        while done < n:
            best = None
            for e in ENGS:
                if heaps[e]:
                    rt, i = heaps[e][0]
                    st = max(rt, free[e])
                    if best is None or (st, i) < (best[0], best[2]):
                        best = (st, e, i)
            st, e, i = best
            heapq.heappop(heaps[e])
            eng, fn, reads, writes, dma, inc, cost = seg[i]
            start[i] = st
            free[e] = st + cost
            finish[i] = st + (DMA_LAT if dma else cost)
            done += 1
            for k in succ[i]:
                lat = LAT_S if seg[k][0] == e and not dma else LAT_X
                ready_t[k] = max(ready_t[k], finish[i] + lat)
                ndep[k] -= 1
                if ndep[k] == 0:
                    heapq.heappush(heaps[seg[k][0]], (ready_t[k], k))
        order = sorted(range(n), key=lambda i: (start[i], i))
        self.est_time += max(finish) if n else 0.0
        return [seg[i] for i in order]

    def finalize(self, reorder=True):
        self.est_time = 0.0
        seg = []
        for item in self.raw + [None]:
            if item is None:
                ops = self._schedule_segment(seg) if (reorder and seg) else seg
                for (eng, fn, reads, writes, dma, inc, cost) in ops:
                    self._emit_op(eng, fn, reads, writes, dma, inc)
                self._emit_barrier()
                seg = []
            else:
                seg.append(item)

    def _emit_op(self, eng, fn, reads=(), writes=(), dma=False, inc=True):
        waits = {}
        for r in reads:
            self._need(eng, waits, r.w)
        for r in writes:
            self._need(eng, waits, r.w)
            for tok in r.r.values():
                self._need(eng, waits, tok)
        if dma:
            i = self.dma_rr[eng]
            self.dma_rr[eng] = (i + 1) % self.n_dma_sems
            key = ("dma", eng, i)
            prev = self.dma_cnt.get(key, 0)
            if prev:
                self._need(eng, waits, (key, prev))
            tok = (key, prev + 16)
            self.dma_cnt[key] = prev + 16
        elif inc:
            self.cnt[eng] += 1
            tok = (eng, self.cnt[eng])
        else:
            tok = None
        for k, v in waits.items():
            self.seen[eng][k] = v
        self.ops[eng].append((sorted(waits.items(), key=str), fn, tok))
        self.nops += 1
        if tok is None:
            self.pending[eng].append((tuple(reads), tuple(writes)))
            return
        allr, allw = list(reads), list(writes)
        if not dma:
            for (pr, pw) in self.pending[eng]:
                allr += pr
                allw += pw
            self.pending[eng] = []
        rkey = tok[0]
        for r in allr:
            r.r[rkey] = tok
        for r in allw:
            r.w = tok
            r.r = {}

    def _emit_barrier(self):
        toks = [(e, self.cnt[e]) for e in ENGS if self.cnt[e]]
        toks += [(k, v) for k, v in self.dma_cnt.items()]
        for e in ENGS:
            waits = {}
            for t in toks:
                self._need(e, waits, t)
            if waits:
                for k, v in waits.items():
                    self.seen[e][k] = v
                self.ops[e].append((sorted(waits.items(), key=str), None, None))

    def emit(self, reorder=True):
        self.finalize(reorder)
        nc = self.nc
        with contextlib.ExitStack() as es:
            sems = {}
            for e in ENGS:
                sems[e] = es.enter_context(nc.semaphore("s_" + e))
            for key in self.dma_cnt:
                sems[key] = es.enter_context(nc.semaphore("d_%s_%d" % (key[1], key[2])))
            block = es.enter_context(nc.Block())

            def run(eng_name):
                def body(eng):
                    for waits, fn, tok in self.ops[eng_name]:
                        for k, v in waits:
                            eng.wait_ge(sems[k], v)
                        if fn is None:
                            continue
                        ins = fn(eng)
                        if tok is not None:
                            key, _ = tok
                            ins.then_inc(sems[key], 16 if isinstance(key, tuple) else 1)
                return body

            block.tensor(run("pe"))
            block.scalar(run("act"))
            block.vector(run("dve"))
            block.gpsimd(run("pool"))
            block.sync(run("sp"))


class Arena:
    def __init__(self, nc, base, limit):
        self.nc, self.base, self.limit, self.off, self.n = nc, base, limit, base, 0
        self.marks = []
        self.names = {}

    def alloc(self, name, free_elems, dtype):
        nb = free_elems * (2 if dtype == BF16 else 4)
        nb = (nb + 63) // 64 * 64
        assert self.off + nb <= self.limit, "SBUF overflow at %s: %d + %d > %d" % (name, self.off, nb, self.limit)
        self.n += 1
        t = self.nc.alloc_sbuf_tensor_at("%s_%d" % (name, self.n), [128, free_elems], dtype, offset=self.off)
        self.off += nb
        self.names.setdefault(name, []).append("%s_%d" % (name, self.n))
        return t

    def mark(self):
        self.marks.append(self.off)

    def release(self):
        self.off = self.marks.pop()


def sb(t, p0, np_, off, dims):
    F = 1
    for s in t.shape[1:]:
        F *= s
    return bass.AP(t, p0 * F + off, [[F, np_]] + [list(d) for d in dims])


COLS = {}
_o = 0
for _n, _w in [("c", 8), ("b_ada", 72), ("n1g", 8), ("n2g", 8), ("n3g", 8), ("mu", 15), ("w0", 4), ("a0", 4),
               ("k_k", 4), ("k_a", 4), ("r_k", 4), ("lnx_g", 4), ("lnx_b", 4), ("qg", 1), ("kg", 1)]:
    COLS[_n] = (_o, _w)
    _o += _w
NCOL = _o

MG = 256
NCH = MG // 64
C_NDL, C_NSU, C_UI, C_BD, C_MRES = 256, 384, 512, 640, 768
C_NDU, C_UI2, C_EM = 768 + MG, 768 + MG + 128, 768 + MG + 256
NCONST = 768 + MG + 384
C0 = float(np.exp(-0.5))
NEG = -30000.0


class K:
    pass


def build(NT, dbg=None, upto="all", flags=()):
    assert NT % 512 == 0
    NG = NT // 512
    nc = bass.Bass("TRN2", target_bir_lowering=False)
    P = Prog(nc, self_sync=("selfsync_off" not in flags))
    dram = {}

    def din(name, shape, dt=F32):
        dram[name] = nc.dram_tensor(name, list(shape), dt, kind="ExternalInput")
        return dram[name]

    x_d = din("x", [NT, D])
    wada_d = din("wada", [128, 8, 9 * D])
    cols_d = din("cols", [128, NCOL])
    consts_d = din("consts", [128, NCONST])
    f1w1_d = din("f1w1", [128, 8 * DFF])
    f1w3_d = din("f1w3", [128, 8 * DFF])
    f1w2_d = din("f1w2", [128, NFC * D])
    win_d = din("win", [128, 8 * NIN])
    wout_d = din("wout", [128, 8 * D])
    wdu_d = din("wdu", [64, 512])
    wau_d = din("wau", [64, 512])
    wgu_d = din("wgu", [128, 512])
    biasT_d = din("biasT", [128, 8 * 640])
    f2w1_d = din("f2w1", [128, 8 * DFF])
    f2w3_d = din("f2w3", [128, 8 * DFF])
    f2w2_d = din("f2w2", [128, NFC * D])
    out_d = nc.dram_tensor("out", [NT, D], F32, kind="ExternalOutput")
    h1_d = nc.dram_tensor("h1s", [NT, D], F32)
    h2_d = nc.dram_tensor("h2s", [NT, D], F32)
    dbg_d = {}
    if dbg:
        for name, shape in dbg.items():
            dbg_d[name] = nc.dram_tensor("dbg_" + name, list(shape), F32, kind="ExternalOutput")

    A = Arena(nc, 16384, 16384 + 212736)
    cols = A.alloc("cols", NCOL, F32)
    consts = A.alloc("consts", NCONST, F32)
    modT = A.alloc("modT", 72, F32)
    Gs = A.alloc("Gs", 24, F32)
    gcol = A.alloc("gcol", 24, F32)
    sc = A.alloc("sc", 8, F32)
    r_cols, r_consts, r_modT, r_Gs, r_gcol, r_sc = [Res(n) for n in ("cols", "consts", "modT", "Gs", "gcol", "sc")]

    def col(name, j=0, w=1, p0=0, np_=128):
        o, _ = COLS[name]
        return sb(cols, p0, np_, o + j, [[1, w]])

    ident = sb(consts, 0, 128, 0, [[1, 128]])
    ones = sb(consts, 0, 128, 128, [[1, 128]])

    P.op("sp", lambda e: e.dma_start(out=cols[:], in_=cols_d.ap()), writes=[r_cols], dma=True)
    P.op("sp", lambda e: e.dma_start(out=consts[:], in_=consts_d.ap()), writes=[r_consts], dma=True)

    psum = [nc.alloc_psum_tensor("ps%d" % i, [128, 512], F32) for i in range(8)]
    r_ps = [Res("ps%d" % i) for i in range(8)]
    r_psq = [[Res("ps%d_%d" % (i, q)) for q in range(4)] for i in range(8)]

    A.mark()
    wst = [A.alloc("wst", 8 * 512, F32) for _ in range(2)]
    r_wst = [Res("wst0"), Res("wst1")]
    r_diag = [Res("diag0"), Res("diag1")]
    P.op("act", lambda e: e.activation(out=sc[:], in_=col("c", 0, 8), func=AF.Silu), reads=[r_cols], writes=[r_sc])
    for slab in range(18):
        s = slab % 2
        P.op("sp", lambda e, slab=slab, s=s: e.dma_start(
            out=sb(wst[s], 0, 128, 0, [[512, 8], [1, 512]]), in_=wada_d.ap()[:, :, slab * 512:(slab + 1) * 512]),
            writes=[r_wst[s]], dma=True)
        for cb in range(4):
            j = slab * 4 + cb
            for kc in range(8):
                P.op("pe", lambda e, s=s, cb=cb, kc=kc, j=j: e.matmul(
                    sb(psum[0], 0, 128, j, [[1, 1]]),
                    sb(wst[s], 0, 128, kc * 512 + cb * 128, [[1, 128]]),
                    sb(sc, 0, 128, kc, [[1, 1]]), start=(kc == 0), stop=(kc == 7)),
                    reads=[r_wst[s], r_sc], writes=[r_ps[0]], inc=(kc == 7))
    P.op("dve", lambda e: e.tensor_tensor(out=modT[:], in0=sb(psum[0], 0, 128, 0, [[1, 72]]), in1=col("b_ada", 0, 72),
                                          op=ALU.add), reads=[r_ps[0], r_cols], writes=[r_modT])
    for n in range(3):
        gname = ("n1g", "n2g", "n3g")[n]
        P.op("dve", lambda e, n=n, gname=gname: e.scalar_tensor_tensor(
            out=sb(Gs, 0, 128, n * 8, [[1, 8]]), in0=sb(modT, 0, 128, (3 * n + 1) * 8, [[1, 8]]), scalar=1.0,
            in1=col(gname, 0, 8), op0=ALU.add, op1=ALU.mult), reads=[r_modT, r_cols], writes=[r_Gs])
        P.op("dve", lambda e, n=n: e.tensor_scalar(
            out=sb(gcol, 0, 128, n * 8, [[1, 8]]), in0=sb(modT, 0, 128, (3 * n + 2) * 8, [[1, 8]]),
            scalar1=(1.0 if n == 1 else 0.5), scalar2=None, op0=ALU.mult), reads=[r_modT], writes=[r_gcol])

    def shiftcol(n, kc):
        return sb(modT, 0, 128, 3 * n * 8 + kc, [[1, 1]])

    def Gcol(n, kc):
        return sb(Gs, 0, 128, n * 8 + kc, [[1, 1]])

    A.release()
    P.barrier()

    def make_gate_bc(n, gate_t, r_gate_t):
        A.mark()
        diag = [A.alloc("diag", 512, F32) for _ in range(2)]
        for half in range(2):
            s = half
            for q in range(4):
                kc = half * 4 + q
                P.op("dve", lambda e, s=s, q=q, kc=kc: e.tensor_scalar(
                    out=sb(diag[s], 0, 128, q * 128, [[1, 128]]), in0=ident,
                    scalar1=sb(gcol, 0, 128, n * 8 + kc, [[1, 1]]), scalar2=None, op0=ALU.mult),
                    reads=[r_gcol, r_consts], writes=[r_diag[s]])
            P.op("pe", lambda e, s=s: e.matmul(psum[1][:], ones, diag[s][:], start=True, stop=True),
                 reads=[r_diag[s], r_consts], writes=[r_ps[1]])
            P.op("act", lambda e, half=half: e.activation(
                out=sb(gate_t, 0, 128, half * 512, [[1, 512]]), in_=psum[1][:], func=AF.Copy),
                reads=[r_ps[1]], writes=[r_gate_t])
        A.release()
        P.barrier()

    if dbg and "modT" in dbg:
        P.op("sp", lambda e: e.dma_start(out=dbg_d["modT"].ap(), in_=modT[:]), reads=[r_modT], dma=True)

    def ffn_phase(nidx, src_d, dst_d, w1_d, w3_d, w2_d):
        A.mark()
        w1b = A.alloc("w1b", 8 * DFF, BF16)
        w3b = A.alloc("w3b", 8 * DFF, BF16)
        w2b = A.alloc("w2b", NFC * D, BF16)
        r_w = {"w1": Res("w1"), "w3": Res("w3"), "w2": Res("w2")}
        A.mark()
        stg = [A.alloc("stg", DFF, F32) for _ in range(3)]
        r_stg = [Res("stg%d" % i) for i in range(3)]
        k = 0
        for (wd, wb, rw) in ((w1_d, w1b, r_w["w1"]), (w3_d, w3b, r_w["w3"]), (w2_d, w2b, r_w["w2"])):
            for ch in range(8):
                s = k % 3
                P.op("sp", lambda e, wd=wd, ch=ch, s=s: e.dma_start(
                    out=stg[s][:], in_=wd.ap()[:, ch * DFF:(ch + 1) * DFF]), writes=[r_stg[s]], dma=True)
                ceng = ("pool", "dve", "act")[k % 3]
                if ceng == "act":
                    P.op("act", lambda e, wb=wb, ch=ch, s=s: e.activation(
                        out=sb(wb, 0, 128, ch * DFF, [[1, DFF]]), in_=stg[s][:], func=AF.Copy),
                        reads=[r_stg[s]], writes=[rw])
                else:
                    P.op(ceng, lambda e, wb=wb, ch=ch, s=s: e.tensor_copy(
                        out=sb(wb, 0, 128, ch * DFF, [[1, DFF]]), in_=stg[s][:]),
                        reads=[r_stg[s]], writes=[rw])
                k += 1
        A.release()
        P.barrier()
        A.mark()
        gate_t = A.alloc("gate_bc", D, F32)
        r_gate_t = Res("gate")
        make_gate_bc(nidx, gate_t, r_gate_t)
        nT = A.alloc("nT", 8 * 512, BF16)
        gT = A.alloc("gT", NFC * 512, BF16)
        r_nT, r_gT = Res("nT"), Res("gT")
        hT = [A.alloc("hT", D, F32) for _ in range(2)]
        hB = [A.alloc("hB", D, F32) for _ in range(2)]
        zb = [A.alloc("zb", D, BF16) for _ in range(2)]
        s1 = [A.alloc("s1", 512, F32) for _ in range(2)]
        tmp = [A.alloc("tmp", 512, F32) for _ in range(2)]
        stat = A.alloc("stat", 16, F32)
        r_hT = [Res("hT0"), Res("hT1")]
        r_hB = [Res("hB0"), Res("hB1")]
        r_zb = [Res("zb0"), Res("zb1")]
        r_stat = [Res("stat0"), Res("stat1")]
        r_s1 = [Res("s10"), Res("s11")]
        r_tmp = [Res("tmp0"), Res("tmp1")]
        tp_bf = psum[0][:].bitcast(BF16)
        cnt = {"t": 0, "a": 0, "b": 0}

        def stageT(g):
            for st in range(4):
                i = cnt["t"] % 2
                cnt["t"] += 1
                tok0 = g * 512 + st * 128
                P.op("sp", lambda e, i=i, tok0=tok0: e.dma_start(out=hT[i][:], in_=src_d.ap()[tok0:tok0 + 128, :]),
                     writes=[r_hT[i]], dma=True)
                P.op("act", lambda e, i=i: e.activation(out=zb[i][:], in_=hT[i][:], func=AF.Square,
                                                         accum_out=sb(stat, 0, 128, i * 4, [[1, 1]])),
                     reads=[r_hT[i]], writes=[r_zb[i], r_stat[i]])
                P.op("act", lambda e, i=i: e.activation(
                    out=sb(stat, 0, 128, i * 4 + 1, [[1, 1]]), in_=sb(stat, 0, 128, i * 4, [[1, 1]]),
                    func=AF.Sqrt, scale=1.0 / D, bias=sb(epsc, 0, 128, 0, [[1, 1]])),
                    reads=[r_stat[i], r_epsc], writes=[r_stat[i]])
                P.op("dve", lambda e, i=i: e.reciprocal(
                    out=sb(stat, 0, 128, i * 4 + 2, [[1, 1]]), in_=sb(stat, 0, 128, i * 4 + 1, [[1, 1]])),
                    reads=[r_stat[i]], writes=[r_stat[i]])
                P.op("dve", lambda e, i=i: e.tensor_scalar(
                    out=zb[i][:], in0=hT[i][:], scalar1=sb(stat, 0, 128, i * 4 + 2, [[1, 1]]), scalar2=None,
                    op0=ALU.mult), reads=[r_hT[i], r_stat[i]], writes=[r_zb[i]])
                for kc in range(8):
                    P.op("pe", lambda e, i=i, kc=kc: e.transpose(
                        tp_bf[:, kc * 128:(kc + 1) * 128], sb(zb[i], 0, 128, kc * 128, [[1, 128]]),
                        ident_bf), reads=[r_zb[i], r_identb], writes=[r_ps[0]], inc=(kc == 7))
                for kc in range(8):
                    P.op("act", lambda e, kc=kc, st=st: e.activation(
                        out=sb(nT, 0, 128, kc * 512 + st * 128, [[1, 128]]), in_=tp_bf[:, kc * 128:(kc + 1) * 128],
                        func=AF.Identity, scale=Gcol(nidx, kc), bias=shiftcol(nidx, kc)),
                        reads=[r_ps[0], r_Gs, r_modT], writes=[r_nT], inc=(kc == 7))

        def stageA(g):
            for f in range(NFC):
                i = cnt["a"] % 2
                cnt["a"] += 1
                for (wb, rw, bank) in ((w1b, r_w["w1"], 1 + i), (w3b, r_w["w3"], 3 + i)):
                    for kc in range(8):
                        P.op("pe", lambda e, wb=wb, kc=kc, f=f, bank=bank: e.matmul(
                            psum[bank][:], sb(wb, 0, 128, kc * DFF + f * 128, [[1, 128]]),
                            sb(nT, 0, 128, kc * 512, [[1, 512]]), start=(kc == 0), stop=(kc == 7)),
                            reads=[rw, r_nT], writes=[r_ps[bank]], inc=(kc == 7))
                P.op("act", lambda e, i=i: e.activation(out=s1[i][:], in_=psum[1 + i][:], func=AF.Silu),
                     reads=[r_ps[1 + i]], writes=[r_s1[i]])
                P.op("dve", lambda e, i=i, f=f: e.tensor_tensor(
                    out=sb(gT, 0, 128, f * 512, [[1, 512]]), in0=s1[i][:], in1=psum[3 + i][:], op=ALU.mult),
                    reads=[r_s1[i], r_ps[3 + i]], writes=[r_gT])

        def stageB(g):
            for st in range(4):
                i = cnt["b"] % 2
                cnt["b"] += 1
                tok0 = g * 512 + st * 128
                P.op("sp", lambda e, i=i, tok0=tok0: e.dma_start(out=hB[i][:], in_=src_d.ap()[tok0:tok0 + 128, :]),
                     writes=[r_hB[i]], dma=True)
                for half in range(2):
                    bank = 5 + half
                    for f in range(NFC):
                        P.op("pe", lambda e, f=f, st=st, half=half, bank=bank: e.matmul(
                            psum[bank][:], sb(gT, 0, 128, f * 512 + st * 128, [[1, 128]]),
                            sb(w2b, 0, 128, f * D + half * 512, [[1, 512]]), start=(f == 0), stop=(f == NFC - 1)),
                            reads=[r_gT, r_w["w2"]], writes=[r_ps[bank]], inc=(f == NFC - 1))
                    P.op("dve", lambda e, half=half, bank=bank: e.tensor_tensor(
                        out=tmp[half][:], in0=psum[bank][:], in1=sb(gate_t, 0, 128, half * 512, [[1, 512]]),
                        op=ALU.mult), reads=[r_ps[bank], r_gate_t], writes=[r_tmp[half]])
                    P.op("pool", lambda e, i=i, half=half: e.tensor_tensor(
                        out=sb(hB[i], 0, 128, half * 512, [[1, 512]]), in0=sb(hB[i], 0, 128, half * 512, [[1, 512]]),
                        in1=tmp[half][:], op=ALU.add), reads=[r_tmp[half], r_hB[i]], writes=[r_hB[i]])
                P.op("sp", lambda e, i=i, tok0=tok0: e.dma_start(out=dst_d.ap()[tok0:tok0 + 128, :], in_=hB[i][:]),
                     reads=[r_hB[i]], writes=[r_dst], dma=True)

        for g in range(NG + 1):
            if g < NG:
                stageT(g)
            if g >= 1:
                stageB(g - 1)
            if g < NG:
                stageA(g)
        A.release()
        A.release()
        P.barrier()


    def mixer_phase(src_d, dst_d):
        NGm = NT // MG
        NST = MG // 128
        A.mark()
        winb = A.alloc("winb", 8 * NIN, BF16)
        woutb = A.alloc("woutb", 8 * D, BF16)
        wdub = A.alloc("wdub", 512, BF16)
        waub = A.alloc("waub", 512, BF16)
        wgub = A.alloc("wgub", 512, BF16)
        biasT = A.alloc("biasT", 8 * 640, F32)
        mhalf = A.alloc("mhalf", MG, F32)
        onesb = A.alloc("onesb", 128, BF16)
        dcols = A.alloc("dcols", 16, F32)
        r_win, r_wout, r_lora, r_biasT, r_mhalf, r_onesb, r_dcols = [Res(n) for n in (
            "win", "wout", "lora", "biasT", "mhalf", "onesb", "dcols")]
        A.mark()
        stg = [A.alloc("mstg", NIN, F32) for _ in range(2)]
        r_stg = [Res("mstg0"), Res("mstg1")]
        kk_ = [0]

        def load_cast(src_ap, dst_ap, nparts, ncols, rdst):
            s_ = kk_[0] % 2
            eng = ("pool", "dve")[kk_[0] % 2]
            kk_[0] += 1
            P.op("sp", lambda e: e.dma_start(out=sb(stg[s_], 0, nparts, 0, [[1, ncols]]), in_=src_ap),
                 writes=[r_stg[s_]], dma=True)
            P.op(eng, lambda e: e.tensor_copy(out=dst_ap, in_=sb(stg[s_], 0, nparts, 0, [[1, ncols]])),
                 reads=[r_stg[s_]], writes=[rdst])

        for kc in range(8):
            load_cast(win_d.ap()[:, kc * NIN:(kc + 1) * NIN], sb(winb, 0, 128, kc * NIN, [[1, NIN]]), 128, NIN, r_win)
        for ch in range(4):
            load_cast(wout_d.ap()[:, ch * 2048:(ch + 1) * 2048], sb(woutb, 0, 128, ch * 2048, [[1, 2048]]), 128, 2048,
                      r_wout)
        load_cast(wdu_d.ap(), sb(wdub, 0, 64, 0, [[1, 512]]), 64, 512, r_lora)
        load_cast(wau_d.ap(), sb(waub, 0, 64, 0, [[1, 512]]), 64, 512, r_lora)
        load_cast(wgu_d.ap(), sb(wgub, 0, 128, 0, [[1, 512]]), 128, 512, r_lora)
        P.op("sp", lambda e: e.dma_start(out=biasT[:], in_=biasT_d.ap()), writes=[r_biasT], dma=True)
        P.op("pool", lambda e: e.memset(mhalf[:], -0.5), writes=[r_mhalf])
        P.op("pool", lambda e: e.memset(onesb[:], 1.0), writes=[r_onesb])
        P.op("dve", lambda e: e.tensor_scalar(out=sb(dcols, 0, 128, 0, [[1, 4]]), in0=col("w0", 0, 4), scalar1=0.5,
                                              scalar2=None, op0=ALU.mult), reads=[r_cols], writes=[r_dcols])
        P.op("dve", lambda e: e.tensor_scalar(out=sb(dcols, 0, 128, 4, [[1, 4]]), in0=col("a0", 0, 4), scalar1=0.5,
                                              scalar2=None, op0=ALU.mult), reads=[r_cols], writes=[r_dcols])
        P.op("dve", lambda e: e.tensor_scalar(out=sb(dcols, 0, 128, 8, [[1, 4]]), in0=col("k_a", 0, 4), scalar1=-1.0,
                                              scalar2=1.0, op0=ALU.mult, op1=ALU.add), reads=[r_cols], writes=[r_dcols])
        P.op("dve", lambda e: e.tensor_scalar(out=sb(dcols, 0, 128, 12, [[1, 1]]), in0=col("qg", 0, 1), scalar1=0.125,
                                              scalar2=None, op0=ALU.mult), reads=[r_cols], writes=[r_dcols])
        A.release()
        P.barrier()

        def dcol(j):
            return sb(dcols, 0, 128, j, [[1, 1]])

        gate_t = A.alloc("gate_bc", D, F32)
        r_gate_t = Res("gate")
        make_gate_bc(1, gate_t, r_gate_t)

        def cst(c0, n=128, p0=0, np_=128):
            return sb(consts, p0, np_, c0, [[1, n]])

        carry = A.alloc("carry", 16, F32)
        Zb = A.alloc("Zb", 4 * 128, BF16)
        fKR = A.alloc("fKR", NCH * 256, BF16)
        fB = A.alloc("fB", NCH * 128, BF16)
        fK = A.alloc("fK", NCH * 128, BF16)
        fV = A.alloc("fV", NCH * 128, BF16)
        fKW = A.alloc("fKW", NCH * 128, BF16)
        fBW = A.alloc("fBW", NCH * 128, BF16)
        r_carry, r_Zb = Res("carry"), [Res("Zb%d" % j) for j in range(4)]
        r_fKR, r_fB, r_fK, r_fV, r_fKW, r_fBW = [Res(n) for n in ("fKR", "fB", "fK", "fV", "fKW", "fBW")]
        for t_, r_ in ((carry, r_carry), (Zb, r_Zb[0]), (fKR, r_fKR), (fB, r_fB), (fK, r_fK), (fV, r_fV),
                       (fKW, r_fKW), (fBW, r_fBW)):
            P.op("pool", lambda e, t_=t_: e.memset(t_[:], 0.0), writes=[r_] if r_ is not r_Zb[0] else r_Zb)
        tV = A.alloc("tV", NCH * 128, BF16)
        tKW = A.alloc("tKW", NCH * 128, BF16)
        tBW = A.alloc("tBW", NCH * 128, BF16)
        r_tV, r_tKW, r_tBW = Res("tV"), Res("tKW"), Res("tBW")
        Nn = A.alloc("Nn", NCH * 128, BF16)
        NtArb = A.alloc("NtArb", NCH * 256, BF16)
        AkArk = A.alloc("AkArk", NCH * 256, BF16)
        Xb = A.alloc("Xb", NCH * 384, BF16)
        Esb = A.alloc("Esb", NCH * 128, BF16)
        nGT = A.alloc("nGT", NCH * 128, BF16)
        r_Esb, r_nGT = Res("Esb"), Res("nGT")
        nGTmI = A.alloc("nGTmI", NCH * 128, BF16)
        Xit = A.alloc("Xit", NCH * 256, BF16)
        r_nGTmI, r_Xit = Res("nGTmI"), Res("Xit")
        PP = [A.alloc("PP", NCH * 256, BF16) for _ in range(2)]
        Reff = A.alloc("Reff", NCH * 128, BF16)
        Mc = A.alloc("Mc", NCH * 128, BF16)
        WLt = A.alloc("WLt", 8, F32)
        r_Nn, r_NtArb, r_AkArk, r_Xb, r_Reff, r_Mc, r_WLt = [Res(n) for n in (
            "Nn", "NtArb", "AkArk", "Xb", "Reff", "Mc", "WLt")]
        r_PP = [Res("PP0"), Res("PP1")]
        yT = A.alloc("yT", MG, F32)
        r_yT = Res("yT")
        ymixT = A.alloc("ymixT", 8 * MG, BF16)
        r_ymix = [Res("ymix%d" % i) for i in range(8)]
        qT = A.alloc("qT", 4 * MG, BF16)
        r_qT = [Res("qT%d" % i) for i in range(4)]
        KT = A.alloc("KT", 4 * 1024, BF16)
        r_KT = Res("KT")
        Vr = A.alloc("Vr", 8 * 512, BF16)
        r_Vr = Res("Vr")
        scb = [A.alloc("scb", 640, F32) for _ in range(2)]
        prb = [A.alloc("prb", 640, BF16) for _ in range(2)]
        rec = A.alloc("rec", 128, F32)
        r_scb, r_prb, r_rec = [Res("scb0"), Res("scb1")], [Res("prb0"), Res("prb1")], Res("rec")
        hT = [A.alloc("hT", D, F32) for _ in range(2)]
        zb = [A.alloc("zb", D, BF16) for _ in range(2)]
        stat = A.alloc("stat", 16, F32)
        r_hT, r_zb, r_stat = [Res("hT0"), Res("hT1")], [Res("zb0"), Res("zb1")], [Res("st0"), Res("st1")]
        nT = A.alloc("nT", 8 * MG, BF16)
        r_nT = Res("nT")
        thb = A.alloc("thb", MG, BF16)
        padb = A.alloc("padb", MG, BF16)
        sgdb = A.alloc("sgdb", MG, BF16)
        r_thb, r_padb, r_sgdb = Res("thb"), Res("padb"), Res("sgdb")
        pbuf = [A.alloc("pbuf", MG + 8, F32) for _ in range(2)]
        r_pbuf = [Res("pbuf0"), Res("pbuf1")]
        mixtmp = A.alloc("mixtmp", MG, F32)
        r_mixtmp = Res("mixtmp")
        NW = 13
        Wt = [A.alloc("W%d" % i, MG, F32) for i in range(NW + 3)]
        r_W = [Res("W%d" % i) for i in range(NW + 3)]
        R_, Kx, Vx = NW, NW + 1, NW + 2
        tmpo = [A.alloc("tmpo", 512, F32) for _ in range(2)]
        r_tmpo = [Res("tmpo0"), Res("tmpo1")]

        def W(i, p0=0, np_=128, c0=0, n=MG):
            return sb(Wt[i], p0, np_, c0, [[1, n]])

        def psr(bank, c0, c1):
            return [r_ps[bank]]

        def PS(bank, c0, n, p0=0, np_=128):
            return sb(psum[bank], p0, np_, c0, [[1, n]])

        cnt = {"t": 0, "pb": 0, "att": 0}
        tp_bf = [psum[b][:].bitcast(BF16) for b in range(8)]

        def psbf(bank, c0, n):
            return tp_bf[bank][:, c0:c0 + n]

        def psr_bf(bank, c0, c1):
            return psr(bank, c0 // 2, (c1 + 1) // 2)

        def proj_cols(c0, ncols, bank):
            for kc in range(8):
                P.op("pe", lambda e, kc=kc: e.matmul(
                    PS(bank, 0, MG, 0, ncols), sb(winb, 0, 128, kc * NIN + c0, [[1, ncols]]),
                    sb(nT, 0, 128, kc * MG, [[1, MG]]), start=(kc == 0), stop=(kc == 7)),
                    reads=[r_win, r_nT], writes=psr(bank, 0, MG), inc=(kc == 7))

        def mix(blk, c0, ncols, bank, out_ap, r_out):
            proj_cols(c0, ncols, bank)
            i = cnt["pb"] % 2
            cnt["pb"] += 1
            pb = pbuf[i]
            P.op("act", lambda e: e.activation(out=sb(pb, 0, ncols, 0, [[1, 1]]), in_=sb(carry, 0, ncols, blk, [[1, 1]]),
                                               func=AF.Copy), reads=[r_carry], writes=[r_pbuf[i]])
            P.op("act", lambda e: e.activation(out=sb(pb, 0, ncols, 1, [[1, MG]]), in_=PS(bank, 0, MG, 0, ncols),
                                               func=AF.Copy), reads=psr(bank, 0, MG), writes=[r_pbuf[i]])
            P.op("act", lambda e: e.activation(out=sb(carry, 0, ncols, blk, [[1, 1]]), in_=sb(pb, 0, ncols, MG, [[1, 1]]),
                                               func=AF.Copy), reads=[r_pbuf[i]], writes=[r_carry])
            P.op("dve", lambda e: e.tensor_tensor(out=sb(mixtmp, 0, ncols, 0, [[1, MG]]), in0=sb(pb, 0, ncols, 0, [[1, MG]]),
                                                  in1=sb(pb, 0, ncols, 1, [[1, MG]]), op=ALU.subtract),
                 reads=[r_pbuf[i]], writes=[r_mixtmp])
            P.op("dve", lambda e: e.scalar_tensor_tensor(
                out=out_ap, in0=sb(mixtmp, 0, ncols, 0, [[1, MG]]), scalar=col("mu", blk, 1, 0, ncols),
                in1=sb(pb, 0, ncols, 1, [[1, MG]]), op0=ALU.mult, op1=ALU.add),
                reads=[r_mixtmp, r_pbuf[i], r_cols], writes=[r_out])

        def bsum(bank, src_i):
            P.op("pe", lambda e: e.matmul(PS(bank, 0, MG), cst(C_BD), W(src_i), start=True, stop=True),
                 reads=[r_consts, r_W[src_i]], writes=psr(bank, 0, MG))

        def norm_T(g):
            for st in range(NST):
                i = cnt["t"] % 2
                cnt["t"] += 1
                tok0 = g * MG + st * 128
                P.op("sp", lambda e, i=i, tok0=tok0: e.dma_start(out=hT[i][:], in_=src_d.ap()[tok0:tok0 + 128, :]),
                     writes=[r_hT[i]], dma=True)
                P.op("act", lambda e, i=i: e.activation(out=zb[i][:], in_=hT[i][:], func=AF.Square,
                                                         accum_out=sb(stat, 0, 128, i * 4, [[1, 1]])),
                     reads=[r_hT[i]], writes=[r_zb[i], r_stat[i]])
                P.op("dve", lambda e, i=i: e.tensor_scalar(
                    out=sb(stat, 0, 128, i * 4 + 1, [[1, 1]]), in0=sb(stat, 0, 128, i * 4, [[1, 1]]),
                    scalar1=1.0 / D, scalar2=EPS, op0=ALU.mult, op1=ALU.add), reads=[r_stat[i]], writes=[r_stat[i]])
                P.op("act", lambda e, i=i: e.activation(
                    out=sb(stat, 0, 128, i * 4 + 1, [[1, 1]]), in_=sb(stat, 0, 128, i * 4 + 1, [[1, 1]]),
                    func=AF.Sqrt), reads=[r_stat[i]], writes=[r_stat[i]])
                P.op("dve", lambda e, i=i: e.reciprocal(
                    out=sb(stat, 0, 128, i * 4 + 2, [[1, 1]]), in_=sb(stat, 0, 128, i * 4 + 1, [[1, 1]])),
                    reads=[r_stat[i]], writes=[r_stat[i]])
                P.op("dve", lambda e, i=i: e.tensor_scalar(
                    out=zb[i][:], in0=hT[i][:], scalar1=sb(stat, 0, 128, i * 4 + 2, [[1, 1]]), scalar2=None,
                    op0=ALU.mult), reads=[r_hT[i], r_stat[i]], writes=[r_zb[i]])
                for kc in range(8):
                    P.op("pe", lambda e, i=i, kc=kc: e.transpose(
                        psbf(0, kc * 128, 128), sb(zb[i], 0, 128, kc * 128, [[1, 128]]), ident_bf),
                        reads=[r_zb[i], r_identb], writes=psr_bf(0, 0, 1024), inc=(kc == 7))
                for kc in range(8):
                    P.op("act", lambda e, kc=kc, st=st: e.activation(
                        out=sb(nT, 0, 128, kc * MG + st * 128, [[1, 128]]), in_=psbf(0, kc * 128, 128),
                        func=AF.Identity, scale=Gcol(1, kc), bias=shiftcol(1, kc)),
                        reads=psr_bf(0, 0, 1024) + [r_Gs, r_modT], writes=[r_nT], inc=(kc == 7))

        def qk_norm(c0, gsc, dst_ap, r_dst_, bank, bank2):
            proj_cols(c0, 128, bank)
            P.op("act", lambda e: e.activation(out=W(0), in_=PS(bank, 0, MG), func=AF.Copy),
                 reads=psr(bank, 0, MG), writes=[r_W[0]])
            P.op("act", lambda e: e.activation(out=W(1), in_=PS(bank, 0, MG), func=AF.Square),
                 reads=psr(bank, 0, MG), writes=[r_W[1]])
            bsum(bank2, 1)
            P.op("dve", lambda e: e.tensor_scalar(out=W(2), in0=PS(bank2, 0, MG), scalar1=1.0 / 64, scalar2=1e-6,
                                                  op0=ALU.mult, op1=ALU.add), reads=psr(bank2, 0, MG), writes=[r_W[2]])
            P.op("act", lambda e: e.activation(out=W(2), in_=W(2), func=AF.Sqrt), reads=[r_W[2]], writes=[r_W[2]])
            P.op("dve", lambda e: e.reciprocal(out=W(2), in_=W(2)), reads=[r_W[2]], writes=[r_W[2]])
            P.op("dve", lambda e: e.scalar_tensor_tensor(out=dst_ap, in0=W(0), scalar=gsc, in1=W(2), op0=ALU.mult,
                                                         op1=ALU.mult), reads=[r_W[0], r_W[2], r_cols, r_dcols],
                 writes=[r_dst_])

        def attn_proj(g):
            slot = (g * MG) % 1024
            for cb in range(4):
                qk_norm(1792 + cb * 128, dcol(12), sb(qT, 0, 128, cb * MG, [[1, MG]]), r_qT[cb], 1, 2)
                qk_norm(2304 + cb * 128, col("kg", 0, 1), sb(KT, 0, 128, cb * 1024 + slot, [[1, MG]]), r_KT, 1, 2)
            for st in range(NST):
                tile = (g * NST + st) % 8
                for kc in range(8):
                    P.op("pe", lambda e, kc=kc, st=st: e.matmul(
                        PS(1, 0, 512), sb(nT, 0, 128, kc * MG + st * 128, [[1, 128]]),
                        sb(winb, 0, 128, kc * NIN + 2816, [[1, 512]]), start=(kc == 0), stop=(kc == 7)),
                        reads=[r_win, r_nT], writes=psr(1, 0, 512), inc=(kc == 7))
                P.op("act", lambda e, tile=tile: e.activation(out=sb(Vr, 0, 128, tile * 512, [[1, 512]]),
                                                             in_=PS(1, 0, 512), func=AF.Copy),
                     reads=psr(1, 0, 512), writes=[r_Vr])

        def attn(g):
            for st in range(NST):
                T = g * NST + st
                jbs = [jb for jb in range(5) if T - 4 + jb >= 0]
                j0 = jbs[0]
                for cb in range(4):
                    for hh in range(2):
                        head = cb * 2 + hh
                        p0 = hh * 64
                        par = cnt["att"] % 2
                        cnt["att"] += 1
                        bS = 2 + par
                        for jb in jbs:
                            kt = (T - 4 + jb) % 8
                            if jb < 4:
                                o_ap, o_r = PS(bS, jb * 128, 128), psr(bS, jb * 128, jb * 128 + 128)
                            else:
                                o_ap, o_r = PS(4, par * 128, 128), psr(4, par * 128, par * 128 + 128)
                            P.op("pe", lambda e, o_ap=o_ap, kt=kt, cb=cb, p0=p0, st=st: e.matmul(
                                o_ap, sb(KT, p0, 64, cb * 1024 + kt * 128, [[1, 128]]),
                                sb(qT, p0, 64, cb * MG + st * 128, [[1, 128]]), start=True, stop=True),
                                reads=[r_KT, r_qT[cb]], writes=o_r)
                        if j0 < 4:
                            P.op("dve", lambda e, par=par, bS=bS, head=head, j0=j0: e.tensor_tensor(
                                out=sb(scb[par], 0, 128, j0 * 128, [[1, 512 - j0 * 128]]),
                                in0=PS(bS, j0 * 128, 512 - j0 * 128),
                                in1=sb(biasT, 0, 128, head * 640 + j0 * 128, [[1, 512 - j0 * 128]]), op=ALU.add),
                                reads=psr(bS, j0 * 128, 512) + [r_biasT], writes=[r_scb[par]])
                        P.op("dve", lambda e, par=par, head=head: e.tensor_tensor(
                            out=sb(scb[par], 0, 128, 512, [[1, 128]]), in0=PS(4, par * 128, 128),
                            in1=sb(biasT, 0, 128, head * 640 + 512, [[1, 128]]), op=ALU.add),
                            reads=psr(4, par * 128, par * 128 + 128) + [r_biasT], writes=[r_scb[par]])
                        P.op("act", lambda e, par=par, j0=j0: e.activation(
                            out=sb(prb[par], 0, 128, j0 * 128, [[1, 640 - j0 * 128]]),
                            in_=sb(scb[par], 0, 128, j0 * 128, [[1, 640 - j0 * 128]]), func=AF.Exp),
                            reads=[r_scb[par]], writes=[r_prb[par]])
                        for jb in jbs:
                            kt = (T - 4 + jb) % 8
                            P.op("pe", lambda e, jb=jb, kt=kt, cb=cb, hh=hh, par=par, f0=(jb == jbs[0]), f1=(jb == jbs[-1]): e.matmul(
                                PS(5, hh * 128, 128), sb(Vr, 0, 128, kt * 512 + cb * 128, [[1, 128]]),
                                sb(prb[par], 0, 128, jb * 128, [[1, 128]]), start=f0, stop=f1),
                                reads=[r_Vr, r_prb[par]], writes=psr(5, hh * 128, hh * 128 + 128), inc=(jb == jbs[-1]))
                        for jb in jbs:
                            P.op("pe", lambda e, jb=jb, hh=hh, par=par, f0=(jb == jbs[0]), f1=(jb == jbs[-1]): e.matmul(
                                PS(5, 256 + hh * 128, 128), onesb[:],
                                sb(prb[par], 0, 128, jb * 128, [[1, 128]]), start=f0, stop=f1),
                                reads=[r_onesb, r_prb[par]], writes=psr(5, 256 + hh * 128, 256 + hh * 128 + 128),
                                inc=(jb == jbs[-1]))
                        P.op("dve", lambda e, hh=hh, p0=p0: e.reciprocal(
                            out=sb(rec, p0, 64, 0, [[1, 128]]), in_=PS(5, 256 + hh * 128, 128, p0, 64)),
                            reads=psr(5, 256 + hh * 128, 256 + hh * 128 + 128), writes=[r_rec])
                        P.op("dve", lambda e, hh=hh, p0=p0, cb=cb, st=st: e.tensor_tensor(
                            out=sb(ymixT, p0, 64, (4 + cb) * MG + st * 128, [[1, 128]]), in0=PS(5, hh * 128, 128, p0, 64),
                            in1=sb(rec, p0, 64, 0, [[1, 128]]), op=ALU.mult),
                            reads=psr(5, hh * 128, hh * 128 + 128) + [r_rec], writes=[r_ymix[4 + cb]])

        def lora_inputs():
            mix(12, 1536, 64, 1, sb(Wt[0], 0, 64, 0, [[1, MG]]), r_W[0])
            P.op("act", lambda e: e.activation(out=sb(thb, 0, 64, 0, [[1, MG]]), in_=sb(Wt[0], 0, 64, 0, [[1, MG]]),
                                               func=AF.Tanh), reads=[r_W[0]], writes=[r_thb])
            mix(13, 1600, 64, 2, sb(Wt[1], 0, 64, 0, [[1, MG]]), r_W[1])
            P.op("act", lambda e: e.activation(out=sb(padb, 0, 64, 0, [[1, MG]]), in_=sb(Wt[1], 0, 64, 0, [[1, MG]]),
                                               func=AF.Copy), reads=[r_W[1]], writes=[r_padb])
            mix(14, 1664, 128, 1, W(2), r_W[2])
            P.op("act", lambda e: e.activation(out=W(2), in_=W(2), func=AF.Tanh, scale=0.5), reads=[r_W[2]],
                 writes=[r_W[2]])
            P.op("dve", lambda e: e.tensor_scalar(out=sgdb[:], in0=W(2), scalar1=0.5, scalar2=0.5, op0=ALU.mult,
                                                  op1=ALU.add), reads=[r_W[2]], writes=[r_sgdb])

        def v3(t, p0, np_, off, cs, n=64):
            return sb(t, p0, np_, off, [[cs, NCH], [1, n]])

        def rwkv_prep(j):
            mix(j, j * 128, 128, 1, W(R_), r_W[R_])
            mix(4 + j, 512 + j * 128, 128, 2, W(Kx), r_W[Kx])
            mix(8 + j, 1024 + j * 128, 128, 1, W(Vx), r_W[Vx])
            P.op("pe", lambda e: e.matmul(PS(2, 0, MG), sb(wdub, 0, 64, j * 128, [[1, 128]]),
                                          sb(thb, 0, 64, 0, [[1, MG]]), start=True, stop=True),
                 reads=[r_lora, r_thb], writes=psr(2, 0, MG))
            P.op("act", lambda e: e.activation(out=W(0), in_=PS(2, 0, MG), func=AF.Tanh, scale=0.5, bias=dcol(j)),
                 reads=psr(2, 0, MG) + [r_dcols], writes=[r_W[0]])
            P.op("dve", lambda e: e.tensor_scalar(out=W(0), in0=W(0), scalar1=0.5, scalar2=0.5, op0=ALU.mult,
                                                  op1=ALU.add), reads=[r_W[0]], writes=[r_W[0]])
            P.op("dve", lambda e: e.tensor_tensor_scan(out=W(1), data0=cst(C_MRES, MG), data1=W(0), initial=0.0,
                                                       op0=ALU.mult, op1=ALU.add), reads=[r_W[0], r_consts],
                 writes=[r_W[1]])
            P.op("act", lambda e: e.activation(out=W(2), in_=W(1), func=AF.Exp, scale=-C0), reads=[r_W[1]],
                 writes=[r_W[2]])
            P.op("act", lambda e: e.activation(out=W(3), in_=W(1), func=AF.Exp, scale=C0), reads=[r_W[1]],
                 writes=[r_W[3]])
            P.op("dve", lambda e: e.tensor_tensor(out=v3(Wt[4], 0, 128, 0, 64), in0=sb(Wt[1], 0, 128, 63, [[64, NCH], [0, 64]]),
                                                  in1=v3(Wt[1], 0, 128, 0, 64), op=ALU.subtract), reads=[r_W[1]],
                 writes=[r_W[4]])
            P.op("act", lambda e: e.activation(out=W(4), in_=W(4), func=AF.Exp, scale=-C0), reads=[r_W[4]],
                 writes=[r_W[4]])
            P.op("act", lambda e: e.activation(out=sb(WLt, 0, 128, 0, [[1, NCH]]), in_=sb(Wt[1], 0, 128, 63, [[64, NCH]]),
                                               func=AF.Exp, scale=-C0), reads=[r_W[1]], writes=[r_WLt])
            P.op("pe", lambda e: e.matmul(PS(1, 0, MG), sb(waub, 0, 64, j * 128, [[1, 128]]),
                                          sb(padb, 0, 64, 0, [[1, MG]]), start=True, stop=True),
                 reads=[r_lora, r_padb], writes=psr(1, 0, MG))
            P.op("act", lambda e: e.activation(out=W(5), in_=PS(1, 0, MG), func=AF.Tanh, scale=0.5, bias=dcol(4 + j)),
                 reads=psr(1, 0, MG) + [r_dcols], writes=[r_W[5]])
            P.op("dve", lambda e: e.tensor_scalar(out=W(5), in0=W(5), scalar1=0.5, scalar2=0.5, op0=ALU.mult,
                                                  op1=ALU.add), reads=[r_W[5]], writes=[r_W[5]])
            P.op("pe", lambda e: e.matmul(PS(2, 0, MG), sb(wgub, 0, 128, j * 128, [[1, 128]]), sgdb[:],
                                          start=True, stop=True), reads=[r_lora, r_sgdb], writes=psr(2, 0, MG))
            P.op("act", lambda e: e.activation(out=W(6), in_=PS(2, 0, MG), func=AF.Copy), reads=psr(2, 0, MG),
                 writes=[r_W[6]])
            P.op("pool", lambda e: e.tensor_scalar(out=W(7), in0=W(Kx), scalar1=col("k_k", j), scalar2=None,
                                                   op0=ALU.mult), reads=[r_W[Kx], r_cols], writes=[r_W[7]])
            P.op("pool", lambda e: e.tensor_tensor(out=W(8), in0=W(7), in1=W(7), op=ALU.mult), reads=[r_W[7]],
                 writes=[r_W[8]])
            bsum(1, 8)
            P.op("dve", lambda e: e.tensor_scalar(out=W(8), in0=PS(1, 0, MG), scalar1=1e-24, scalar2=None, op0=ALU.max),
                 reads=psr(1, 0, MG), writes=[r_W[8]])
            P.op("act", lambda e: e.activation(out=W(8), in_=W(8), func=AF.Sqrt), reads=[r_W[8]], writes=[r_W[8]])
            P.op("dve", lambda e: e.reciprocal(out=W(8), in_=W(8)), reads=[r_W[8]], writes=[r_W[8]])
            P.op("pool", lambda e: e.tensor_tensor(out=W(7), in0=W(7), in1=W(8), op=ALU.mult), reads=[r_W[7], r_W[8]],
                 writes=[r_W[7]])
            P.op("dve", lambda e: e.tensor_scalar(out=W(0), in0=W(5), scalar1=col("k_a", j), scalar2=dcol(8 + j),
                                                  op0=ALU.mult, op1=ALU.add), reads=[r_W[5], r_cols, r_dcols],
                 writes=[r_W[0]])
            P.op("pool", lambda e: e.tensor_tensor(out=W(9), in0=W(Kx), in1=W(0), op=ALU.mult), reads=[r_W[Kx], r_W[0]],
                 writes=[r_W[9]])
            P.op("pool", lambda e: e.tensor_tensor(out=W(10), in0=W(7), in1=W(5), op=ALU.mult), reads=[r_W[7], r_W[5]],
                 writes=[r_W[10]])
            P.op("dve", lambda e: e.scalar_tensor_tensor(out=W(12), in0=W(R_), scalar=col("r_k", j), in1=W(9),
                                                         op0=ALU.mult, op1=ALU.mult), reads=[r_W[R_], r_W[9], r_cols],
                 writes=[r_W[12]])
            bsum(2, 12)
            P.op("dve", lambda e: e.tensor_tensor(out=W(11), in0=PS(2, 0, MG), in1=W(Vx), op=ALU.mult),
                 reads=psr(2, 0, MG) + [r_W[Vx]], writes=[r_W[11]])
            k_ = 0
            for hh in range(2):
                p0 = hh * 64
                specs = [
                    (fKR, 256, 128 + hh * 64, r_fKR, R_, 2),
                    (fB, 128, hh * 64, r_fB, 10, 3),
                    (fK, 128, hh * 64, r_fK, 9, 3),
                    (fKW, 128, hh * 64, r_fKW, 9, 4),
                    (fBW, 128, hh * 64, r_fBW, 10, 4),
                ]
                for (dst, cs, off, rd, a_i, b_i) in specs:
                    eng = ("dve", "pool")[k_ % 2]
                    k_ += 1
                    P.op(eng, lambda e, dst=dst, cs=cs, off=off, a_i=a_i, b_i=b_i, p0=p0: e.tensor_tensor(
                        out=v3(dst, p0, 64, off, cs), in0=v3(Wt[a_i], p0, 64, 0, 64), in1=v3(Wt[b_i], p0, 64, 0, 64),
                        op=ALU.mult), reads=[r_W[a_i], r_W[b_i]], writes=[rd])
                P.op("pool", lambda e, p0=p0, hh=hh: e.tensor_copy(out=v3(fV, p0, 64, hh * 64, 128),
                                                                   in_=v3(Wt[Vx], p0, 64, 0, 64)),
                     reads=[r_W[Vx]], writes=[r_fV])
                P.op("dve", lambda e, p0=p0, hh=hh: e.tensor_tensor(
                    out=v3(fKR, p0, 64, hh * 64 + 1, 256, 63), in0=v3(Wt[7], p0, 64, 1, 64, 63),
                    in1=v3(Wt[2], p0, 64, 0, 64, 63), op=ALU.mult), reads=[r_W[7], r_W[2]], writes=[r_fKR])
                P.op("pool", lambda e, p0=p0, hh=hh: e.tensor_copy(
                    out=v3(fKR, p0, 64, hh * 64, 256, 1), in_=v3(Wt[7], p0, 64, 0, 64, 1)),
                    reads=[r_W[7]], writes=[r_fKR])
            for (src, rs, dst, rd, bank, c0) in ((fV, r_fV, tV, r_tV, 1, 0), (fKW, r_fKW, tKW, r_tKW, 1, 512),
                                                 (fBW, r_fBW, tBW, r_tBW, 2, 0)):
                for c in range(NCH):
                    P.op("pe", lambda e, src=src, c=c, bank=bank, c0=c0: e.transpose(
                        psbf(bank, c0 + c * 128, 128), sb(src, 0, 128, c * 128, [[1, 128]]), ident_bf),
                        reads=[rs, r_identb], writes=psr_bf(bank, c0, c0 + 512), inc=(c == NCH - 1))
                P.op("act", lambda e, dst=dst, bank=bank, c0=c0: e.activation(
                    out=dst[:], in_=psbf(bank, c0, NCH * 128), func=AF.Copy),
                    reads=psr_bf(bank, c0, c0 + 512), writes=[rd])

        def rwkv_chunks(j):
            def fKA(c):
                return sb(fKR, 0, 128, c * 256, [[1, 128]])

            def fR(c):
                return sb(fKR, 0, 128, c * 256 + 128, [[1, 128]])

            def blk(t, c, w=128, o=0):
                return sb(t, 0, 128, c * w + o, [[1, 128]])

            for c in range(NCH):
                P.op("pe", lambda e, c=c: e.matmul(PS(3, c * 128, 128), fKA(c), blk(fB, c), start=True, stop=True),
                     reads=[r_fKR, r_fB], writes=psr(3, c * 128, c * 128 + 128))
            P.op("dve", lambda e: e.tensor_tensor(
                out=sb(Nn, 0, 128, 0, [[128, NCH], [1, 128]]), in0=sb(psum[3], 0, 128, 0, [[128, NCH], [1, 128]]),
                in1=sb(consts, 0, 128, C_NDL, [[0, NCH], [1, 128]]), op=ALU.mult),
                reads=psr(3, 0, 512) + [r_consts], writes=[r_Nn])
            P.op("dve", lambda e: e.tensor_tensor(
                out=sb(Esb, 0, 128, 0, [[128, NCH], [1, 128]]), in0=sb(psum[3], 0, 128, 0, [[128, NCH], [1, 128]]),
                in1=sb(consts, 0, 128, C_EM, [[0, NCH], [1, 128]]), op=ALU.mult),
                reads=psr(3, 0, 512) + [r_consts], writes=[r_Esb])
            for (lh, rl, dst, rd, b0, cm) in ((fB, r_fB, NtArb, r_NtArb, 4, C_NDU), (fK, r_fK, AkArk, r_AkArk, 6, C_NSU)):
                for c in range(NCH):
                    P.op("pe", lambda e, c=c, lh=lh, b0=b0: e.matmul(
                        PS(b0 + c // 2, (c % 2) * 256, 256), blk(lh, c), sb(fKR, 0, 128, c * 256, [[1, 256]]),
                        start=True, stop=True), reads=[rl, r_fKR],
                        writes=psr(b0 + c // 2, (c % 2) * 256, (c % 2) * 256 + 256))
                for hb in range(NCH // 2):
                    P.op("dve", lambda e, hb=hb, dst=dst, b0=b0, cm=cm: e.tensor_tensor(
                        out=sb(dst, 0, 128, hb * 512, [[256, 2], [1, 256]]),
                        in0=sb(psum[b0 + hb], 0, 128, 0, [[256, 2], [1, 256]]),
                        in1=sb(consts, 0, 128, cm, [[0, 2], [1, 256]]), op=ALU.mult),
                        reads=psr(b0 + hb, 0, 512) + [r_consts], writes=[rd])

            def XP(c, o, n):
                return PS(3 + c, o, n)

            def XR(c):
                return [r_ps[3 + c]]

            def Rb(c, o, n):
                return sb(Xb, 0, 128, c * 384 + o, [[1, n]])

            for c in range(NCH):
                P.op("pe", lambda e, c=c: e.matmul(XP(c, 0, 128), blk(AkArk, c, 256), blk(tV, c), start=True,
                                                   stop=True, skip_group_check=True),
                     reads=[r_AkArk, r_tV], writes=XR(c))
                P.op("pe", lambda e, c=c: e.matmul(XP(c, 128, 128), fKA(c), ident_bf, start=False, stop=True,
                                                   skip_group_check=True),
                     reads=[r_fKR, r_identb], writes=XR(c))
                P.op("pe", lambda e, c=c: e.matmul(XP(c, 256, 128), ident_bf, blk(Esb, c), start=False, stop=True,
                                                   skip_group_check=True),
                     reads=[r_Esb, r_identb], writes=XR(c))
            Pc = [blk(NtArb, c, 256) for c in range(NCH)]
            PTc = [blk(Nn, c) for c in range(NCH)]
            rP, rPT = [r_NtArb], [r_Nn]

            def pbank(c):
                return (7, 2)[c // 2]

            for lvl in range(4):
                for c in range(NCH):
                    P.op("act", lambda e, c=c: e.activation(out=Rb(c, 0, 384), in_=XP(c, 0, 384), func=AF.Copy),
                         reads=XR(c), writes=[r_Xb])
                if lvl >= 1:
                    pp = PP[lvl % 2]
                    for c in range(NCH):
                        P.op("pe", lambda e, c=c, Pc=Pc, PTc=PTc: e.matmul(
                            PS(pbank(c), (c % 2) * 256, 128), PTc[c], Pc[c], start=True, stop=True),
                            reads=rP + rPT, writes=[r_ps[pbank(c)]])
                        P.op("pe", lambda e, c=c, Pc=Pc, PTc=PTc: e.matmul(
                            PS(pbank(c), (c % 2) * 256 + 128, 128), Pc[c], PTc[c], start=True, stop=True),
                            reads=rP + rPT, writes=[r_ps[pbank(c)]])
                    for hb in range(NCH // 2):
                        P.op("dve", lambda e, hb=hb, pp=pp: e.tensor_copy(out=sb(pp, 0, 128, hb * 512, [[1, 512]]),
                                                                         in_=PS((7, 2)[hb], 0, 512)),
                             reads=[r_ps[(7, 2)[hb]]], writes=[r_PP[lvl % 2]])
                    Pc = [blk(pp, c, 256) for c in range(NCH)]
                    PTc = [blk(pp, c, 256, 128) for c in range(NCH)]
                    rP = rPT = [r_PP[lvl % 2]]
                for c in range(NCH):
                    P.op("pe", lambda e, c=c, Pc=Pc: e.matmul(XP(c, 0, 384), Pc[c], Rb(c, 0, 384),
                                                              start=False, stop=True, skip_group_check=True),
                         reads=rP + [r_Xb], writes=XR(c))
            for c in range(NCH):
                P.op("act", lambda e, c=c: e.activation(out=Rb(c, 0, 384), in_=XP(c, 0, 384), func=AF.Copy),
                     reads=XR(c), writes=[r_Xb])
            for c in range(NCH):
                P.op("pe", lambda e, c=c: e.transpose(psbf(1, c * 128, 128), Rb(c, 256, 128), ident_bf),
                     reads=[r_Xb, r_identb], writes=[r_ps[1]], inc=(c == NCH - 1))
            P.op("act", lambda e: e.activation(out=nGT[:], in_=psbf(1, 0, NCH * 128), func=AF.Copy, scale=-1.0),
                 reads=[r_ps[1]], writes=[r_nGT])
            P.op("dve", lambda e: e.tensor_tensor(
                out=sb(nGTmI, 0, 128, 0, [[128, NCH], [1, 128]]), in0=sb(nGT, 0, 128, 0, [[128, NCH], [1, 128]]),
                in1=sb(consts, 0, 128, 0, [[0, NCH], [1, 128]]), op=ALU.subtract),
                reads=[r_nGT, r_consts], writes=[r_nGTmI])
            for c in range(NCH):
                P.op("pe", lambda e, c=c: e.matmul(XP(c, 0, 256), blk(nGT, c), Rb(c, 0, 256), start=False, stop=True,
                                                   skip_group_check=True), reads=[r_nGT, r_Xb], writes=XR(c))
            for it in range(2):
                for c in range(NCH):
                    P.op("act", lambda e, c=c: e.activation(out=sb(Xit, 0, 128, c * 256, [[1, 256]]), in_=XP(c, 0, 256),
                                                            func=AF.Copy), reads=XR(c), writes=[r_Xit])
                for c in range(NCH):
                    P.op("pe", lambda e, c=c: e.matmul(XP(c, 0, 256), blk(nGTmI, c), sb(Xit, 0, 128, c * 256, [[1, 256]]),
                                                       start=False, stop=True, skip_group_check=True),
                         reads=[r_nGTmI, r_Xit], writes=XR(c))
                    P.op("pe", lambda e, c=c: e.matmul(XP(c, 0, 256), ident_bf, Rb(c, 0, 256),
                                                       start=False, stop=True, skip_group_check=True),
                         reads=[r_identb, r_Xb], writes=XR(c))
            for c in range(NCH):
                P.op("act", lambda e, c=c: e.activation(out=sb(Xit, 0, 128, c * 256, [[1, 256]]), in_=XP(c, 0, 256),
                                                        func=AF.Copy), reads=XR(c), writes=[r_Xit])

            def Ul(c):
                return sb(Xit, 0, 128, c * 256, [[1, 128]])

            def Qc(c):
                return sb(Xit, 0, 128, c * 256 + 128, [[1, 128]])

            def ArbT(c):
                return sb(NtArb, 0, 128, c * 256 + 128, [[1, 128]])

            def ArkT(c):
                return sb(AkArk, 0, 128, c * 256 + 128, [[1, 128]])

            for c in range(NCH):
                P.op("pe", lambda e, c=c: e.matmul(PS(7, c * 128, 128), Qc(c), ArbT(c), start=True, stop=True),
                     reads=[r_Xit, r_NtArb], writes=psr(7, c * 128, c * 128 + 128))
            P.op("dve", lambda e: e.tensor_tensor(
                out=sb(Reff, 0, 128, 0, [[128, NCH], [1, 128]]), in0=sb(fKR, 0, 128, 128, [[256, NCH], [1, 128]]),
                in1=sb(psum[7], 0, 128, 0, [[128, NCH], [1, 128]]), op=ALU.subtract),
                reads=psr(7, 0, 512) + [r_fKR], writes=[r_Reff])
            for c in range(NCH):
                P.op("pe", lambda e, c=c: e.matmul(PS(2, c * 128, 128), Qc(c), blk(tBW, c), start=True, stop=True),
                     reads=[r_Xit, r_tBW], writes=psr(2, c * 128, c * 128 + 128))
            for c in range(NCH):
                P.op("dve", lambda e, c=c: e.scalar_tensor_tensor(
                    out=blk(Mc, c), in0=ident, scalar=sb(WLt, 0, 128, c, [[1, 1]]), in1=PS(2, c * 128, 128),
                    op0=ALU.mult, op1=ALU.subtract), reads=psr(2, c * 128, c * 128 + 128) + [r_WLt, r_consts],
                    writes=[r_Mc])
            Zj = sb(Zb, 0, 128, j * 128, [[1, 128]])
            for c in range(NCH):
                yo, yr = PS(3, c * 128, 128), psr(3, c * 128, c * 128 + 128)
                P.op("pe", lambda e, c=c, yo=yo: e.matmul(yo, Ul(c), ArbT(c), start=True, stop=False),
                     reads=[r_Xit, r_NtArb], writes=yr, inc=False)
                P.op("pe", lambda e, c=c, yo=yo: e.matmul(yo, blk(tV, c), ArkT(c), start=False, stop=False),
                     reads=[r_tV, r_AkArk], writes=yr, inc=False)
                P.op("pe", lambda e, c=c, yo=yo: e.matmul(yo, Zj, blk(Reff, c), start=False, stop=True),
                     reads=[r_Zb[j], r_Reff], writes=yr)
                zo, zr = PS(4, (c % 2) * 128, 128), psr(4, (c % 2) * 128, (c % 2) * 128 + 128)
                P.op("pe", lambda e, c=c, zo=zo: e.matmul(zo, blk(tBW, c), Ul(c), start=True, stop=False),
                     reads=[r_tBW, r_Xit], writes=zr, inc=False)
                P.op("pe", lambda e, c=c, zo=zo: e.matmul(zo, blk(tKW, c), blk(tV, c), start=False, stop=False),
                     reads=[r_tKW, r_tV], writes=zr, inc=False)
                P.op("pe", lambda e, c=c, zo=zo: e.matmul(zo, blk(Mc, c), Zj, start=False, stop=True),
                     reads=[r_Mc, r_Zb[j]], writes=zr)
                P.op("act", lambda e, zo=zo: e.activation(out=Zj, in_=zo, func=AF.Copy), reads=zr, writes=[r_Zb[j]])
                for hh in range(2):
                    p0 = hh * 64
                    P.op("dve", lambda e, c=c, p0=p0, hh=hh: e.tensor_copy(
                        out=sb(yT, p0, 64, c * 64, [[1, 64]]), in_=PS(3, c * 128 + hh * 64, 64, p0, 64)),
                        reads=yr, writes=[r_yT])

        def rwkv_out(j):
            P.op("pe", lambda e: e.matmul(PS(1, 0, MG), cst(C_BD), yT[:], start=True, stop=True),
                 reads=[r_consts, r_yT], writes=psr(1, 0, MG))
            P.op("act", lambda e: e.activation(out=W(12), in_=yT[:], func=AF.Square), reads=[r_yT], writes=[r_W[12]])
            bsum(2, 12)
            P.op("dve", lambda e: e.tensor_scalar(out=W(0), in0=PS(1, 0, MG), scalar1=1.0 / 64, scalar2=None,
                                                  op0=ALU.mult), reads=psr(1, 0, MG), writes=[r_W[0]])
            P.op("dve", lambda e: e.tensor_tensor(out=W(8), in0=W(0), in1=W(0), op=ALU.mult), reads=[r_W[0]],
                 writes=[r_W[8]])
            P.op("dve", lambda e: e.scalar_tensor_tensor(out=W(8), in0=PS(2, 0, MG), scalar=1.0 / 64, in1=W(8),
                                                         op0=ALU.mult, op1=ALU.subtract),
                 reads=psr(2, 0, MG) + [r_W[8]], writes=[r_W[8]])
            P.op("pool", lambda e: e.tensor_scalar(out=W(8), in0=W(8), scalar1=64e-5, scalar2=None, op0=ALU.add),
                 reads=[r_W[8]], writes=[r_W[8]])
            P.op("act", lambda e: e.activation(out=W(8), in_=W(8), func=AF.Sqrt), reads=[r_W[8]], writes=[r_W[8]])
            P.op("dve", lambda e: e.reciprocal(out=W(8), in_=W(8)), reads=[r_W[8]], writes=[r_W[8]])
            P.op("dve", lambda e: e.tensor_tensor(out=W(12), in0=yT[:], in1=W(0), op=ALU.subtract),
                 reads=[r_yT, r_W[0]], writes=[r_W[12]])
            P.op("dve", lambda e: e.tensor_tensor(out=W(12), in0=W(12), in1=W(8), op=ALU.mult),
                 reads=[r_W[12], r_W[8]], writes=[r_W[12]])
            P.op("act", lambda e: e.activation(out=W(12), in_=W(12), func=AF.Identity, scale=col("lnx_g", j),
                                               bias=col("lnx_b", j)), reads=[r_W[12], r_cols], writes=[r_W[12]])
            P.op("dve", lambda e: e.tensor_tensor(out=W(12), in0=W(12), in1=W(11), op=ALU.add),
                 reads=[r_W[12], r_W[11]], writes=[r_W[12]])
            P.op("dve", lambda e: e.tensor_tensor(out=sb(ymixT, 0, 128, j * MG, [[1, MG]]), in0=W(12), in1=W(6),
                                                  op=ALU.mult), reads=[r_W[12], r_W[6]], writes=[r_ymix[j]])

        def out_proj(g):
            for st in range(NST):
                i = cnt["t"] % 2
                cnt["t"] += 1
                tok0 = g * MG + st * 128
                P.op("sp", lambda e, i=i, tok0=tok0: e.dma_start(out=hT[i][:], in_=src_d.ap()[tok0:tok0 + 128, :]),
                     writes=[r_hT[i]], dma=True)
                for half in range(2):
                    bank = 6 + half
                    for kc in range(8):
                        P.op("pe", lambda e, kc=kc, st=st, half=half, bank=bank: e.matmul(
                            PS(bank, 0, 512), sb(ymixT, 0, 128, kc * MG + st * 128, [[1, 128]]),
                            sb(woutb, 0, 128, kc * D + half * 512, [[1, 512]]), start=(kc == 0), stop=(kc == 7)),
                            reads=[r_ymix[kc], r_wout], writes=psr(bank, 0, 512), inc=(kc == 7))
                    P.op("dve", lambda e, half=half, bank=bank: e.tensor_tensor(
                        out=tmpo[half][:], in0=PS(bank, 0, 512), in1=sb(gate_t, 0, 128, half * 512, [[1, 512]]),
                        op=ALU.mult), reads=psr(bank, 0, 512) + [r_gate_t], writes=[r_tmpo[half]])
                    P.op("pool", lambda e, i=i, half=half: e.tensor_tensor(
                        out=sb(hT[i], 0, 128, half * 512, [[1, 512]]), in0=sb(hT[i], 0, 128, half * 512, [[1, 512]]),
                        in1=tmpo[half][:], op=ALU.add), reads=[r_tmpo[half], r_hT[i]], writes=[r_hT[i]])
                P.op("sp", lambda e, i=i, tok0=tok0: e.dma_start(out=dst_d.ap()[tok0:tok0 + 128, :], in_=hT[i][:]),
                     reads=[r_hT[i]], writes=[r_dst], dma=True)

        if "no_attn" in flags or "no_rwkv" in flags:
            P.op("pool", lambda e: e.memset(ymixT[:], 0.0), writes=r_ymix)
        for g in range(NGm):
            norm_T(g)
            if "no_attn" not in flags:
                attn_proj(g)
                attn(g)
            if "no_rwkv" not in flags:
                lora_inputs()
                for j in range(4):
                    rwkv_prep(j)
                    rwkv_chunks(j)
                    rwkv_out(j)
            out_proj(g)
        A.release()
        P.barrier()

    ident_bf_t = A.alloc("identb", 128, BF16)
    ident_bf = ident_bf_t[:]
    r_identb = Res("identb")
    P.op("dve", lambda e: e.tensor_copy(out=ident_bf, in_=ident), reads=[r_consts], writes=[r_identb])
    r_dst = Res("dst")
    epsc = A.alloc("epsc", 4, F32)
    r_epsc = Res("epsc")
    P.op("pool", lambda e: e.memset(sb(epsc, 0, 128, 0, [[1, 1]]), EPS), writes=[r_epsc])

    if upto == "ffn1":
        ffn_phase(0, x_d, out_d, f1w1_d, f1w3_d, f1w2_d)
    elif upto == "mixer":
        ffn_phase(0, x_d, h1_d, f1w1_d, f1w3_d, f1w2_d)
        mixer_phase(h1_d, out_d)
    else:
        ffn_phase(0, x_d, h1_d, f1w1_d, f1w3_d, f1w2_d)
        mixer_phase(h1_d, h2_d)
        ffn_phase(2, h2_d, out_d, f2w1_d, f2w3_d, f2w2_d)

    P.emit(reorder=("no_reorder" not in flags))
    P.names = A.names
    return nc, P


def _kc_layout(w):
    Kd, N = w.shape
    return np.ascontiguousarray(w.reshape(Kd // 128, 128, N).transpose(1, 0, 2).reshape(128, (Kd // 128) * N))


def _colvec(v):
    v = np.asarray(v, np.float32).reshape(-1)
    return np.ascontiguousarray(v.reshape(-1, 128).T)


def make_consts():
    c = np.zeros((128, NCONST), np.float32)
    c[:, 0:128] = np.eye(128, dtype=np.float32)
    c[:, 128:256] = 1.0
    p = np.arange(128)
    same = (p[:, None] // 64) == (p[None, :] // 64)
    row, colm = p[:, None] % 64, p[None, :] % 64
    same16 = (p[:, None] // 16) == (p[None, :] // 16)
    c[:, C_NDL:C_NDL + 128] = -(same16 & (row > colm)).astype(np.float32)
    c[:, C_NDU:C_NDU + 128] = -(same16 & (row < colm)).astype(np.float32)
    c[:, C_NSU:C_NSU + 128] = -(same & (row < colm)).astype(np.float32)
    c[:, C_UI:C_UI + 128] = (same & (row <= colm)).astype(np.float32)
    c[:, C_UI2:C_UI2 + 128] = (same & (row <= colm)).astype(np.float32)
    c[:, C_EM:C_EM + 128] = (same & ((row // 16) > (colm // 16))).astype(np.float32)
    c[:, C_BD:C_BD + 128] = same.astype(np.float32)
    c[:, C_MRES:C_MRES + MG] = (np.arange(MG) % 64 != 0).astype(np.float32)[None, :]
    return c


def make_bias_table(rel_bias):
    p = np.arange(128)[:, None, None]
    jb = np.arange(5)[None, :, None]
    qi = np.arange(128)[None, None, :]
    kpos = (jb - 4) * 128 + p
    dist = qi - kpos
    dch = qi // 64 - np.floor_divide(kpos, 64)
    valid = (dch >= 0) & (dch <= 8)
    idx = np.clip(dist, -256, 256) + 256
    rb = np.asarray(rel_bias, np.float32)
    tab = rb[:, idx]
    tab = np.where(valid[None], tab, np.float32(NEG)).astype(np.float32)
    return np.ascontiguousarray(tab.transpose(1, 0, 2, 3).reshape(128, 8 * 640))


def make_core_inputs(b, inp):
    cols = np.zeros((128, NCOL), np.float32)

    def put(name, v):
        o, w = COLS[name]
        cols[:, o:o + w] = v

    put("c", _colvec(inp["c"][b]))
    put("b_ada", _colvec(inp["b_ada"][0]))
    put("n1g", _colvec(inp["norm1_g"][0]))
    put("n2g", _colvec(inp["norm2_g"][0]))
    put("n3g", _colvec(inp["norm3_g"][0]))
    mu = np.asarray(inp["mu_shift"][0], np.float32)
    mucols = np.zeros((128, 15), np.float32)
    mucols[:, 0:12] = _colvec(mu[0:1536])
    mucols[0:64, 12] = mu[1536:1600]
    mucols[0:64, 13] = mu[1600:1664]
    mucols[:, 14] = mu[1664:1792]
    put("mu", mucols)
    for nm, key in (("w0", "w0"), ("a0", "a0"), ("k_k", "k_k"), ("k_a", "k_a"), ("r_k", "r_k"), ("lnx_g", "lnx_g"),
                    ("lnx_b", "lnx_b")):
        put(nm, _colvec(inp[key][0]))
    put("qg", np.tile(np.asarray(inp["q_norm_g"][0], np.float32), 2)[:, None])
    put("kg", np.tile(np.asarray(inp["k_norm_g"][0], np.float32), 2)[:, None])
    m = {
        "x": np.ascontiguousarray(inp["x"][b]),
        "wada": np.ascontiguousarray(inp["w_ada"][0].reshape(8, 128, 9 * D).transpose(1, 0, 2)),
        "cols": cols,
        "consts": make_consts(),
        "f1w1": _kc_layout(inp["ffn1_w1"][0]),
        "f1w3": _kc_layout(inp["ffn1_w3"][0]),
        "f1w2": _kc_layout(inp["ffn1_w2"][0]),
        "f2w1": _kc_layout(inp["ffn2_w1"][0]),
        "f2w3": _kc_layout(inp["ffn2_w3"][0]),
        "f2w2": _kc_layout(inp["ffn2_w2"][0]),
        "win": _kc_layout(inp["w_in"][0]),
        "wout": _kc_layout(inp["w_out"][0]),
        "wdu": np.ascontiguousarray(inp["w_decay_up"][0]),
        "wau": np.ascontiguousarray(inp["w_a_up"][0]),
        "wgu": np.ascontiguousarray(inp["w_g_up"][0]),
        "biasT": make_bias_table(inp["rel_bias"][0]),
    }
    return m


def kernel(**inp):
    inp = {k: np.asarray(v) for k, v in inp.items()}
    B, S, _ = inp["x"].shape
    nc, P = build(S)
    in_maps = [make_core_inputs(b % B, inp) for b in range(8)]
    res = run_bass_kernel_spmd(nc, in_maps, core_ids=list(range(8)))
    out = np.stack([res.results[b]["out"] for b in range(B)], axis=0)
    return out.astype(np.float32)
```

```python
import contextlib
import numpy as np
import ml_dtypes
import concourse.bass as bass
import concourse.mybir as mybir
from concourse.bass_utils import run_bass_kernel_spmd

F32 = mybir.dt.float32
BF16 = mybir.dt.bfloat16
AF = mybir.ActivationFunctionType
ALU = mybir.AluOpType
AX = mybir.AxisListType

ENGS = ["pe", "act", "dve", "pool", "sp"]

D = 1024
DFF = 2816
NFC = DFF // 128
NIN = 3328
EPS = 1e-6


class Res:
    __slots__ = ("name", "w", "r")

    def __init__(self, name):
        self.name = name
        self.w = None
        self.r = {}


class Prog:
    def __init__(self, nc, n_dma_sems=8, self_sync=True):
        self.nc = nc
        self.ops = {e: [] for e in ENGS}
        self.cnt = {e: 0 for e in ENGS}
        self.seen = {e: {} for e in ENGS}
        self.pending = {e: [] for e in ENGS}
        self.self_sync = self_sync
        self.n_dma_sems = n_dma_sems
        self.dma_cnt = {}
        self.dma_rr = {e: 0 for e in ENGS}
        self.nops = 0
        self.raw = []

    def _need(self, eng, waits, tok):
        if tok is None:
            return
        key, val = tok
        if key == eng and (eng == "pe" or not self.self_sync):
            return
        if self.seen[eng].get(key, 0) >= val:
            return
        if waits.get(key, 0) < val:
            waits[key] = val

    COST = {"pe": 0.13, "act": 0.35, "dve": 0.40, "pool": 0.60, "sp": 0.10}

    def op(self, eng, fn, reads=(), writes=(), dma=False, inc=True, cost=None):
        self.raw.append((eng, fn, tuple(reads), tuple(writes), dma, inc, cost if cost is not None else self.COST[eng]))

    def barrier(self):
        self.raw.append(None)

    def _schedule_segment(self, seg):
        import heapq
        n = len(seg)
        deps = [set() for _ in range(n)]
        lastw, readers = {}, {}
        for i, (eng, fn, reads, writes, dma, inc, cost) in enumerate(seg):
            for r in reads:
                if id(r) in lastw:
                    deps[i].add(lastw[id(r)])
            for r in writes:
                if id(r) in lastw:
                    deps[i].add(lastw[id(r)])
                for j in readers.get(id(r), ()):
                    deps[i].add(j)
            for r in reads:
                readers.setdefault(id(r), []).append(i)
            for r in writes:
                lastw[id(r)] = i
                readers[id(r)] = []
            deps[i].discard(i)
        succ = [[] for _ in range(n)]
        ndep = [len(d) for d in deps]
        for i, d in enumerate(deps):
            for j in d:
                succ[j].append(i)
        LAT_X, LAT_S, DMA_LAT = 0.9, 0.15, 2.5
        ready_t = [0.0] * n
        finish = [0.0] * n
        start = [0.0] * n
        heaps = {e: [] for e in ENGS}
        free = {e: 0.0 for e in ENGS}
        for i in range(n):
            if ndep[i] == 0:
                heapq.heappush(heaps[seg[i][0]], (0.0, i))
        done = 0
        while done < n:
            best = None
            for e in ENGS:
                if heaps[e]:
                    rt, i = heaps[e][0]
                    st = max(rt, free[e])
                    if best is None or (st, i) < (best[0], best[2]):
                        best = (st, e, i)
            st, e, i = best
            heapq.heappop(heaps[e])
            eng, fn, reads, writes, dma, inc, cost = seg[i]
            start[i] = st
            free[e] = st + cost
            finish[i] = st + (DMA_LAT if dma else cost)
            done += 1
            for k in succ[i]:
                lat = LAT_S if seg[k][0] == e and not dma else LAT_X
                ready_t[k] = max(ready_t[k], finish[i] + lat)
                ndep[k] -= 1
                if ndep[k] == 0:
                    heapq.heappush(heaps[seg[k][0]], (ready_t[k], k))
        order = sorted(range(n), key=lambda i: (start[i], i))
        self.est_time += max(finish) if n else 0.0
        return [seg[i] for i in order]

    def finalize(self, reorder=True):
        self.est_time = 0.0
        seg = []
        for item in self.raw + [None]:
            if item is None:
                ops = self._schedule_segment(seg) if (reorder and seg) else seg
                for (eng, fn, reads, writes, dma, inc, cost) in ops:
                    self._emit_op(eng, fn, reads, writes, dma, inc)
                self._emit_barrier()
                seg = []
            else:
                seg.append(item)

    def _emit_op(self, eng, fn, reads=(), writes=(), dma=False, inc=True):
        waits = {}
        for r in reads:
            self._need(eng, waits, r.w)
        for r in writes:
            self._need(eng, waits, r.w)
            for tok in r.r.values():
                self._need(eng, waits, tok)
        if dma:
            i = self.dma_rr[eng]
            self.dma_rr[eng] = (i + 1) % self.n_dma_sems
            key = ("dma", eng, i)
            prev = self.dma_cnt.get(key, 0)
            if prev:
                self._need(eng, waits, (key, prev))
            tok = (key, prev + 16)
            self.dma_cnt[key] = prev + 16
        elif inc:
            self.cnt[eng] += 1
            tok = (eng, self.cnt[eng])
        else:
            tok = None
        for k, v in waits.items():
            self.seen[eng][k] = v
        self.ops[eng].append((sorted(waits.items(), key=str), fn, tok))
        self.nops += 1
        if tok is None:
            self.pending[eng].append((tuple(reads), tuple(writes)))
            return
        allr, allw = list(reads), list(writes)
        if not dma:
            for (pr, pw) in self.pending[eng]:
                allr += pr
                allw += pw
            self.pending[eng] = []
        rkey = tok[0]
        for r in allr:
            r.r[rkey] = tok
        for r in allw:
            r.w = tok
            r.r = {}

    def _emit_barrier(self):
        toks = [(e, self.cnt[e]) for e in ENGS if self.cnt[e]]
        toks += [(k, v) for k, v in self.dma_cnt.items()]
        for e in ENGS:
            waits = {}
            for t in toks:
                self._need(e, waits, t)
            if waits:
                for k, v in waits.items():
                    self.seen[e][k] = v
                self.ops[e].append((sorted(waits.items(), key=str), None, None))

    def emit(self, reorder=True):
        self.finalize(reorder)
        nc = self.nc
        with contextlib.ExitStack() as es:
            sems = {}
            for e in ENGS:
                sems[e] = es.enter_context(nc.semaphore("s_" + e))
            for key in self.dma_cnt:
                sems[key] = es.enter_context(nc.semaphore("d_%s_%d" % (key[1], key[2])))
            block = es.enter_context(nc.Block())

            def run(eng_name):
                def body(eng):
                    for waits, fn, tok in self.ops[eng_name]:
                        for k, v in waits:
                            eng.wait_ge(sems[k], v)
                        if fn is None:
                            continue
                        ins = fn(eng)
                        if tok is not None:
                            key, _ = tok
                            ins.then_inc(sems[key], 16 if isinstance(key, tuple) else 1)
                return body

            block.tensor(run("pe"))
            block.scalar(run("act"))
            block.vector(run("dve"))
            block.gpsimd(run("pool"))
            block.sync(run("sp"))


class Arena:
    def __init__(self, nc, base, limit):
        self.nc, self.base, self.limit, self.off, self.n = nc, base, limit, base, 0
        self.marks = []
        self.names = {}

    def alloc(self, name, free_elems, dtype):
        nb = free_elems * (2 if dtype == BF16 else 4)
        nb = (nb + 63) // 64 * 64
        assert self.off + nb <= self.limit, "SBUF overflow at %s: %d + %d > %d" % (name, self.off, nb, self.limit)
        self.n += 1
        t = self.nc.alloc_sbuf_tensor_at("%s_%d" % (name, self.n), [128, free_elems], dtype, offset=self.off)
        self.off += nb
        self.names.setdefault(name, []).append("%s_%d" % (name, self.n))
        return t

    def mark(self):
        self.marks.append(self.off)

    def release(self):
        self.off = self.marks.pop()


def sb(t, p0, np_, off, dims):
    F = 1
    for s in t.shape[1:]:
        F *= s
    return bass.AP(t, p0 * F + off, [[F, np_]] + [list(d) for d in dims])


COLS = {}
_o = 0
for _n, _w in [("c", 8), ("b_ada", 72), ("n1g", 8), ("n2g", 8), ("n3g", 8), ("mu", 15), ("w0", 4), ("a0", 4),
               ("k_k", 4), ("k_a", 4), ("r_k", 4), ("lnx_g", 4), ("lnx_b", 4), ("qg", 1), ("kg", 1)]:
    COLS[_n] = (_o, _w)
    _o += _w
NCOL = _o

MG = 256
NCH = MG // 64
C_NDL, C_NSU, C_UI, C_BD, C_MRES = 256, 384, 512, 640, 768
C_NDU, C_UI2, C_EM = 768 + MG, 768 + MG + 128, 768 + MG + 256
NCONST = 768 + MG + 384
C0 = float(np.exp(-0.5))
NEG = -30000.0


class K:
    pass


def build(NT, dbg=None, upto="all", flags=()):
    assert NT % 512 == 0
    NG = NT // 512
    nc = bass.Bass("TRN2", target_bir_lowering=False)
    P = Prog(nc, self_sync=("selfsync_off" not in flags))
    dram = {}

    def din(name, shape, dt=F32):
        dram[name] = nc.dram_tensor(name, list(shape), dt, kind="ExternalInput")
        return dram[name]

    x_d = din("x", [NT, D])
    wada_d = din("wada", [128, 8, 9 * D])
    cols_d = din("cols", [128, NCOL])
    consts_d = din("consts", [128, NCONST])
    f1w1_d = din("f1w1", [128, 8 * DFF])
    f1w3_d = din("f1w3", [128, 8 * DFF])
    f1w2_d = din("f1w2", [128, NFC * D])
    win_d = din("win", [128, 8 * NIN])
    wout_d = din("wout", [128, 8 * D])
    wdu_d = din("wdu", [64, 512])
    wau_d = din("wau", [64, 512])
    wgu_d = din("wgu", [128, 512])
    biasT_d = din("biasT", [128, 8 * 640])
    f2w1_d = din("f2w1", [128, 8 * DFF])
    f2w3_d = din("f2w3", [128, 8 * DFF])
    f2w2_d = din("f2w2", [128, NFC * D])
    out_d = nc.dram_tensor("out", [NT, D], F32, kind="ExternalOutput")
    h1_d = nc.dram_tensor("h1s", [NT, D], F32)
    h2_d = nc.dram_tensor("h2s", [NT, D], F32)
    dbg_d = {}
    if dbg:
        for name, shape in dbg.items():
            dbg_d[name] = nc.dram_tensor("dbg_" + name, list(shape), F32, kind="ExternalOutput")

    A = Arena(nc, 16384, 16384 + 212736)
    cols = A.alloc("cols", NCOL, F32)
    consts = A.alloc("consts", NCONST, F32)
    modT = A.alloc("modT", 72, F32)
    Gs = A.alloc("Gs", 24, F32)
    gcol = A.alloc("gcol", 24, F32)
    sc = A.alloc("sc", 8, F32)
    r_cols, r_consts, r_modT, r_Gs, r_gcol, r_sc = [Res(n) for n in ("cols", "consts", "modT", "Gs", "gcol", "sc")]

    def col(name, j=0, w=1, p0=0, np_=128):
        o, _ = COLS[name]
        return sb(cols, p0, np_, o + j, [[1, w]])

    ident = sb(consts, 0, 128, 0, [[1, 128]])
    ones = sb(consts, 0, 128, 128, [[1, 128]])

    P.op("sp", lambda e: e.dma_start(out=cols[:], in_=cols_d.ap()), writes=[r_cols], dma=True)
    P.op("sp", lambda e: e.dma_start(out=consts[:], in_=consts_d.ap()), writes=[r_consts], dma=True)

    psum = [nc.alloc_psum_tensor("ps%d" % i, [128, 512], F32) for i in range(8)]
    r_ps = [Res("ps%d" % i) for i in range(8)]
    r_psq = [[Res("ps%d_%d" % (i, q)) for q in range(4)] for i in range(8)]

    A.mark()
    wst = [A.alloc("wst", 8 * 512, F32) for _ in range(2)]
    r_wst = [Res("wst0"), Res("wst1")]
    r_diag = [Res("diag0"), Res("diag1")]
    P.op("act", lambda e: e.activation(out=sc[:], in_=col("c", 0, 8), func=AF.Silu), reads=[r_cols], writes=[r_sc])
    for slab in range(18):
        s = slab % 2
        P.op("sp", lambda e, slab=slab, s=s: e.dma_start(
            out=sb(wst[s], 0, 128, 0, [[512, 8], [1, 512]]), in_=wada_d.ap()[:, :, slab * 512:(slab + 1) * 512]),
            writes=[r_wst[s]], dma=True)
        for cb in range(4):
            j = slab * 4 + cb
            for kc in range(8):
                P.op("pe", lambda e, s=s, cb=cb, kc=kc, j=j: e.matmul(
                    sb(psum[0], 0, 128, j, [[1, 1]]),
                    sb(wst[s], 0, 128, kc * 512 + cb * 128, [[1, 128]]),
                    sb(sc, 0, 128, kc, [[1, 1]]), start=(kc == 0), stop=(kc == 7)),
                    reads=[r_wst[s], r_sc], writes=[r_ps[0]], inc=(kc == 7))
    P.op("dve", lambda e: e.tensor_tensor(out=modT[:], in0=sb(psum[0], 0, 128, 0, [[1, 72]]), in1=col("b_ada", 0, 72),
                                          op=ALU.add), reads=[r_ps[0], r_cols], writes=[r_modT])
    for n in range(3):
        gname = ("n1g", "n2g", "n3g")[n]
        P.op("dve", lambda e, n=n, gname=gname: e.scalar_tensor_tensor(
            out=sb(Gs, 0, 128, n * 8, [[1, 8]]), in0=sb(modT, 0, 128, (3 * n + 1) * 8, [[1, 8]]), scalar=1.0,
            in1=col(gname, 0, 8), op0=ALU.add, op1=ALU.mult), reads=[r_modT, r_cols], writes=[r_Gs])
        P.op("dve", lambda e, n=n: e.tensor_scalar(
            out=sb(gcol, 0, 128, n * 8, [[1, 8]]), in0=sb(modT, 0, 128, (3 * n + 2) * 8, [[1, 8]]),
            scalar1=(1.0 if n == 1 else 0.5), scalar2=None, op0=ALU.mult), reads=[r_modT], writes=[r_gcol])

    def shiftcol(n, kc):
        return sb(modT, 0, 128, 3 * n * 8 + kc, [[1, 1]])

    def Gcol(n, kc):
        return sb(Gs, 0, 128, n * 8 + kc, [[1, 1]])

    A.release()
    P.barrier()

    def make_gate_bc(n, gate_t, r_gate_t):
        A.mark()
        diag = [A.alloc("diag", 512, F32) for _ in range(2)]
        for half in range(2):
            s = half
            for q in range(4):
                kc = half * 4 + q
                P.op("dve", lambda e, s=s, q=q, kc=kc: e.tensor_scalar(
                    out=sb(diag[s], 0, 128, q * 128, [[1, 128]]), in0=ident,
                    scalar1=sb(gcol, 0, 128, n * 8 + kc, [[1, 1]]), scalar2=None, op0=ALU.mult),
                    reads=[r_gcol, r_consts], writes=[r_diag[s]])
            P.op("pe", lambda e, s=s: e.matmul(psum[1][:], ones, diag[s][:], start=True, stop=True),
                 reads=[r_diag[s], r_consts], writes=[r_ps[1]])
            P.op("act", lambda e, half=half: e.activation(
                out=sb(gate_t, 0, 128, half * 512, [[1, 512]]), in_=psum[1][:], func=AF.Copy),
                reads=[r_ps[1]], writes=[r_gate_t])
        A.release()
        P.barrier()

    if dbg and "modT" in dbg:
        P.op("sp", lambda e: e.dma_start(out=dbg_d["modT"].ap(), in_=modT[:]), reads=[r_modT], dma=True)

    def ffn_phase(nidx, src_d, dst_d, w1_d, w3_d, w2_d):
        A.mark()
        w1b = A.alloc("w1b", 8 * DFF, BF16)
        w3b = A.alloc("w3b", 8 * DFF, BF16)
        w2b = A.alloc("w2b", NFC * D, BF16)
        r_w = {"w1": Res("w1"), "w3": Res("w3"), "w2": Res("w2")}
        A.mark()
        stg = [A.alloc("stg", DFF, F32) for _ in range(3)]
        r_stg = [Res("stg%d" % i) for i in range(3)]
        k = 0
        for (wd, wb, rw) in ((w1_d, w1b, r_w["w1"]), (w3_d, w3b, r_w["w3"]), (w2_d, w2b, r_w["w2"])):
            for ch in range(8):
                s = k % 3
                P.op("sp", lambda e, wd=wd, ch=ch, s=s: e.dma_start(
                    out=stg[s][:], in_=wd.ap()[:, ch * DFF:(ch + 1) * DFF]), writes=[r_stg[s]], dma=True)
                ceng = ("pool", "dve", "act")[k % 3]
                if ceng == "act":
                    P.op("act", lambda e, wb=wb, ch=ch, s=s: e.activation(
                        out=sb(wb, 0, 128, ch * DFF, [[1, DFF]]), in_=stg[s][:], func=AF.Copy),
                        reads=[r_stg[s]], writes=[rw])
                else:
                    P.op(ceng, lambda e, wb=wb, ch=ch, s=s: e.tensor_copy(
                        out=sb(wb, 0, 128, ch * DFF, [[1, DFF]]), in_=stg[s][:]),
                        reads=[r_stg[s]], writes=[rw])
                k += 1
        A.release()
        P.barrier()
        A.mark()
        gate_t = A.alloc("gate_bc", D, F32)
        r_gate_t = Res("gate")
        make_gate_bc(nidx, gate_t, r_gate_t)
        nT = A.alloc("nT", 8 * 512, BF16)
        gT = A.alloc("gT", NFC * 512, BF16)
        r_nT, r_gT = Res("nT"), Res("gT")
        hT = [A.alloc("hT", D, F32) for _ in range(2)]
        hB = [A.alloc("hB", D, F32) for _ in range(2)]
        zb = [A.alloc("zb", D, BF16) for _ in range(2)]
        s1 = [A.alloc("s1", 512, F32) for _ in range(2)]
        tmp = [A.alloc("tmp", 512, F32) for _ in range(2)]
        stat = A.alloc("stat", 16, F32)
        r_hT = [Res("hT0"), Res("hT1")]
        r_hB = [Res("hB0"), Res("hB1")]
        r_zb = [Res("zb0"), Res("zb1")]
        r_stat = [Res("stat0"), Res("stat1")]
        r_s1 = [Res("s10"), Res("s11")]
        r_tmp = [Res("tmp0"), Res("tmp1")]
        tp_bf = psum[0][:].bitcast(BF16)
        cnt = {"t": 0, "a": 0, "b": 0}

        def stageT(g):
            for st in range(4):
                i = cnt["t"] % 2
                cnt["t"] += 1
                tok0 = g * 512 + st * 128
                P.op("sp", lambda e, i=i, tok0=tok0: e.dma_start(out=hT[i][:], in_=src_d.ap()[tok0:tok0 + 128, :]),
                     writes=[r_hT[i]], dma=True)
                P.op("act", lambda e, i=i: e.activation(out=zb[i][:], in_=hT[i][:], func=AF.Square,
                                                         accum_out=sb(stat, 0, 128, i * 4, [[1, 1]])),
                     reads=[r_hT[i]], writes=[r_zb[i], r_stat[i]])
                P.op("act", lambda e, i=i: e.activation(
                    out=sb(stat, 0, 128, i * 4 + 1, [[1, 1]]), in_=sb(stat, 0, 128, i * 4, [[1, 1]]),
                    func=AF.Sqrt, scale=1.0 / D, bias=sb(epsc, 0, 128, 0, [[1, 1]])),
                    reads=[r_stat[i], r_epsc], writes=[r_stat[i]])
                P.op("dve", lambda e, i=i: e.reciprocal(
                    out=sb(stat, 0, 128, i * 4 + 2, [[1, 1]]), in_=sb(stat, 0, 128, i * 4 + 1, [[1, 1]])),
                    reads=[r_stat[i]], writes=[r_stat[i]])
                P.op("dve", lambda e, i=i: e.tensor_scalar(
                    out=zb[i][:], in0=hT[i][:], scalar1=sb(stat, 0, 128, i * 4 + 2, [[1, 1]]), scalar2=None,
                    op0=ALU.mult), reads=[r_hT[i], r_stat[i]], writes=[r_zb[i]])
                for kc in range(8):
                    P.op("pe", lambda e, i=i, kc=kc: e.transpose(
                        tp_bf[:, kc * 128:(kc + 1) * 128], sb(zb[i], 0, 128, kc * 128, [[1, 128]]),
                        ident_bf), reads=[r_zb[i], r_identb], writes=[r_ps[0]], inc=(kc == 7))
                for kc in range(8):
                    P.op("act", lambda e, kc=kc, st=st: e.activation(
                        out=sb(nT, 0, 128, kc * 512 + st * 128, [[1, 128]]), in_=tp_bf[:, kc * 128:(kc + 1) * 128],
                        func=AF.Identity, scale=Gcol(nidx, kc), bias=shiftcol(nidx, kc)),
                        reads=[r_ps[0], r_Gs, r_modT], writes=[r_nT], inc=(kc == 7))

        def stageA(g):
            for f in range(NFC):
                i = cnt["a"] % 2
                cnt["a"] += 1
                for (wb, rw, bank) in ((w1b, r_w["w1"], 1 + i), (w3b, r_w["w3"], 3 + i)):
                    for kc in range(8):
                        P.op("pe", lambda e, wb=wb, kc=kc, f=f, bank=bank: e.matmul(
                            psum[bank][:], sb(wb, 0, 128, kc * DFF + f * 128, [[1, 128]]),
                            sb(nT, 0, 128, kc * 512, [[1, 512]]), start=(kc == 0), stop=(kc == 7)),
                            reads=[rw, r_nT], writes=[r_ps[bank]], inc=(kc == 7))
                P.op("act", lambda e, i=i: e.activation(out=s1[i][:], in_=psum[1 + i][:], func=AF.Silu),
                     reads=[r_ps[1 + i]], writes=[r_s1[i]])
                P.op("dve", lambda e, i=i, f=f: e.tensor_tensor(
                    out=sb(gT, 0, 128, f * 512, [[1, 512]]), in0=s1[i][:], in1=psum[3 + i][:], op=ALU.mult),
                    reads=[r_s1[i], r_ps[3 + i]], writes=[r_gT])

        def stageB(g):
            for st in range(4):
                i = cnt["b"] % 2
                cnt["b"] += 1
                tok0 = g * 512 + st * 128
                P.op("sp", lambda e, i=i, tok0=tok0: e.dma_start(out=hB[i][:], in_=src_d.ap()[tok0:tok0 + 128, :]),
                     writes=[r_hB[i]], dma=True)
                for half in range(2):
                    bank = 5 + half
                    for f in range(NFC):
                        P.op("pe", lambda e, f=f, st=st, half=half, bank=bank: e.matmul(
                            psum[bank][:], sb(gT, 0, 128, f * 512 + st * 128, [[1, 128]]),
                            sb(w2b, 0, 128, f * D + half * 512, [[1, 512]]), start=(f == 0), stop=(f == NFC - 1)),
                            reads=[r_gT, r_w["w2"]], writes=[r_ps[bank]], inc=(f == NFC - 1))
                    P.op("dve", lambda e, half=half, bank=bank: e.tensor_tensor(
                        out=tmp[half][:], in0=psum[bank][:], in1=sb(gate_t, 0, 128, half * 512, [[1, 512]]),
                        op=ALU.mult), reads=[r_ps[bank], r_gate_t], writes=[r_tmp[half]])
                    P.op("pool", lambda e, i=i, half=half: e.tensor_tensor(
                        out=sb(hB[i], 0, 128, half * 512, [[1, 512]]), in0=sb(hB[i], 0, 128, half * 512, [[1, 512]]),
                        in1=tmp[half][:], op=ALU.add), reads=[r_tmp[half], r_hB[i]], writes=[r_hB[i]])
                P.op("sp", lambda e, i=i, tok0=tok0: e.dma_start(out=dst_d.ap()[tok0:tok0 + 128, :], in_=hB[i][:]),
                     reads=[r_hB[i]], writes=[r_dst], dma=True)

        for g in range(NG + 1):
            if g < NG:
                stageT(g)
            if g >= 1:
                stageB(g - 1)
            if g < NG:
                stageA(g)
        A.release()
        A.release()
        P.barrier()


    def mixer_phase(src_d, dst_d):
        NGm = NT // MG
        NST = MG // 128
        A.mark()
        winb = A.alloc("winb", 8 * NIN, BF16)
        woutb = A.alloc("woutb", 8 * D, BF16)
        wdub = A.alloc("wdub", 512, BF16)
        waub = A.alloc("waub", 512, BF16)
        wgub = A.alloc("wgub", 512, BF16)
        biasT = A.alloc("biasT", 8 * 640, BF16)
        onesb = A.alloc("onesb", 128, BF16)
        dcols = A.alloc("dcols", 16, F32)
        r_win, r_wout, r_lora, r_biasT, r_onesb, r_dcols = [Res(n) for n in (
            "win", "wout", "lora", "biasT", "onesb", "dcols")]
        A.mark()
        stg = [A.alloc("mstg", NIN, F32) for _ in range(2)]
        r_stg = [Res("mstg0"), Res("mstg1")]
        kk_ = [0]

        def load_cast(src_ap, dst_ap, nparts, ncols, rdst):
            s_ = kk_[0] % 2
            eng = ("pool", "dve")[kk_[0] % 2]
            kk_[0] += 1
            P.op("sp", lambda e: e.dma_start(out=sb(stg[s_], 0, nparts, 0, [[1, ncols]]), in_=src_ap),
                 writes=[r_stg[s_]], dma=True)
            P.op(eng, lambda e: e.tensor_copy(out=dst_ap, in_=sb(stg[s_], 0, nparts, 0, [[1, ncols]])),
                 reads=[r_stg[s_]], writes=[rdst])

        for kc in range(8):
            load_cast(win_d.ap()[:, kc * NIN:(kc + 1) * NIN], sb(winb, 0, 128, kc * NIN, [[1, NIN]]), 128, NIN, r_win)
        for ch in range(4):
            load_cast(wout_d.ap()[:, ch * 2048:(ch + 1) * 2048], sb(woutb, 0, 128, ch * 2048, [[1, 2048]]), 128, 2048,
                      r_wout)
        load_cast(wdu_d.ap(), sb(wdub, 0, 64, 0, [[1, 512]]), 64, 512, r_lora)
        load_cast(wau_d.ap(), sb(waub, 0, 64, 0, [[1, 512]]), 64, 512, r_lora)
        load_cast(wgu_d.ap(), sb(wgub, 0, 128, 0, [[1, 512]]), 128, 512, r_lora)
        for hb_ in range(2):
            load_cast(biasT_d.ap()[:, hb_ * 2560:(hb_ + 1) * 2560], sb(biasT, 0, 128, hb_ * 2560, [[1, 2560]]), 128, 2560,
                      r_biasT)
        P.op("pool", lambda e: e.memset(onesb[:], 1.0), writes=[r_onesb])
        P.op("dve", lambda e: e.tensor_scalar(out=sb(dcols, 0, 128, 0, [[1, 4]]), in0=col("w0", 0, 4), scalar1=0.5,
                                              scalar2=None, op0=ALU.mult), reads=[r_cols], writes=[r_dcols])
        P.op("dve", lambda e: e.tensor_scalar(out=sb(dcols, 0, 128, 4, [[1, 4]]), in0=col("a0", 0, 4), scalar1=0.5,
                                              scalar2=None, op0=ALU.mult), reads=[r_cols], writes=[r_dcols])
        P.op("dve", lambda e: e.tensor_scalar(out=sb(dcols, 0, 128, 8, [[1, 4]]), in0=col("k_a", 0, 4), scalar1=-1.0,
                                              scalar2=1.0, op0=ALU.mult, op1=ALU.add), reads=[r_cols], writes=[r_dcols])
        P.op("pool", lambda e: e.memset(sb(dcols, 0, 128, 13, [[1, 1]]), 64e-5), writes=[r_dcols])
        P.op("dve", lambda e: e.tensor_scalar(out=sb(dcols, 0, 128, 12, [[1, 1]]), in0=col("qg", 0, 1), scalar1=0.125,
                                              scalar2=None, op0=ALU.mult), reads=[r_cols], writes=[r_dcols])
        A.release()
        P.barrier()

        def dcol(j):
            return sb(dcols, 0, 128, j, [[1, 1]])

        gate_t = A.alloc("gate_bc", D, F32)
        r_gate_t = Res("gate")
        make_gate_bc(1, gate_t, r_gate_t)

        def cst(c0, n=128, p0=0, np_=128):
            return sb(consts, p0, np_, c0, [[1, n]])

        carry = A.alloc("carry", 16, F32)
        Zb = A.alloc("Zb", 4 * 128, BF16)
        fKR2 = [A.alloc("fKR", NCH * 256, BF16) for _ in range(2)]
        fB2 = [A.alloc("fB", NCH * 128, BF16) for _ in range(2)]
        fK2 = [A.alloc("fK", NCH * 128, BF16) for _ in range(2)]
        fV2 = [A.alloc("fV", NCH * 128, BF16) for _ in range(2)]
        fKW2 = [A.alloc("fKW", NCH * 128, BF16) for _ in range(2)]
        fBW2 = [A.alloc("fBW", NCH * 128, BF16) for _ in range(2)]
        r_carry, r_Zb = Res("carry"), [Res("Zb%d" % j) for j in range(4)]
        r_f2 = [[Res("%s%d" % (n, q)) for n in ("fKR", "fB", "fK", "fV", "fKW", "fBW")] for q in range(2)]
        P.op("pool", lambda e: e.memset(carry[:], 0.0), writes=[r_carry])
        P.op("pool", lambda e: e.memset(Zb[:], 0.0), writes=r_Zb)
        for q in range(2):
            for t_, r_ in zip((fKR2[q], fB2[q], fK2[q], fV2[q], fKW2[q], fBW2[q]), r_f2[q]):
                P.op("pool", lambda e, t_=t_: e.memset(t_[:], 0.0), writes=[r_])
        tV2 = [A.alloc("tV", NCH * 128, BF16) for _ in range(2)]
        tKW2 = [A.alloc("tKW", NCH * 128, BF16) for _ in range(2)]
        tBW2 = [A.alloc("tBW", NCH * 128, BF16) for _ in range(2)]
        r_t2 = [[Res("%s%d" % (n, q)) for n in ("tV", "tKW", "tBW")] for q in range(2)]
        Wg2 = [A.alloc("Wg", MG, F32) for _ in range(2)]
        Wb2 = [A.alloc("Wbon", MG, F32) for _ in range(2)]
        r_Wg2, r_Wb2 = [Res("Wg0"), Res("Wg1")], [Res("Wbon0"), Res("Wbon1")]
        O2 = [A.alloc("Osc", MG, F32) for _ in range(2)]
        r_O2 = [Res("Osc0"), Res("Osc1")]
        WLt2 = [A.alloc("WLt", 8, F32) for _ in range(2)]
        r_WLt2 = [Res("WLt0"), Res("WLt1")]
        Nn = A.alloc("Nn", NCH * 128, BF16)
        NtArb = A.alloc("NtArb", NCH * 256, BF16)
        AkArk = A.alloc("AkArk", NCH * 256, BF16)
        Xb = A.alloc("Xb", NCH * 384, BF16)
        Esb = A.alloc("Esb", NCH * 128, BF16)
        nGT = A.alloc("nGT", NCH * 128, BF16)
        r_Esb, r_nGT = Res("Esb"), Res("nGT")
        nGTmI = A.alloc("nGTmI", NCH * 128, BF16)
        Xit = A.alloc("Xit", NCH * 256, BF16)
        r_nGTmI, r_Xit = Res("nGTmI"), Res("Xit")
        PP = [A.alloc("PP", NCH * 256, BF16) for _ in range(2)]
        Reff = A.alloc("Reff", NCH * 128, BF16)
        Mc = A.alloc("Mc", NCH * 128, BF16)
        r_Nn, r_NtArb, r_AkArk, r_Xb, r_Reff, r_Mc, r_WLt = [Res(n) for n in (
            "Nn", "NtArb", "AkArk", "Xb", "Reff", "Mc", "WLtX")]
        r_PP = [Res("PP0"), Res("PP1")]
        yT = A.alloc("yT", MG, F32)
        r_yT = Res("yT")
        ymixT = A.alloc("ymixT", 8 * MG, BF16)
        r_ymix = [Res("ymix%d" % i) for i in range(8)]
        qT = A.alloc("qT", 4 * MG, BF16)
        r_qT = [Res("qT%d" % i) for i in range(4)]
        KT = A.alloc("KT", 4 * 1024, BF16)
        r_KT = Res("KT")
        Vr = A.alloc("Vr", 8 * 512, BF16)
        r_Vr = Res("Vr")
        scb = [A.alloc("scb", 640, F32) for _ in range(2)]
        prb = [A.alloc("prb", 640, BF16) for _ in range(2)]
        rec = A.alloc("rec", 128, F32)
        r_scb, r_prb, r_rec = [Res("scb0"), Res("scb1")], [Res("prb0"), Res("prb1")], Res("rec")
        hT = [A.alloc("hT", D, F32) for _ in range(2)]
        zb = [A.alloc("zb", D, BF16) for _ in range(2)]
        stat = A.alloc("stat", 16, F32)
        r_hT, r_zb, r_stat = [Res("hT0"), Res("hT1")], [Res("zb0"), Res("zb1")], [Res("st0"), Res("st1")]
        nT = A.alloc("nT", 8 * MG, BF16)
        r_nT = Res("nT")
        thb = A.alloc("thb", MG, BF16)
        padb = A.alloc("padb", MG, BF16)
        sgdb = A.alloc("sgdb", MG, BF16)
        r_thb, r_padb, r_sgdb = Res("thb"), Res("padb"), Res("sgdb")
        pbuf = [A.alloc("pbuf", MG + 8, F32) for _ in range(2)]
        r_pbuf = [Res("pbuf0"), Res("pbuf1")]
        mixtmp = A.alloc("mixtmp", MG, F32)
        r_mixtmp = Res("mixtmp")
        NW = 13
        Wt = [A.alloc("W%d" % i, MG, F32) if i not in (6, 11, 12) else None for i in range(NW + 3)]
        r_W = [Res("W%d" % i) for i in range(NW + 3)]
        R_, Kx, Vx = NW, NW + 1, NW + 2
        tmpo = [A.alloc("tmpo", 512, F32) for _ in range(2)]
        r_tmpo = [Res("tmpo0"), Res("tmpo1")]

        def W(i, p0=0, np_=128, c0=0, n=MG):
            return sb(Wt[i], p0, np_, c0, [[1, n]])

        def psr(bank, c0, c1):
            return [r_ps[bank]]

        def PS(bank, c0, n, p0=0, np_=128):
            return sb(psum[bank], p0, np_, c0, [[1, n]])

        cnt = {"t": 0, "pb": 0, "att": 0}
        tp_bf = [psum[b][:].bitcast(BF16) for b in range(8)]

        def psbf(bank, c0, n):
            return tp_bf[bank][:, c0:c0 + n]

        def psr_bf(bank, c0, c1):
            return psr(bank, c0 // 2, (c1 + 1) // 2)

        def proj_cols(c0, ncols, bank):
            for kc in range(8):
                P.op("pe", lambda e, kc=kc: e.matmul(
                    PS(bank, 0, MG, 0, ncols), sb(winb, 0, 128, kc * NIN + c0, [[1, ncols]]),
                    sb(nT, 0, 128, kc * MG, [[1, MG]]), start=(kc == 0), stop=(kc == 7)),
                    reads=[r_win, r_nT], writes=psr(bank, 0, MG), inc=(kc == 7))

        def mix(blk, c0, ncols, bank, out_ap, r_out):
            proj_cols(c0, ncols, bank)
            i = cnt["pb"] % 2
            cnt["pb"] += 1
            pb = pbuf[i]
            P.op("act", lambda e: e.activation(out=sb(pb, 0, ncols, 0, [[1, 1]]), in_=sb(carry, 0, ncols, blk, [[1, 1]]),
                                               func=AF.Copy), reads=[r_carry], writes=[r_pbuf[i]])
            P.op("act", lambda e: e.activation(out=sb(pb, 0, ncols, 1, [[1, MG]]), in_=PS(bank, 0, MG, 0, ncols),
                                               func=AF.Copy), reads=psr(bank, 0, MG), writes=[r_pbuf[i]])
            P.op("act", lambda e: e.activation(out=sb(carry, 0, ncols, blk, [[1, 1]]), in_=sb(pb, 0, ncols, MG, [[1, 1]]),
                                               func=AF.Copy), reads=[r_pbuf[i]], writes=[r_carry])
            P.op("dve", lambda e: e.tensor_tensor(out=sb(mixtmp, 0, ncols, 0, [[1, MG]]), in0=sb(pb, 0, ncols, 0, [[1, MG]]),
                                                  in1=sb(pb, 0, ncols, 1, [[1, MG]]), op=ALU.subtract),
                 reads=[r_pbuf[i]], writes=[r_mixtmp])
            P.op("dve", lambda e: e.scalar_tensor_tensor(
                out=out_ap, in0=sb(mixtmp, 0, ncols, 0, [[1, MG]]), scalar=col("mu", blk, 1, 0, ncols),
                in1=sb(pb, 0, ncols, 1, [[1, MG]]), op0=ALU.mult, op1=ALU.add),
                reads=[r_mixtmp, r_pbuf[i], r_cols], writes=[r_out])

        def bsum(bank, src_i):
            P.op("pe", lambda e: e.matmul(PS(bank, 0, MG), cst(C_BD), W(src_i), start=True, stop=True),
                 reads=[r_consts, r_W[src_i]], writes=psr(bank, 0, MG))

        def norm_T(g):
            for st in range(NST):
                i = cnt["t"] % 2
                cnt["t"] += 1
                tok0 = g * MG + st * 128
                P.op("sp", lambda e, i=i, tok0=tok0: e.dma_start(out=hT[i][:], in_=src_d.ap()[tok0:tok0 + 128, :]),
                     writes=[r_hT[i]], dma=True)
                P.op("act", lambda e, i=i: e.activation(out=zb[i][:], in_=hT[i][:], func=AF.Square,
                                                         accum_out=sb(stat, 0, 128, i * 4, [[1, 1]])),
                     reads=[r_hT[i]], writes=[r_zb[i], r_stat[i]])
                P.op("dve", lambda e, i=i: e.tensor_scalar(
                    out=sb(stat, 0, 128, i * 4 + 1, [[1, 1]]), in0=sb(stat, 0, 128, i * 4, [[1, 1]]),
                    scalar1=1.0 / D, scalar2=EPS, op0=ALU.mult, op1=ALU.add), reads=[r_stat[i]], writes=[r_stat[i]])
                P.op("act", lambda e, i=i: e.activation(
                    out=sb(stat, 0, 128, i * 4 + 1, [[1, 1]]), in_=sb(stat, 0, 128, i * 4 + 1, [[1, 1]]),
                    func=AF.Sqrt), reads=[r_stat[i]], writes=[r_stat[i]])
                P.op("dve", lambda e, i=i: e.reciprocal(
                    out=sb(stat, 0, 128, i * 4 + 2, [[1, 1]]), in_=sb(stat, 0, 128, i * 4 + 1, [[1, 1]])),
                    reads=[r_stat[i]], writes=[r_stat[i]])
                P.op("dve", lambda e, i=i: e.tensor_scalar(
                    out=zb[i][:], in0=hT[i][:], scalar1=sb(stat, 0, 128, i * 4 + 2, [[1, 1]]), scalar2=None,
                    op0=ALU.mult), reads=[r_hT[i], r_stat[i]], writes=[r_zb[i]])
                for kc in range(8):
                    P.op("pe", lambda e, i=i, kc=kc: e.transpose(
                        psbf(0, kc * 128, 128), sb(zb[i], 0, 128, kc * 128, [[1, 128]]), ident_bf),
                        reads=[r_zb[i], r_identb], writes=psr_bf(0, 0, 1024), inc=(kc == 7))
                for kc in range(8):
                    P.op("act", lambda e, kc=kc, st=st: e.activation(
                        out=sb(nT, 0, 128, kc * MG + st * 128, [[1, 128]]), in_=psbf(0, kc * 128, 128),
                        func=AF.Identity, scale=Gcol(1, kc), bias=shiftcol(1, kc)),
                        reads=psr_bf(0, 0, 1024) + [r_Gs, r_modT], writes=[r_nT], inc=(kc == 7))

        def QW(i):
            return sb(tmpo[i // 2], 0, 128, (i % 2) * MG, [[1, MG]])

        def qk_norm(c0, gsc, dst_ap, r_dst_, bank, bank2):
            proj_cols(c0, 128, bank)
            P.op("act", lambda e: e.activation(out=QW(0), in_=PS(bank, 0, MG), func=AF.Copy),
                 reads=psr(bank, 0, MG), writes=[r_tmpo[0]])
            P.op("act", lambda e: e.activation(out=QW(1), in_=PS(bank, 0, MG), func=AF.Square),
                 reads=psr(bank, 0, MG), writes=[r_tmpo[0]])
            P.op("pe", lambda e: e.matmul(PS(bank2, 0, MG), cst(C_BD), QW(1), start=True, stop=True),
                 reads=[r_consts, r_tmpo[0]], writes=psr(bank2, 0, MG))
            P.op("dve", lambda e: e.tensor_scalar(out=QW(2), in0=PS(bank2, 0, MG), scalar1=1.0 / 64, scalar2=1e-6,
                                                  op0=ALU.mult, op1=ALU.add), reads=psr(bank2, 0, MG), writes=[r_tmpo[1]])
            P.op("act", lambda e: e.activation(out=QW(2), in_=QW(2), func=AF.Sqrt), reads=[r_tmpo[1]], writes=[r_tmpo[1]])
            P.op("dve", lambda e: e.reciprocal(out=QW(2), in_=QW(2)), reads=[r_tmpo[1]], writes=[r_tmpo[1]])
            P.op("dve", lambda e: e.scalar_tensor_tensor(out=dst_ap, in0=QW(0), scalar=gsc, in1=QW(2), op0=ALU.mult,
                                                         op1=ALU.mult), reads=[r_tmpo[0], r_tmpo[1], r_cols, r_dcols],
                 writes=[r_dst_])

        def attn_proj(g):
            slot = (g * MG) % 1024
            for cb in range(4):
                qk_norm(1792 + cb * 128, dcol(12), sb(qT, 0, 128, cb * MG, [[1, MG]]), r_qT[cb], 1, 2)
                qk_norm(2304 + cb * 128, col("kg", 0, 1), sb(KT, 0, 128, cb * 1024 + slot, [[1, MG]]), r_KT, 1, 2)
            for st in range(NST):
                tile = (g * NST + st) % 8
                for kc in range(8):
                    P.op("pe", lambda e, kc=kc, st=st: e.matmul(
                        PS(1, 0, 512), sb(nT, 0, 128, kc * MG + st * 128, [[1, 128]]),
                        sb(winb, 0, 128, kc * NIN + 2816, [[1, 512]]), start=(kc == 0), stop=(kc == 7)),
                        reads=[r_win, r_nT], writes=psr(1, 0, 512), inc=(kc == 7))
                P.op("act", lambda e, tile=tile: e.activation(out=sb(Vr, 0, 128, tile * 512, [[1, 512]]),
                                                             in_=PS(1, 0, 512), func=AF.Copy),
                     reads=psr(1, 0, 512), writes=[r_Vr])

        def attn(g):
            for st in range(NST):
                T = g * NST + st
                jbs = [jb for jb in range(5) if T - 4 + jb >= 0]
                j0 = jbs[0]
                for cb in range(4):
                    for hh in range(2):
                        head = cb * 2 + hh
                        p0 = hh * 64
                        par = cnt["att"] % 2
                        cnt["att"] += 1
                        bS = 2 + par
                        for jb in jbs:
                            kt = (T - 4 + jb) % 8
                            if jb < 4:
                                o_ap, o_r = PS(bS, jb * 128, 128), psr(bS, jb * 128, jb * 128 + 128)
                            else:
                                o_ap, o_r = PS(4, par * 128, 128), psr(4, par * 128, par * 128 + 128)
                            P.op("pe", lambda e, o_ap=o_ap, kt=kt, cb=cb, p0=p0, st=st: e.matmul(
                                o_ap, sb(KT, p0, 64, cb * 1024 + kt * 128, [[1, 128]]),
                                sb(qT, p0, 64, cb * MG + st * 128, [[1, 128]]), start=True, stop=True),
                                reads=[r_KT, r_qT[cb]], writes=o_r)
                        if j0 < 4:
                            P.op("dve", lambda e, par=par, bS=bS, head=head, j0=j0: e.tensor_tensor(
                                out=sb(scb[par], 0, 128, j0 * 128, [[1, 512 - j0 * 128]]),
                                in0=PS(bS, j0 * 128, 512 - j0 * 128),
                                in1=sb(biasT, 0, 128, head * 640 + j0 * 128, [[1, 512 - j0 * 128]]), op=ALU.add),
                                reads=psr(bS, j0 * 128, 512) + [r_biasT], writes=[r_scb[par]])
                        P.op("dve", lambda e, par=par, head=head: e.tensor_tensor(
                            out=sb(scb[par], 0, 128, 512, [[1, 128]]), in0=PS(4, par * 128, 128),
                            in1=sb(biasT, 0, 128, head * 640 + 512, [[1, 128]]), op=ALU.add),
                            reads=psr(4, par * 128, par * 128 + 128) + [r_biasT], writes=[r_scb[par]])
                        P.op("act", lambda e, par=par, j0=j0: e.activation(
                            out=sb(prb[par], 0, 128, j0 * 128, [[1, 640 - j0 * 128]]),
                            in_=sb(scb[par], 0, 128, j0 * 128, [[1, 640 - j0 * 128]]), func=AF.Exp),
                            reads=[r_scb[par]], writes=[r_prb[par]])
                        for jb in jbs:
                            kt = (T - 4 + jb) % 8
                            P.op("pe", lambda e, jb=jb, kt=kt, cb=cb, hh=hh, par=par, f0=(jb == jbs[0]), f1=(jb == jbs[-1]): e.matmul(
                                PS(5, hh * 128, 128), sb(Vr, 0, 128, kt * 512 + cb * 128, [[1, 128]]),
                                sb(prb[par], 0, 128, jb * 128, [[1, 128]]), start=f0, stop=f1),
                                reads=[r_Vr, r_prb[par]], writes=psr(5, hh * 128, hh * 128 + 128), inc=(jb == jbs[-1]))
                        for jb in jbs:
                            P.op("pe", lambda e, jb=jb, hh=hh, par=par, f0=(jb == jbs[0]), f1=(jb == jbs[-1]): e.matmul(
                                PS(5, 256 + hh * 128, 128), onesb[:],
                                sb(prb[par], 0, 128, jb * 128, [[1, 128]]), start=f0, stop=f1),
                                reads=[r_onesb, r_prb[par]], writes=psr(5, 256 + hh * 128, 256 + hh * 128 + 128),
                                inc=(jb == jbs[-1]))
                        P.op("dve", lambda e, hh=hh, p0=p0: e.reciprocal(
                            out=sb(rec, p0, 64, 0, [[1, 128]]), in_=PS(5, 256 + hh * 128, 128, p0, 64)),
                            reads=psr(5, 256 + hh * 128, 256 + hh * 128 + 128), writes=[r_rec])
                        P.op("dve", lambda e, hh=hh, p0=p0, cb=cb, st=st: e.tensor_tensor(
                            out=sb(ymixT, p0, 64, (4 + cb) * MG + st * 128, [[1, 128]]), in0=PS(5, hh * 128, 128, p0, 64),
                            in1=sb(rec, p0, 64, 0, [[1, 128]]), op=ALU.mult),
                            reads=psr(5, hh * 128, hh * 128 + 128) + [r_rec], writes=[r_ymix[4 + cb]])

        def lora_inputs():
            mix(12, 1536, 64, 1, sb(Wt[0], 0, 64, 0, [[1, MG]]), r_W[0])
            P.op("act", lambda e: e.activation(out=sb(thb, 0, 64, 0, [[1, MG]]), in_=sb(Wt[0], 0, 64, 0, [[1, MG]]),
                                               func=AF.Tanh), reads=[r_W[0]], writes=[r_thb])
            mix(13, 1600, 64, 2, sb(Wt[1], 0, 64, 0, [[1, MG]]), r_W[1])
            P.op("act", lambda e: e.activation(out=sb(padb, 0, 64, 0, [[1, MG]]), in_=sb(Wt[1], 0, 64, 0, [[1, MG]]),
                                               func=AF.Copy), reads=[r_W[1]], writes=[r_padb])
            mix(14, 1664, 128, 1, W(2), r_W[2])
            P.op("act", lambda e: e.activation(out=W(2), in_=W(2), func=AF.Tanh, scale=0.5), reads=[r_W[2]],
                 writes=[r_W[2]])
            P.op("dve", lambda e: e.tensor_scalar(out=sgdb[:], in0=W(2), scalar1=0.5, scalar2=0.5, op0=ALU.mult,
                                                  op1=ALU.add), reads=[r_W[2]], writes=[r_sgdb])

        def v3(t, p0, np_, off, cs, n=64):
            return sb(t, p0, np_, off, [[cs, NCH], [1, n]])

        def rwkv_prep(j):
            q_ = j % 2
            fKR, fB, fK, fV, fKW, fBW = fKR2[q_], fB2[q_], fK2[q_], fV2[q_], fKW2[q_], fBW2[q_]
            r_fKR, r_fB, r_fK, r_fV, r_fKW, r_fBW = r_f2[q_]
            tV, tKW, tBW = tV2[q_], tKW2[q_], tBW2[q_]
            r_tV, r_tKW, r_tBW = r_t2[q_]
            WLt, r_WLt = WLt2[q_], r_WLt2[q_]
            Wg, r_Wg, Wbon, r_Wbon = Wg2[q_], r_Wg2[q_], Wb2[q_], r_Wb2[q_]
            mix(j, j * 128, 128, 1, W(R_), r_W[R_])
            mix(4 + j, 512 + j * 128, 128, 2, W(Kx), r_W[Kx])
            mix(8 + j, 1024 + j * 128, 128, 1, W(Vx), r_W[Vx])
            P.op("pe", lambda e: e.matmul(PS(2, 0, MG), sb(wdub, 0, 64, j * 128, [[1, 128]]),
                                          sb(thb, 0, 64, 0, [[1, MG]]), start=True, stop=True),
                 reads=[r_lora, r_thb], writes=psr(2, 0, MG))
            P.op("act", lambda e: e.activation(out=W(0), in_=PS(2, 0, MG), func=AF.Tanh, scale=0.5, bias=dcol(j)),
                 reads=psr(2, 0, MG) + [r_dcols], writes=[r_W[0]])
            P.op("dve", lambda e: e.tensor_scalar(out=W(0), in0=W(0), scalar1=0.5, scalar2=0.5, op0=ALU.mult,
                                                  op1=ALU.add), reads=[r_W[0]], writes=[r_W[0]])
            P.op("dve", lambda e: e.tensor_tensor_scan(out=W(1), data0=cst(C_MRES, MG), data1=W(0), initial=0.0,
                                                       op0=ALU.mult, op1=ALU.add), reads=[r_W[0], r_consts],
                 writes=[r_W[1]])
            P.op("act", lambda e: e.activation(out=W(2), in_=W(1), func=AF.Exp, scale=-C0), reads=[r_W[1]],
                 writes=[r_W[2]])
            P.op("act", lambda e: e.activation(out=W(3), in_=W(1), func=AF.Exp, scale=C0), reads=[r_W[1]],
                 writes=[r_W[3]])
            P.op("dve", lambda e: e.tensor_tensor(out=v3(Wt[4], 0, 128, 0, 64), in0=sb(Wt[1], 0, 128, 63, [[64, NCH], [0, 64]]),
                                                  in1=v3(Wt[1], 0, 128, 0, 64), op=ALU.subtract), reads=[r_W[1]],
                 writes=[r_W[4]])
            P.op("act", lambda e: e.activation(out=W(4), in_=W(4), func=AF.Exp, scale=-C0), reads=[r_W[4]],
                 writes=[r_W[4]])
            P.op("act", lambda e: e.activation(out=sb(WLt, 0, 128, 0, [[1, NCH]]), in_=sb(Wt[1], 0, 128, 63, [[64, NCH]]),
                                               func=AF.Exp, scale=-C0), reads=[r_W[1]], writes=[r_WLt])
            P.op("pe", lambda e: e.matmul(PS(1, 0, MG), sb(waub, 0, 64, j * 128, [[1, 128]]),
                                          sb(padb, 0, 64, 0, [[1, MG]]), start=True, stop=True),
                 reads=[r_lora, r_padb], writes=psr(1, 0, MG))
            P.op("act", lambda e: e.activation(out=W(5), in_=PS(1, 0, MG), func=AF.Tanh, scale=0.5, bias=dcol(4 + j)),
                 reads=psr(1, 0, MG) + [r_dcols], writes=[r_W[5]])
            P.op("dve", lambda e: e.tensor_scalar(out=W(5), in0=W(5), scalar1=0.5, scalar2=0.5, op0=ALU.mult,
                                                  op1=ALU.add), reads=[r_W[5]], writes=[r_W[5]])
            P.op("pe", lambda e: e.matmul(PS(2, 0, MG), sb(wgub, 0, 128, j * 128, [[1, 128]]), sgdb[:],
                                          start=True, stop=True), reads=[r_lora, r_sgdb], writes=psr(2, 0, MG))
            P.op("act", lambda e: e.activation(out=Wg[:], in_=PS(2, 0, MG), func=AF.Copy), reads=psr(2, 0, MG),
                 writes=[r_Wg])
            P.op("act", lambda e: e.activation(out=W(7), in_=W(Kx), func=AF.Copy, scale=col("k_k", j)),
                 reads=[r_W[Kx], r_cols], writes=[r_W[7]])
            P.op("pool", lambda e: e.tensor_tensor(out=W(8), in0=W(7), in1=W(7), op=ALU.mult), reads=[r_W[7]],
                 writes=[r_W[8]])
            bsum(1, 8)
            P.op("dve", lambda e: e.tensor_scalar(out=W(8), in0=PS(1, 0, MG), scalar1=1e-24, scalar2=None, op0=ALU.max),
                 reads=psr(1, 0, MG), writes=[r_W[8]])
            P.op("act", lambda e: e.activation(out=W(8), in_=W(8), func=AF.Sqrt), reads=[r_W[8]], writes=[r_W[8]])
            P.op("dve", lambda e: e.reciprocal(out=W(8), in_=W(8)), reads=[r_W[8]], writes=[r_W[8]])
            P.op("pool", lambda e: e.tensor_tensor(out=W(7), in0=W(7), in1=W(8), op=ALU.mult), reads=[r_W[7], r_W[8]],
                 writes=[r_W[7]])
            P.op("dve", lambda e: e.tensor_scalar(out=W(0), in0=W(5), scalar1=col("k_a", j), scalar2=dcol(8 + j),
                                                  op0=ALU.mult, op1=ALU.add), reads=[r_W[5], r_cols, r_dcols],
                 writes=[r_W[0]])
            P.op("pool", lambda e: e.tensor_tensor(out=W(9), in0=W(Kx), in1=W(0), op=ALU.mult), reads=[r_W[Kx], r_W[0]],
                 writes=[r_W[9]])
            P.op("pool", lambda e: e.tensor_tensor(out=W(10), in0=W(7), in1=W(5), op=ALU.mult), reads=[r_W[7], r_W[5]],
                 writes=[r_W[10]])
            P.op("dve", lambda e: e.scalar_tensor_tensor(out=mixtmp[:], in0=W(R_), scalar=col("r_k", j), in1=W(9),
                                                         op0=ALU.mult, op1=ALU.mult), reads=[r_W[R_], r_W[9], r_cols],
                 writes=[r_mixtmp])
            P.op("pe", lambda e: e.matmul(PS(2, 0, MG), cst(C_BD), mixtmp[:], start=True, stop=True),
                 reads=[r_consts, r_mixtmp], writes=psr(2, 0, MG))
            P.op("dve", lambda e: e.tensor_tensor(out=Wbon[:], in0=PS(2, 0, MG), in1=W(Vx), op=ALU.mult),
                 reads=psr(2, 0, MG) + [r_W[Vx]], writes=[r_Wbon])
            k_ = 0
            for hh in range(2):
                p0 = hh * 64
                specs = [
                    (fKR, 256, 128 + hh * 64, r_fKR, R_, 2),
                    (fB, 128, hh * 64, r_fB, 10, 3),
                    (fK, 128, hh * 64, r_fK, 9, 3),
                    (fKW, 128, hh * 64, r_fKW, 9, 4),
                    (fBW, 128, hh * 64, r_fBW, 10, 4),
                ]
                for (dst, cs, off, rd, a_i, b_i) in specs:
                    eng = ("dve", "pool")[k_ % 2]
                    k_ += 1
                    P.op(eng, lambda e, dst=dst, cs=cs, off=off, a_i=a_i, b_i=b_i, p0=p0: e.tensor_tensor(
                        out=v3(dst, p0, 64, off, cs), in0=v3(Wt[a_i], p0, 64, 0, 64), in1=v3(Wt[b_i], p0, 64, 0, 64),
                        op=ALU.mult), reads=[r_W[a_i], r_W[b_i]], writes=[rd])
                P.op("pool", lambda e, p0=p0, hh=hh: e.tensor_copy(out=v3(fV, p0, 64, hh * 64, 128),
                                                                   in_=v3(Wt[Vx], p0, 64, 0, 64)),
                     reads=[r_W[Vx]], writes=[r_fV])
                P.op("dve", lambda e, p0=p0, hh=hh: e.tensor_tensor(
                    out=v3(fKR, p0, 64, hh * 64 + 1, 256, 63), in0=v3(Wt[7], p0, 64, 1, 64, 63),
                    in1=v3(Wt[2], p0, 64, 0, 64, 63), op=ALU.mult), reads=[r_W[7], r_W[2]], writes=[r_fKR])
                P.op("pool", lambda e, p0=p0, hh=hh: e.tensor_copy(
                    out=v3(fKR, p0, 64, hh * 64, 256, 1), in_=v3(Wt[7], p0, 64, 0, 64, 1)),
                    reads=[r_W[7]], writes=[r_fKR])
            for (src, rs, dst, rd, bank, c0) in ((fV, r_fV, tV, r_tV, 1, 0), (fKW, r_fKW, tKW, r_tKW, 1, 512),
                                                 (fBW, r_fBW, tBW, r_tBW, 2, 0)):
                for c in range(NCH):
                    P.op("pe", lambda e, src=src, c=c, bank=bank, c0=c0: e.transpose(
                        psbf(bank, c0 + c * 128, 128), sb(src, 0, 128, c * 128, [[1, 128]]), ident_bf),
                        reads=[rs, r_identb], writes=psr_bf(bank, c0, c0 + 512), inc=(c == NCH - 1))
                P.op("act", lambda e, dst=dst, bank=bank, c0=c0: e.activation(
                    out=dst[:], in_=psbf(bank, c0, NCH * 128), func=AF.Copy),
                    reads=psr_bf(bank, c0, c0 + 512), writes=[rd])

        def rwkv_chunks(j):
            q_ = j % 2
            fKR, fB, fK, fV, fKW, fBW = fKR2[q_], fB2[q_], fK2[q_], fV2[q_], fKW2[q_], fBW2[q_]
            r_fKR, r_fB, r_fK, r_fV, r_fKW, r_fBW = r_f2[q_]
            tV, tKW, tBW = tV2[q_], tKW2[q_], tBW2[q_]
            r_tV, r_tKW, r_tBW = r_t2[q_]
            WLt, r_WLt = WLt2[q_], r_WLt2[q_]
            Wg, r_Wg, Wbon, r_Wbon = Wg2[q_], r_Wg2[q_], Wb2[q_], r_Wb2[q_]
            def fKA(c):
                return sb(fKR, 0, 128, c * 256, [[1, 128]])

            def fR(c):
                return sb(fKR, 0, 128, c * 256 + 128, [[1, 128]])

            def blk(t, c, w=128, o=0):
                return sb(t, 0, 128, c * w + o, [[1, 128]])

            for c in range(NCH):
                P.op("pe", lambda e, c=c: e.matmul(PS(3, c * 128, 128), fKA(c), blk(fB, c), start=True, stop=True),
                     reads=[r_fKR, r_fB], writes=psr(3, c * 128, c * 128 + 128))
            P.op("dve", lambda e: e.tensor_tensor(
                out=sb(Nn, 0, 128, 0, [[128, NCH], [1, 128]]), in0=sb(psum[3], 0, 128, 0, [[128, NCH], [1, 128]]),
                in1=sb(consts, 0, 128, C_NDL, [[0, NCH], [1, 128]]), op=ALU.mult),
                reads=psr(3, 0, 512) + [r_consts], writes=[r_Nn])
            P.op("dve", lambda e: e.tensor_tensor(
                out=sb(Esb, 0, 128, 0, [[128, NCH], [1, 128]]), in0=sb(psum[3], 0, 128, 0, [[128, NCH], [1, 128]]),
                in1=sb(consts, 0, 128, C_EM, [[0, NCH], [1, 128]]), op=ALU.mult),
                reads=psr(3, 0, 512) + [r_consts], writes=[r_Esb])
            for (lh, rl, dst, rd, b0, cm) in ((fB, r_fB, NtArb, r_NtArb, 4, C_NDU), (fK, r_fK, AkArk, r_AkArk, 6, C_NSU)):
                for c in range(NCH):
                    P.op("pe", lambda e, c=c, lh=lh, b0=b0: e.matmul(
                        PS(b0 + c // 2, (c % 2) * 256, 256), blk(lh, c), sb(fKR, 0, 128, c * 256, [[1, 256]]),
                        start=True, stop=True), reads=[rl, r_fKR],
                        writes=psr(b0 + c // 2, (c % 2) * 256, (c % 2) * 256 + 256))
                for hb in range(NCH // 2):
                    P.op("dve", lambda e, hb=hb, dst=dst, b0=b0, cm=cm: e.tensor_tensor(
                        out=sb(dst, 0, 128, hb * 512, [[256, 2], [1, 256]]),
                        in0=sb(psum[b0 + hb], 0, 128, 0, [[256, 2], [1, 256]]),
                        in1=sb(consts, 0, 128, cm, [[0, 2], [1, 256]]), op=ALU.mult),
                        reads=psr(b0 + hb, 0, 512) + [r_consts], writes=[rd])

            def XP(c, o, n):
                return PS(3 + c, o, n)

            def XR(c):
                return [r_ps[3 + c]]

            def Rb(c, o, n):
                return sb(Xb, 0, 128, c * 384 + o, [[1, n]])

            for c in range(NCH):
                P.op("pe", lambda e, c=c: e.matmul(XP(c, 0, 128), blk(AkArk, c, 256), blk(tV, c), start=True,
                                                   stop=True, skip_group_check=True),
                     reads=[r_AkArk, r_tV], writes=XR(c))
                P.op("pe", lambda e, c=c: e.matmul(XP(c, 128, 128), fKA(c), ident_bf, start=False, stop=True,
                                                   skip_group_check=True),
                     reads=[r_fKR, r_identb], writes=XR(c))
                P.op("pe", lambda e, c=c: e.matmul(XP(c, 256, 128), ident_bf, blk(Esb, c), start=False, stop=True,
                                                   skip_group_check=True),
                     reads=[r_Esb, r_identb], writes=XR(c))
            Pc = [blk(NtArb, c, 256) for c in range(NCH)]
            PTc = [blk(Nn, c) for c in range(NCH)]
            rP, rPT = [r_NtArb], [r_Nn]

            def pbank(c):
                return (7, 2)[c // 2]

            for lvl in range(4):
                for c in range(NCH):
                    P.op("act", lambda e, c=c: e.activation(out=Rb(c, 0, 384), in_=XP(c, 0, 384), func=AF.Copy),
                         reads=XR(c), writes=[r_Xb])
                if lvl >= 1:
                    pp = PP[lvl % 2]
                    for c in range(NCH):
                        P.op("pe", lambda e, c=c, Pc=Pc, PTc=PTc: e.matmul(
                            PS(pbank(c), (c % 2) * 256, 128), PTc[c], Pc[c], start=True, stop=True),
                            reads=rP + rPT, writes=[r_ps[pbank(c)]])
                        P.op("pe", lambda e, c=c, Pc=Pc, PTc=PTc: e.matmul(
                            PS(pbank(c), (c % 2) * 256 + 128, 128), Pc[c], PTc[c], start=True, stop=True),
                            reads=rP + rPT, writes=[r_ps[pbank(c)]])
                    for hb in range(NCH // 2):
                        P.op("dve", lambda e, hb=hb, pp=pp: e.tensor_copy(out=sb(pp, 0, 128, hb * 512, [[1, 512]]),
                                                                         in_=PS((7, 2)[hb], 0, 512)),
                             reads=[r_ps[(7, 2)[hb]]], writes=[r_PP[lvl % 2]])
                    Pc = [blk(pp, c, 256) for c in range(NCH)]
                    PTc = [blk(pp, c, 256, 128) for c in range(NCH)]
                    rP = rPT = [r_PP[lvl % 2]]
                for c in range(NCH):
                    P.op("pe", lambda e, c=c, Pc=Pc: e.matmul(XP(c, 0, 384), Pc[c], Rb(c, 0, 384),
                                                              start=False, stop=True, skip_group_check=True),
                         reads=rP + [r_Xb], writes=XR(c))
            for c in range(NCH):
                P.op("act", lambda e, c=c: e.activation(out=Rb(c, 0, 384), in_=XP(c, 0, 384), func=AF.Copy),
                     reads=XR(c), writes=[r_Xb])
            for c in range(NCH):
                P.op("pe", lambda e, c=c: e.transpose(psbf(1, c * 128, 128), Rb(c, 256, 128), ident_bf),
                     reads=[r_Xb, r_identb], writes=[r_ps[1]], inc=(c == NCH - 1))
            P.op("act", lambda e: e.activation(out=nGT[:], in_=psbf(1, 0, NCH * 128), func=AF.Copy, scale=-1.0),
                 reads=[r_ps[1]], writes=[r_nGT])
            P.op("dve", lambda e: e.tensor_tensor(
                out=sb(nGTmI, 0, 128, 0, [[128, NCH], [1, 128]]), in0=sb(nGT, 0, 128, 0, [[128, NCH], [1, 128]]),
                in1=sb(consts, 0, 128, 0, [[0, NCH], [1, 128]]), op=ALU.subtract),
                reads=[r_nGT, r_consts], writes=[r_nGTmI])
            for c in range(NCH):
                P.op("pe", lambda e, c=c: e.matmul(XP(c, 0, 256), blk(nGT, c), Rb(c, 0, 256), start=False, stop=True,
                                                   skip_group_check=True), reads=[r_nGT, r_Xb], writes=XR(c))
            for it in range(2):
                for c in range(NCH):
                    P.op("act", lambda e, c=c: e.activation(out=sb(Xit, 0, 128, c * 256, [[1, 256]]), in_=XP(c, 0, 256),
                                                            func=AF.Copy), reads=XR(c), writes=[r_Xit])
                for c in range(NCH):
                    P.op("pe", lambda e, c=c: e.matmul(XP(c, 0, 256), blk(nGTmI, c), sb(Xit, 0, 128, c * 256, [[1, 256]]),
                                                       start=False, stop=True, skip_group_check=True),
                         reads=[r_nGTmI, r_Xit], writes=XR(c))
                    P.op("pe", lambda e, c=c: e.matmul(XP(c, 0, 256), ident_bf, Rb(c, 0, 256),
                                                       start=False, stop=True, skip_group_check=True),
                         reads=[r_identb, r_Xb], writes=XR(c))
            for c in range(NCH):
                P.op("act", lambda e, c=c: e.activation(out=sb(Xit, 0, 128, c * 256, [[1, 256]]), in_=XP(c, 0, 256),
                                                        func=AF.Copy), reads=XR(c), writes=[r_Xit])

            def Ul(c):
                return sb(Xit, 0, 128, c * 256, [[1, 128]])

            def Qc(c):
                return sb(Xit, 0, 128, c * 256 + 128, [[1, 128]])

            def ArbT(c):
                return sb(NtArb, 0, 128, c * 256 + 128, [[1, 128]])

            def ArkT(c):
                return sb(AkArk, 0, 128, c * 256 + 128, [[1, 128]])

            for c in range(NCH):
                P.op("pe", lambda e, c=c: e.matmul(PS(7, c * 128, 128), Qc(c), ArbT(c), start=True, stop=True),
                     reads=[r_Xit, r_NtArb], writes=psr(7, c * 128, c * 128 + 128))
            P.op("dve", lambda e: e.tensor_tensor(
                out=sb(Reff, 0, 128, 0, [[128, NCH], [1, 128]]), in0=sb(fKR, 0, 128, 128, [[256, NCH], [1, 128]]),
                in1=sb(psum[7], 0, 128, 0, [[128, NCH], [1, 128]]), op=ALU.subtract),
                reads=psr(7, 0, 512) + [r_fKR], writes=[r_Reff])
            for c in range(NCH):
                P.op("pe", lambda e, c=c: e.matmul(PS(2, c * 128, 128), Qc(c), blk(tBW, c), start=True, stop=True),
                     reads=[r_Xit, r_tBW], writes=psr(2, c * 128, c * 128 + 128))
            for c in range(NCH):
                P.op("dve", lambda e, c=c: e.scalar_tensor_tensor(
                    out=blk(Mc, c), in0=ident, scalar=sb(WLt, 0, 128, c, [[1, 1]]), in1=PS(2, c * 128, 128),
                    op0=ALU.mult, op1=ALU.subtract), reads=psr(2, c * 128, c * 128 + 128) + [r_WLt, r_consts],
                    writes=[r_Mc])
            Zj = sb(Zb, 0, 128, j * 128, [[1, 128]])
            for c in range(NCH):
                yo, yr = PS(3, c * 128, 128), psr(3, c * 128, c * 128 + 128)
                P.op("pe", lambda e, c=c, yo=yo: e.matmul(yo, Ul(c), ArbT(c), start=True, stop=False),
                     reads=[r_Xit, r_NtArb], writes=yr, inc=False)
                P.op("pe", lambda e, c=c, yo=yo: e.matmul(yo, blk(tV, c), ArkT(c), start=False, stop=False),
                     reads=[r_tV, r_AkArk], writes=yr, inc=False)
                P.op("pe", lambda e, c=c, yo=yo: e.matmul(yo, Zj, blk(Reff, c), start=False, stop=True),
                     reads=[r_Zb[j], r_Reff], writes=yr)
                zo, zr = PS(4, (c % 2) * 128, 128), psr(4, (c % 2) * 128, (c % 2) * 128 + 128)
                P.op("pe", lambda e, c=c, zo=zo: e.matmul(zo, blk(tBW, c), Ul(c), start=True, stop=False),
                     reads=[r_tBW, r_Xit], writes=zr, inc=False)
                P.op("pe", lambda e, c=c, zo=zo: e.matmul(zo, blk(tKW, c), blk(tV, c), start=False, stop=False),
                     reads=[r_tKW, r_tV], writes=zr, inc=False)
                P.op("pe", lambda e, c=c, zo=zo: e.matmul(zo, blk(Mc, c), Zj, start=False, stop=True),
                     reads=[r_Mc, r_Zb[j]], writes=zr)
                P.op("act", lambda e, zo=zo: e.activation(out=Zj, in_=zo, func=AF.Copy), reads=zr, writes=[r_Zb[j]])
                for hh in range(2):
                    p0 = hh * 64
                    P.op("dve", lambda e, c=c, p0=p0, hh=hh: e.tensor_copy(
                        out=sb(yT, p0, 64, c * 64, [[1, 64]]), in_=PS(3, c * 128 + hh * 64, 64, p0, 64)),
                        reads=yr, writes=[r_yT])

        def rwkv_out(j):
            q_ = j % 2
            fKR, fB, fK, fV, fKW, fBW = fKR2[q_], fB2[q_], fK2[q_], fV2[q_], fKW2[q_], fBW2[q_]
            r_fKR, r_fB, r_fK, r_fV, r_fKW, r_fBW = r_f2[q_]
            tV, tKW, tBW = tV2[q_], tKW2[q_], tBW2[q_]
            r_tV, r_tKW, r_tBW = r_t2[q_]
            WLt, r_WLt = WLt2[q_], r_WLt2[q_]
            Wg, r_Wg, Wbon, r_Wbon = Wg2[q_], r_Wg2[q_], Wb2[q_], r_Wb2[q_]
            O0, O1 = O2[0][:], O2[1][:]
            P.op("pe", lambda e: e.matmul(PS(1, 0, MG), cst(C_BD), yT[:], start=True, stop=True),
                 reads=[r_consts, r_yT], writes=psr(1, 0, MG))
            P.op("act", lambda e: e.activation(out=O1, in_=yT[:], func=AF.Square), reads=[r_yT], writes=[r_O2[1]])
            P.op("pe", lambda e: e.matmul(PS(2, 0, MG), cst(C_BD), O1, start=True, stop=True),
                 reads=[r_consts, r_O2[1]], writes=psr(2, 0, MG))
            P.op("dve", lambda e: e.tensor_scalar(out=O0, in0=PS(1, 0, MG), scalar1=1.0 / 64, scalar2=None,
                                                  op0=ALU.mult), reads=psr(1, 0, MG), writes=[r_O2[0]])
            P.op("dve", lambda e: e.tensor_tensor(out=O1, in0=O0, in1=O0, op=ALU.mult), reads=[r_O2[0]],
                 writes=[r_O2[1]])
            P.op("dve", lambda e: e.scalar_tensor_tensor(out=O1, in0=PS(2, 0, MG), scalar=1.0 / 64, in1=O1,
                                                         op0=ALU.mult, op1=ALU.subtract),
                 reads=psr(2, 0, MG) + [r_O2[1]], writes=[r_O2[1]])
            P.op("act", lambda e: e.activation(out=O1, in_=O1, func=AF.Sqrt, bias=sb(dcols, 0, 128, 13, [[1, 1]])),
                 reads=[r_O2[1], r_dcols], writes=[r_O2[1]])
            P.op("dve", lambda e: e.reciprocal(out=O1, in_=O1), reads=[r_O2[1]], writes=[r_O2[1]])
            P.op("dve", lambda e: e.tensor_tensor(out=yT[:], in0=yT[:], in1=O0, op=ALU.subtract),
                 reads=[r_yT, r_O2[0]], writes=[r_yT])
            P.op("dve", lambda e: e.tensor_tensor(out=yT[:], in0=yT[:], in1=O1, op=ALU.mult),
                 reads=[r_yT, r_O2[1]], writes=[r_yT])
            P.op("act", lambda e: e.activation(out=yT[:], in_=yT[:], func=AF.Identity, scale=col("lnx_g", j),
                                               bias=col("lnx_b", j)), reads=[r_yT, r_cols], writes=[r_yT])
            P.op("dve", lambda e: e.tensor_tensor(out=yT[:], in0=yT[:], in1=Wbon[:], op=ALU.add),
                 reads=[r_yT, r_Wbon], writes=[r_yT])
            P.op("dve", lambda e: e.tensor_tensor(out=sb(ymixT, 0, 128, j * MG, [[1, MG]]), in0=yT[:], in1=Wg[:],
                                                  op=ALU.mult), reads=[r_yT, r_Wg], writes=[r_ymix[j]])

        def out_proj(g):
            for st in range(NST):
                i = cnt["t"] % 2
                cnt["t"] += 1
                tok0 = g * MG + st * 128
                P.op("sp", lambda e, i=i, tok0=tok0: e.dma_start(out=hT[i][:], in_=src_d.ap()[tok0:tok0 + 128, :]),
                     writes=[r_hT[i]], dma=True)
                for half in range(2):
                    bank = 6 + half
                    for kc in range(8):
                        P.op("pe", lambda e, kc=kc, st=st, half=half, bank=bank: e.matmul(
                            PS(bank, 0, 512), sb(ymixT, 0, 128, kc * MG + st * 128, [[1, 128]]),
                            sb(woutb, 0, 128, kc * D + half * 512, [[1, 512]]), start=(kc == 0), stop=(kc == 7)),
                            reads=[r_ymix[kc], r_wout], writes=psr(bank, 0, 512), inc=(kc == 7))
                    P.op("dve", lambda e, half=half, bank=bank: e.tensor_tensor(
                        out=tmpo[half][:], in0=PS(bank, 0, 512), in1=sb(gate_t, 0, 128, half * 512, [[1, 512]]),
                        op=ALU.mult), reads=psr(bank, 0, 512) + [r_gate_t], writes=[r_tmpo[half]])
                    P.op("pool", lambda e, i=i, half=half: e.tensor_tensor(
                        out=sb(hT[i], 0, 128, half * 512, [[1, 512]]), in0=sb(hT[i], 0, 128, half * 512, [[1, 512]]),
                        in1=tmpo[half][:], op=ALU.add), reads=[r_tmpo[half], r_hT[i]], writes=[r_hT[i]])
                P.op("sp", lambda e, i=i, tok0=tok0: e.dma_start(out=dst_d.ap()[tok0:tok0 + 128, :], in_=hT[i][:]),
                     reads=[r_hT[i]], writes=[r_dst], dma=True)

        if "no_attn" in flags or "no_rwkv" in flags:
            P.op("pool", lambda e: e.memset(ymixT[:], 0.0), writes=r_ymix)
        for g in range(NGm):
            norm_T(g)
            if "no_attn" not in flags:
                attn_proj(g)
                attn(g)
            if "no_rwkv" not in flags:
                lora_inputs()
                for j in range(4):
                    rwkv_prep(j)
                    rwkv_chunks(j)
                    rwkv_out(j)
            out_proj(g)
        A.release()
        P.barrier()

    ident_bf_t = A.alloc("identb", 128, BF16)
    ident_bf = ident_bf_t[:]
    r_identb = Res("identb")
    P.op("dve", lambda e: e.tensor_copy(out=ident_bf, in_=ident), reads=[r_consts], writes=[r_identb])
    r_dst = Res("dst")
    epsc = A.alloc("epsc", 4, F32)
    r_epsc = Res("epsc")
    P.op("pool", lambda e: e.memset(sb(epsc, 0, 128, 0, [[1, 1]]), EPS), writes=[r_epsc])

    if upto == "ffn1":
        ffn_phase(0, x_d, out_d, f1w1_d, f1w3_d, f1w2_d)
    elif upto == "mixer":
        ffn_phase(0, x_d, h1_d, f1w1_d, f1w3_d, f1w2_d)
        mixer_phase(h1_d, out_d)
    else:
        ffn_phase(0, x_d, h1_d, f1w1_d, f1w3_d, f1w2_d)
        mixer_phase(h1_d, h2_d)
        ffn_phase(2, h2_d, out_d, f2w1_d, f2w3_d, f2w2_d)

    P.emit(reorder=("no_reorder" not in flags))
    P.names = A.names
    return nc, P


def _kc_layout(w):
    Kd, N = w.shape
    return np.ascontiguousarray(w.reshape(Kd // 128, 128, N).transpose(1, 0, 2).reshape(128, (Kd // 128) * N))


def _colvec(v):
    v = np.asarray(v, np.float32).reshape(-1)
    return np.ascontiguousarray(v.reshape(-1, 128).T)


def make_consts():
    c = np.zeros((128, NCONST), np.float32)
    c[:, 0:128] = np.eye(128, dtype=np.float32)
    c[:, 128:256] = 1.0
    p = np.arange(128)
    same = (p[:, None] // 64) == (p[None, :] // 64)
    row, colm = p[:, None] % 64, p[None, :] % 64
    same16 = (p[:, None] // 16) == (p[None, :] // 16)
    c[:, C_NDL:C_NDL + 128] = -(same16 & (row > colm)).astype(np.float32)
    c[:, C_NDU:C_NDU + 128] = -(same16 & (row < colm)).astype(np.float32)
    c[:, C_NSU:C_NSU + 128] = -(same & (row < colm)).astype(np.float32)
    c[:, C_UI:C_UI + 128] = (same & (row <= colm)).astype(np.float32)
    c[:, C_UI2:C_UI2 + 128] = (same & (row <= colm)).astype(np.float32)
    c[:, C_EM:C_EM + 128] = (same & ((row // 16) > (colm // 16))).astype(np.float32)
    c[:, C_BD:C_BD + 128] = same.astype(np.float32)
    c[:, C_MRES:C_MRES + MG] = (np.arange(MG) % 64 != 0).astype(np.float32)[None, :]
    return c


def make_bias_table(rel_bias):
    p = np.arange(128)[:, None, None]
    jb = np.arange(5)[None, :, None]
    qi = np.arange(128)[None, None, :]
    kpos = (jb - 4) * 128 + p
    dist = qi - kpos
    dch = qi // 64 - np.floor_divide(kpos, 64)
    valid = (dch >= 0) & (dch <= 8)
    idx = np.clip(dist, -256, 256) + 256
    rb = np.asarray(rel_bias, np.float32)
    tab = rb[:, idx]
    tab = np.where(valid[None], tab, np.float32(NEG)).astype(np.float32)
    return np.ascontiguousarray(tab.transpose(1, 0, 2, 3).reshape(128, 8 * 640))


def make_core_inputs(b, inp):
    cols = np.zeros((128, NCOL), np.float32)

    def put(name, v):
        o, w = COLS[name]
        cols[:, o:o + w] = v

    put("c", _colvec(inp["c"][b]))
    put("b_ada", _colvec(inp["b_ada"][0]))
    put("n1g", _colvec(inp["norm1_g"][0]))
    put("n2g", _colvec(inp["norm2_g"][0]))
    put("n3g", _colvec(inp["norm3_g"][0]))
    mu = np.asarray(inp["mu_shift"][0], np.float32)
    mucols = np.zeros((128, 15), np.float32)
    mucols[:, 0:12] = _colvec(mu[0:1536])
    mucols[0:64, 12] = mu[1536:1600]
    mucols[0:64, 13] = mu[1600:1664]
    mucols[:, 14] = mu[1664:1792]
    put("mu", mucols)
    for nm, key in (("w0", "w0"), ("a0", "a0"), ("k_k", "k_k"), ("k_a", "k_a"), ("r_k", "r_k"), ("lnx_g", "lnx_g"),
                    ("lnx_b", "lnx_b")):
        put(nm, _colvec(inp[key][0]))
    put("qg", np.tile(np.asarray(inp["q_norm_g"][0], np.float32), 2)[:, None])
    put("kg", np.tile(np.asarray(inp["k_norm_g"][0], np.float32), 2)[:, None])
    m = {
        "x": np.ascontiguousarray(inp["x"][b]),
        "wada": np.ascontiguousarray(inp["w_ada"][0].reshape(8, 128, 9 * D).transpose(1, 0, 2)),
        "cols": cols,
        "consts": make_consts(),
        "f1w1": _kc_layout(inp["ffn1_w1"][0]),
        "f1w3": _kc_layout(inp["ffn1_w3"][0]),
        "f1w2": _kc_layout(inp["ffn1_w2"][0]),
        "f2w1": _kc_layout(inp["ffn2_w1"][0]),
        "f2w3": _kc_layout(inp["ffn2_w3"][0]),
        "f2w2": _kc_layout(inp["ffn2_w2"][0]),
        "win": _kc_layout(inp["w_in"][0]),
        "wout": _kc_layout(inp["w_out"][0]),
        "wdu": np.ascontiguousarray(inp["w_decay_up"][0]),
        "wau": np.ascontiguousarray(inp["w_a_up"][0]),
        "wgu": np.ascontiguousarray(inp["w_g_up"][0]),
        "biasT": make_bias_table(inp["rel_bias"][0]),
    }
    return m


def kernel(**inp):
    inp = {k: np.asarray(v) for k, v in inp.items()}
    B, S, _ = inp["x"].shape
    nc, P = build(S)
    in_maps = [make_core_inputs(b % B, inp) for b in range(8)]
    res = run_bass_kernel_spmd(nc, in_maps, core_ids=list(range(8)))
    out = np.stack([res.results[b]["out"] for b in range(B)], axis=0)
    return out.astype(np.float32)
```

```python
import contextlib
import numpy as np
import ml_dtypes
import concourse.bass as bass
import concourse.mybir as mybir
from concourse.bass_utils import run_bass_kernel_spmd

F32 = mybir.dt.float32
BF16 = mybir.dt.bfloat16
AF = mybir.ActivationFunctionType
ALU = mybir.AluOpType
AX = mybir.AxisListType

ENGS = ["pe", "act", "dve", "pool", "sp"]

D = 1024
DFF = 2816
NFC = DFF // 128
NIN = 3328
EPS = 1e-6


class Res:
    __slots__ = ("name", "w", "r")

    def __init__(self, name):
        self.name = name
        self.w = None
        self.r = {}


class Prog:
    def __init__(self, nc, n_dma_sems=8, self_sync=True):
        self.nc = nc
        self.ops = {e: [] for e in ENGS}
        self.cnt = {e: 0 for e in ENGS}
        self.seen = {e: {} for e in ENGS}
        self.pending = {e: [] for e in ENGS}
        self.self_sync = self_sync
        self.n_dma_sems = n_dma_sems
        self.dma_cnt = {}
        self.dma_rr = {e: 0 for e in ENGS}
        self.nops = 0
        self.raw = []

    def _need(self, eng, waits, tok):
        if tok is None:
            return
        key, val = tok
        if key == eng and (eng == "pe" or not self.self_sync):
            return
        if self.seen[eng].get(key, 0) >= val:
            return
        if waits.get(key, 0) < val:
            waits[key] = val

    COST = {"pe": 0.13, "act": 0.35, "dve": 0.40, "pool": 0.60, "sp": 0.10}

    def op(self, eng, fn, reads=(), writes=(), dma=False, inc=True, cost=None):
        self.raw.append((eng, fn, tuple(reads), tuple(writes), dma, inc, cost if cost is not None else self.COST[eng]))

    def barrier(self):
        self.raw.append(None)

    def _schedule_segment(self, seg):
        import heapq
        n = len(seg)
        deps = [set() for _ in range(n)]
        lastw, readers = {}, {}
        for i, (eng, fn, reads, writes, dma, inc, cost) in enumerate(seg):
            for r in reads:
                if id(r) in lastw:
                    deps[i].add(lastw[id(r)])
            for r in writes:
                if id(r) in lastw:
                    deps[i].add(lastw[id(r)])
                for j in readers.get(id(r), ()):
                    deps[i].add(j)
            for r in reads:
                readers.setdefault(id(r), []).append(i)
            for r in writes:
                lastw[id(r)] = i
                readers[id(r)] = []
            deps[i].discard(i)
        succ = [[] for _ in range(n)]
        ndep = [len(d) for d in deps]
        for i, d in enumerate(deps):
            for j in d:
                succ[j].append(i)
        LAT_X, LAT_S, DMA_LAT = 0.9, 0.15, 2.5
        blevel = [0.0] * n
        for i in range(n - 1, -1, -1):
            c_i = (DMA_LAT if seg[i][4] else seg[i][6])
            m = 0.0
            for k in succ[i]:
                v = blevel[k] + (LAT_S if (seg[k][0] == seg[i][0] and not seg[i][4]) else LAT_X)
                if v > m:
                    m = v
            blevel[i] = c_i + m
        ready_t = [0.0] * n
        finish = [0.0] * n
        start = [0.0] * n
        fut = {e: [] for e in ENGS}
        avail = {e: [] for e in ENGS}
        free = {e: 0.0 for e in ENGS}
        for i in range(n):
            if ndep[i] == 0:
                heapq.heappush(fut[seg[i][0]], (0.0, i))
        done = 0
        while done < n:
            best = None
            for e in ENGS:
                while fut[e] and fut[e][0][0] <= free[e]:
                    rt, i = heapq.heappop(fut[e])
                    heapq.heappush(avail[e], (-blevel[i], i))
                if avail[e]:
                    st = free[e]
                elif fut[e]:
                    st = fut[e][0][0]
                else:
                    continue
                if best is None or st < best[0]:
                    best = (st, e)
            st, e = best
            if avail[e]:
                _, i = heapq.heappop(avail[e])
            else:
                rt, i = heapq.heappop(fut[e])
            eng, fn, reads, writes, dma, inc, cost = seg[i]
            start[i] = st
            free[e] = st + cost
            finish[i] = st + (DMA_LAT if dma else cost)
            done += 1
            for k in succ[i]:
                lat = LAT_S if seg[k][0] == e and not dma else LAT_X
                ready_t[k] = max(ready_t[k], finish[i] + lat)
                ndep[k] -= 1
                if ndep[k] == 0:
                    heapq.heappush(fut[seg[k][0]], (ready_t[k], k))
        order = sorted(range(n), key=lambda i: (start[i], i))
        self.est_time += max(finish) if n else 0.0
        return [seg[i] for i in order]

    def finalize(self, reorder=True):
        self.est_time = 0.0
        seg = []
        for item in self.raw + [None]:
            if item is None:
                ops = self._schedule_segment(seg) if (reorder and seg) else seg
                for (eng, fn, reads, writes, dma, inc, cost) in ops:
                    self._emit_op(eng, fn, reads, writes, dma, inc)
                self._emit_barrier()
                seg = []
            else:
                seg.append(item)

    def _emit_op(self, eng, fn, reads=(), writes=(), dma=False, inc=True):
        waits = {}
        for r in reads:
            self._need(eng, waits, r.w)
        for r in writes:
            self._need(eng, waits, r.w)
            for tok in r.r.values():
                self._need(eng, waits, tok)
        if dma:
            i = self.dma_rr[eng]
            self.dma_rr[eng] = (i + 1) % self.n_dma_sems
            key = ("dma", eng, i)
            prev = self.dma_cnt.get(key, 0)
            if prev:
                self._need(eng, waits, (key, prev))
            tok = (key, prev + 16)
            self.dma_cnt[key] = prev + 16
        elif inc:
            self.cnt[eng] += 1
            tok = (eng, self.cnt[eng])
        else:
            tok = None
        for k, v in waits.items():
            self.seen[eng][k] = v
        self.ops[eng].append((sorted(waits.items(), key=str), fn, tok))
        self.nops += 1
        if tok is None:
            self.pending[eng].append((tuple(reads), tuple(writes)))
            return
        allr, allw = list(reads), list(writes)
        if not dma:
            for (pr, pw) in self.pending[eng]:
                allr += pr
                allw += pw
            self.pending[eng] = []
        rkey = tok[0]
        for r in allr:
            r.r[rkey] = tok
        for r in allw:
            r.w = tok
            r.r = {}

    def _emit_barrier(self):
        toks = [(e, self.cnt[e]) for e in ENGS if self.cnt[e]]
        toks += [(k, v) for k, v in self.dma_cnt.items()]
        for e in ENGS:
            waits = {}
            for t in toks:
                self._need(e, waits, t)
            if waits:
                for k, v in waits.items():
                    self.seen[e][k] = v
                self.ops[e].append((sorted(waits.items(), key=str), None, None))

    def emit(self, reorder=True):
        self.finalize(reorder)
        nc = self.nc
        with contextlib.ExitStack() as es:
            sems = {}
            for e in ENGS:
                sems[e] = es.enter_context(nc.semaphore("s_" + e))
            for key in self.dma_cnt:
                sems[key] = es.enter_context(nc.semaphore("d_%s_%d" % (key[1], key[2])))
            block = es.enter_context(nc.Block())

            def run(eng_name):
                def body(eng):
                    for waits, fn, tok in self.ops[eng_name]:
                        for k, v in waits:
                            eng.wait_ge(sems[k], v)
                        if fn is None:
                            continue
                        ins = fn(eng)
                        if tok is not None:
                            key, _ = tok
                            ins.then_inc(sems[key], 16 if isinstance(key, tuple) else 1)
                return body

            block.tensor(run("pe"))
            block.scalar(run("act"))
            block.vector(run("dve"))
            block.gpsimd(run("pool"))
            block.sync(run("sp"))


class Arena:
    def __init__(self, nc, base, limit):
        self.nc, self.base, self.limit, self.off, self.n = nc, base, limit, base, 0
        self.marks = []
        self.names = {}

    def alloc(self, name, free_elems, dtype):
        nb = free_elems * (2 if dtype == BF16 else 4)
        nb = (nb + 63) // 64 * 64
        assert self.off + nb <= self.limit, "SBUF overflow at %s: %d + %d > %d" % (name, self.off, nb, self.limit)
        self.n += 1
        t = self.nc.alloc_sbuf_tensor_at("%s_%d" % (name, self.n), [128, free_elems], dtype, offset=self.off)
        self.off += nb
        self.names.setdefault(name, []).append("%s_%d" % (name, self.n))
        return t

    def mark(self):
        self.marks.append(self.off)

    def release(self):
        self.off = self.marks.pop()


def sb(t, p0, np_, off, dims):
    F = 1
    for s in t.shape[1:]:
        F *= s
    return bass.AP(t, p0 * F + off, [[F, np_]] + [list(d) for d in dims])


COLS = {}
_o = 0
for _n, _w in [("c", 8), ("b_ada", 72), ("n1g", 8), ("n2g", 8), ("n3g", 8), ("mu", 15), ("w0", 4), ("a0", 4),
               ("k_k", 4), ("k_a", 4), ("r_k", 4), ("lnx_g", 4), ("lnx_b", 4), ("qg", 1), ("kg", 1)]:
    COLS[_n] = (_o, _w)
    _o += _w
NCOL = _o

MG = 256
NCH = MG // 64
C_NDL, C_NSU, C_UI, C_BD, C_MRES = 256, 384, 512, 640, 768
C_NDU, C_UI2, C_EM = 768 + MG, 768 + MG + 128, 768 + MG + 256
NCONST = 768 + MG + 384
C0 = float(np.exp(-0.5))
NEG = -30000.0


class K:
    pass


def build(NT, dbg=None, upto="all", flags=()):
    assert NT % 512 == 0
    NG = NT // 512
    nc = bass.Bass("TRN2", target_bir_lowering=False)
    P = Prog(nc, self_sync=("selfsync_off" not in flags))
    dram = {}

    def din(name, shape, dt=F32):
        dram[name] = nc.dram_tensor(name, list(shape), dt, kind="ExternalInput")
        return dram[name]

    x_d = din("x", [NT, D])
    wada_d = din("wada", [128, 8, 9 * D])
    cols_d = din("cols", [128, NCOL])
    consts_d = din("consts", [128, NCONST])
    f1w1_d = din("f1w1", [128, 8 * DFF])
    f1w3_d = din("f1w3", [128, 8 * DFF])
    f1w2_d = din("f1w2", [128, NFC * D])
    win_d = din("win", [128, 8 * NIN])
    wout_d = din("wout", [128, 8 * D])
    wdu_d = din("wdu", [64, 512])
    wau_d = din("wau", [64, 512])
    wgu_d = din("wgu", [128, 512])
    biasT_d = din("biasT", [128, 8 * 640])
    f2w1_d = din("f2w1", [128, 8 * DFF])
    f2w3_d = din("f2w3", [128, 8 * DFF])
    f2w2_d = din("f2w2", [128, NFC * D])
    out_d = nc.dram_tensor("out", [NT, D], F32, kind="ExternalOutput")
    h1_d = nc.dram_tensor("h1s", [NT, D], F32)
    h2_d = nc.dram_tensor("h2s", [NT, D], F32)
    dbg_d = {}
    if dbg:
        for name, shape in dbg.items():
            dbg_d[name] = nc.dram_tensor("dbg_" + name, list(shape), F32, kind="ExternalOutput")

    A = Arena(nc, 16384, 16384 + 212736)
    cols = A.alloc("cols", NCOL, F32)
    consts = A.alloc("consts", NCONST, F32)
    modT = A.alloc("modT", 72, F32)
    Gs = A.alloc("Gs", 24, F32)
    gcol = A.alloc("gcol", 24, F32)
    sc = A.alloc("sc", 8, F32)
    r_cols, r_consts, r_modT, r_Gs, r_gcol, r_sc = [Res(n) for n in ("cols", "consts", "modT", "Gs", "gcol", "sc")]

    def col(name, j=0, w=1, p0=0, np_=128):
        o, _ = COLS[name]
        return sb(cols, p0, np_, o + j, [[1, w]])

    ident = sb(consts, 0, 128, 0, [[1, 128]])
    ones = sb(consts, 0, 128, 128, [[1, 128]])

    P.op("sp", lambda e: e.dma_start(out=cols[:], in_=cols_d.ap()), writes=[r_cols], dma=True)
    P.op("sp", lambda e: e.dma_start(out=consts[:], in_=consts_d.ap()), writes=[r_consts], dma=True)

    psum = [nc.alloc_psum_tensor("ps%d" % i, [128, 512], F32) for i in range(8)]
    r_ps = [Res("ps%d" % i) for i in range(8)]
    r_psq = [[Res("ps%d_%d" % (i, q)) for q in range(4)] for i in range(8)]

    A.mark()
    wst = [A.alloc("wst", 8 * 512, F32) for _ in range(2)]
    r_wst = [Res("wst0"), Res("wst1")]
    r_diag = [Res("diag0"), Res("diag1")]
    P.op("act", lambda e: e.activation(out=sc[:], in_=col("c", 0, 8), func=AF.Silu), reads=[r_cols], writes=[r_sc])
    for slab in range(18):
        s = slab % 2
        P.op("sp", lambda e, slab=slab, s=s: e.dma_start(
            out=sb(wst[s], 0, 128, 0, [[512, 8], [1, 512]]), in_=wada_d.ap()[:, :, slab * 512:(slab + 1) * 512]),
            writes=[r_wst[s]], dma=True)
        for cb in range(4):
            j = slab * 4 + cb
            for kc in range(8):
                P.op("pe", lambda e, s=s, cb=cb, kc=kc, j=j: e.matmul(
                    sb(psum[0], 0, 128, j, [[1, 1]]),
                    sb(wst[s], 0, 128, kc * 512 + cb * 128, [[1, 128]]),
                    sb(sc, 0, 128, kc, [[1, 1]]), start=(kc == 0), stop=(kc == 7)),
                    reads=[r_wst[s], r_sc], writes=[r_ps[0]], inc=(kc == 7))
    P.op("dve", lambda e: e.tensor_tensor(out=modT[:], in0=sb(psum[0], 0, 128, 0, [[1, 72]]), in1=col("b_ada", 0, 72),
                                          op=ALU.add), reads=[r_ps[0], r_cols], writes=[r_modT])
    for n in range(3):
        gname = ("n1g", "n2g", "n3g")[n]
        P.op("dve", lambda e, n=n, gname=gname: e.scalar_tensor_tensor(
            out=sb(Gs, 0, 128, n * 8, [[1, 8]]), in0=sb(modT, 0, 128, (3 * n + 1) * 8, [[1, 8]]), scalar=1.0,
            in1=col(gname, 0, 8), op0=ALU.add, op1=ALU.mult), reads=[r_modT, r_cols], writes=[r_Gs])
        P.op("dve", lambda e, n=n: e.tensor_scalar(
            out=sb(gcol, 0, 128, n * 8, [[1, 8]]), in0=sb(modT, 0, 128, (3 * n + 2) * 8, [[1, 8]]),
            scalar1=(1.0 if n == 1 else 0.5), scalar2=None, op0=ALU.mult), reads=[r_modT], writes=[r_gcol])

    def shiftcol(n, kc):
        return sb(modT, 0, 128, 3 * n * 8 + kc, [[1, 1]])

    def Gcol(n, kc):
        return sb(Gs, 0, 128, n * 8 + kc, [[1, 1]])

    A.release()
    P.barrier()

    def make_gate_bc(n, gate_t, r_gate_t):
        A.mark()
        diag = [A.alloc("diag", 512, F32) for _ in range(2)]
        for half in range(2):
            s = half
            for q in range(4):
                kc = half * 4 + q
                P.op("dve", lambda e, s=s, q=q, kc=kc: e.tensor_scalar(
                    out=sb(diag[s], 0, 128, q * 128, [[1, 128]]), in0=ident,
                    scalar1=sb(gcol, 0, 128, n * 8 + kc, [[1, 1]]), scalar2=None, op0=ALU.mult),
                    reads=[r_gcol, r_consts], writes=[r_diag[s]])
            P.op("pe", lambda e, s=s: e.matmul(psum[1][:], ones, diag[s][:], start=True, stop=True),
                 reads=[r_diag[s], r_consts], writes=[r_ps[1]])
            P.op("act", lambda e, half=half: e.activation(
                out=sb(gate_t, 0, 128, half * 512, [[1, 512]]), in_=psum[1][:], func=AF.Copy),
                reads=[r_ps[1]], writes=[r_gate_t])
        A.release()
        P.barrier()

    if dbg and "modT" in dbg:
        P.op("sp", lambda e: e.dma_start(out=dbg_d["modT"].ap(), in_=modT[:]), reads=[r_modT], dma=True)

    def ffn_phase(nidx, src_d, dst_d, w1_d, w3_d, w2_d):
        A.mark()
        w1b = A.alloc("w1b", 8 * DFF, BF16)
        w3b = A.alloc("w3b", 8 * DFF, BF16)
        w2b = A.alloc("w2b", NFC * D, BF16)
        r_w = {"w1": Res("w1"), "w3": Res("w3"), "w2": Res("w2")}
        A.mark()
        stg = [A.alloc("stg", DFF, F32) for _ in range(3)]
        r_stg = [Res("stg%d" % i) for i in range(3)]
        k = 0
        for (wd, wb, rw) in ((w1_d, w1b, r_w["w1"]), (w3_d, w3b, r_w["w3"]), (w2_d, w2b, r_w["w2"])):
            for ch in range(8):
                s = k % 3
                P.op("sp", lambda e, wd=wd, ch=ch, s=s: e.dma_start(
                    out=stg[s][:], in_=wd.ap()[:, ch * DFF:(ch + 1) * DFF]), writes=[r_stg[s]], dma=True)
                ceng = ("pool", "dve", "act")[k % 3]
                if ceng == "act":
                    P.op("act", lambda e, wb=wb, ch=ch, s=s: e.activation(
                        out=sb(wb, 0, 128, ch * DFF, [[1, DFF]]), in_=stg[s][:], func=AF.Copy),
                        reads=[r_stg[s]], writes=[rw])
                else:
                    P.op(ceng, lambda e, wb=wb, ch=ch, s=s: e.tensor_copy(
                        out=sb(wb, 0, 128, ch * DFF, [[1, DFF]]), in_=stg[s][:]),
                        reads=[r_stg[s]], writes=[rw])
                k += 1
        A.release()
        P.barrier()
        A.mark()
        gate_t = A.alloc("gate_bc", D, F32)
        r_gate_t = Res("gate")
        make_gate_bc(nidx, gate_t, r_gate_t)
        nT = A.alloc("nT", 8 * 512, BF16)
        gT = A.alloc("gT", NFC * 512, BF16)
        r_nT, r_gT = Res("nT"), Res("gT")
        hT = [A.alloc("hT", D, F32) for _ in range(2)]
        hB = [A.alloc("hB", D, F32) for _ in range(2)]
        zb = [A.alloc("zb", D, BF16) for _ in range(2)]
        s1 = [A.alloc("s1", 512, F32) for _ in range(2)]
        tmp = [A.alloc("tmp", 512, F32) for _ in range(2)]
        stat = A.alloc("stat", 16, F32)
        r_hT = [Res("hT0"), Res("hT1")]
        r_hB = [Res("hB0"), Res("hB1")]
        r_zb = [Res("zb0"), Res("zb1")]
        r_stat = [Res("stat0"), Res("stat1")]
        r_s1 = [Res("s10"), Res("s11")]
        r_tmp = [Res("tmp0"), Res("tmp1")]
        tp_bf = psum[0][:].bitcast(BF16)
        cnt = {"t": 0, "a": 0, "b": 0}

        def stageT(g):
            for st in range(4):
                i = cnt["t"] % 2
                cnt["t"] += 1
                tok0 = g * 512 + st * 128
                P.op("sp", lambda e, i=i, tok0=tok0: e.dma_start(out=hT[i][:], in_=src_d.ap()[tok0:tok0 + 128, :]),
                     writes=[r_hT[i]], dma=True)
                P.op("act", lambda e, i=i: e.activation(out=zb[i][:], in_=hT[i][:], func=AF.Square,
                                                         accum_out=sb(stat, 0, 128, i * 4, [[1, 1]])),
                     reads=[r_hT[i]], writes=[r_zb[i], r_stat[i]])
                P.op("act", lambda e, i=i: e.activation(
                    out=sb(stat, 0, 128, i * 4 + 1, [[1, 1]]), in_=sb(stat, 0, 128, i * 4, [[1, 1]]),
                    func=AF.Sqrt, scale=1.0 / D, bias=sb(epsc, 0, 128, 0, [[1, 1]])),
                    reads=[r_stat[i], r_epsc], writes=[r_stat[i]])
                P.op("dve", lambda e, i=i: e.reciprocal(
                    out=sb(stat, 0, 128, i * 4 + 2, [[1, 1]]), in_=sb(stat, 0, 128, i * 4 + 1, [[1, 1]])),
                    reads=[r_stat[i]], writes=[r_stat[i]])
                P.op("dve", lambda e, i=i: e.tensor_scalar(
                    out=zb[i][:], in0=hT[i][:], scalar1=sb(stat, 0, 128, i * 4 + 2, [[1, 1]]), scalar2=None,
                    op0=ALU.mult), reads=[r_hT[i], r_stat[i]], writes=[r_zb[i]])
                for kc in range(8):
                    P.op("pe", lambda e, i=i, kc=kc: e.transpose(
                        tp_bf[:, kc * 128:(kc + 1) * 128], sb(zb[i], 0, 128, kc * 128, [[1, 128]]),
                        ident_bf), reads=[r_zb[i], r_identb], writes=[r_ps[0]], inc=(kc == 7))
                for kc in range(8):
                    P.op("act", lambda e, kc=kc, st=st: e.activation(
                        out=sb(nT, 0, 128, kc * 512 + st * 128, [[1, 128]]), in_=tp_bf[:, kc * 128:(kc + 1) * 128],
                        func=AF.Identity, scale=Gcol(nidx, kc), bias=shiftcol(nidx, kc)),
                        reads=[r_ps[0], r_Gs, r_modT], writes=[r_nT], inc=(kc == 7))

        def stageA(g):
            for f in range(NFC):
                i = cnt["a"] % 2
                cnt["a"] += 1
                for (wb, rw, bank) in ((w1b, r_w["w1"], 1 + i), (w3b, r_w["w3"], 3 + i)):
                    for kc in range(8):
                        P.op("pe", lambda e, wb=wb, kc=kc, f=f, bank=bank: e.matmul(
                            psum[bank][:], sb(wb, 0, 128, kc * DFF + f * 128, [[1, 128]]),
                            sb(nT, 0, 128, kc * 512, [[1, 512]]), start=(kc == 0), stop=(kc == 7)),
                            reads=[rw, r_nT], writes=[r_ps[bank]], inc=(kc == 7))
                P.op("act", lambda e, i=i: e.activation(out=s1[i][:], in_=psum[1 + i][:], func=AF.Silu),
                     reads=[r_ps[1 + i]], writes=[r_s1[i]])
                P.op("dve", lambda e, i=i, f=f: e.tensor_tensor(
                    out=sb(gT, 0, 128, f * 512, [[1, 512]]), in0=s1[i][:], in1=psum[3 + i][:], op=ALU.mult),
                    reads=[r_s1[i], r_ps[3 + i]], writes=[r_gT])

        def stageB(g):
            for st in range(4):
                i = cnt["b"] % 2
                cnt["b"] += 1
                tok0 = g * 512 + st * 128
                P.op("sp", lambda e, i=i, tok0=tok0: e.dma_start(out=hB[i][:], in_=src_d.ap()[tok0:tok0 + 128, :]),
                     writes=[r_hB[i]], dma=True)
                for half in range(2):
                    bank = 5 + half
                    for f in range(NFC):
                        P.op("pe", lambda e, f=f, st=st, half=half, bank=bank: e.matmul(
                            psum[bank][:], sb(gT, 0, 128, f * 512 + st * 128, [[1, 128]]),
                            sb(w2b, 0, 128, f * D + half * 512, [[1, 512]]), start=(f == 0), stop=(f == NFC - 1)),
                            reads=[r_gT, r_w["w2"]], writes=[r_ps[bank]], inc=(f == NFC - 1))
                    P.op("dve", lambda e, half=half, bank=bank: e.tensor_tensor(
                        out=tmp[half][:], in0=psum[bank][:], in1=sb(gate_t, 0, 128, half * 512, [[1, 512]]),
                        op=ALU.mult), reads=[r_ps[bank], r_gate_t], writes=[r_tmp[half]])
                    P.op("pool", lambda e, i=i, half=half: e.tensor_tensor(
                        out=sb(hB[i], 0, 128, half * 512, [[1, 512]]), in0=sb(hB[i], 0, 128, half * 512, [[1, 512]]),
                        in1=tmp[half][:], op=ALU.add), reads=[r_tmp[half], r_hB[i]], writes=[r_hB[i]])
                P.op("sp", lambda e, i=i, tok0=tok0: e.dma_start(out=dst_d.ap()[tok0:tok0 + 128, :], in_=hB[i][:]),
                     reads=[r_hB[i]], writes=[r_dst], dma=True)

        for g in range(NG + 1):
            if g < NG:
                stageT(g)
            if g >= 1:
                stageB(g - 1)
            if g < NG:
                stageA(g)
        A.release()
        A.release()
        P.barrier()


    def mixer_phase(src_d, dst_d):
        NGm = NT // MG
        NST = MG // 128
        A.mark()
        winb = A.alloc("winb", 8 * NIN, BF16)
        woutb = A.alloc("woutb", 8 * D, BF16)
        wdub = A.alloc("wdub", 512, BF16)
        waub = A.alloc("waub", 512, BF16)
        wgub = A.alloc("wgub", 512, BF16)
        biasT = A.alloc("biasT", 8 * 640, BF16)
        onesb = A.alloc("onesb", 128, BF16)
        dcols = A.alloc("dcols", 16, F32)
        r_win, r_wout, r_lora, r_biasT, r_onesb, r_dcols = [Res(n) for n in (
            "win", "wout", "lora", "biasT", "onesb", "dcols")]
        A.mark()
        stg = [A.alloc("mstg", NIN, F32) for _ in range(2)]
        r_stg = [Res("mstg0"), Res("mstg1")]
        kk_ = [0]

        def load_cast(src_ap, dst_ap, nparts, ncols, rdst):
            s_ = kk_[0] % 2
            eng = ("pool", "dve")[kk_[0] % 2]
            kk_[0] += 1
            P.op("sp", lambda e: e.dma_start(out=sb(stg[s_], 0, nparts, 0, [[1, ncols]]), in_=src_ap),
                 writes=[r_stg[s_]], dma=True)
            P.op(eng, lambda e: e.tensor_copy(out=dst_ap, in_=sb(stg[s_], 0, nparts, 0, [[1, ncols]])),
                 reads=[r_stg[s_]], writes=[rdst])

        for kc in range(8):
            load_cast(win_d.ap()[:, kc * NIN:(kc + 1) * NIN], sb(winb, 0, 128, kc * NIN, [[1, NIN]]), 128, NIN, r_win)
        for ch in range(4):
            load_cast(wout_d.ap()[:, ch * 2048:(ch + 1) * 2048], sb(woutb, 0, 128, ch * 2048, [[1, 2048]]), 128, 2048,
                      r_wout)
        load_cast(wdu_d.ap(), sb(wdub, 0, 64, 0, [[1, 512]]), 64, 512, r_lora)
        load_cast(wau_d.ap(), sb(waub, 0, 64, 0, [[1, 512]]), 64, 512, r_lora)
        load_cast(wgu_d.ap(), sb(wgub, 0, 128, 0, [[1, 512]]), 128, 512, r_lora)
        for hb_ in range(2):
            load_cast(biasT_d.ap()[:, hb_ * 2560:(hb_ + 1) * 2560], sb(biasT, 0, 128, hb_ * 2560, [[1, 2560]]), 128, 2560,
                      r_biasT)
        P.op("pool", lambda e: e.memset(onesb[:], 1.0), writes=[r_onesb])
        P.op("dve", lambda e: e.tensor_scalar(out=sb(dcols, 0, 128, 0, [[1, 4]]), in0=col("w0", 0, 4), scalar1=0.5,
                                              scalar2=None, op0=ALU.mult), reads=[r_cols], writes=[r_dcols])
        P.op("dve", lambda e: e.tensor_scalar(out=sb(dcols, 0, 128, 4, [[1, 4]]), in0=col("a0", 0, 4), scalar1=0.5,
                                              scalar2=None, op0=ALU.mult), reads=[r_cols], writes=[r_dcols])
        P.op("dve", lambda e: e.tensor_scalar(out=sb(dcols, 0, 128, 8, [[1, 4]]), in0=col("k_a", 0, 4), scalar1=-1.0,
                                              scalar2=1.0, op0=ALU.mult, op1=ALU.add), reads=[r_cols], writes=[r_dcols])
        P.op("pool", lambda e: e.memset(sb(dcols, 0, 128, 13, [[1, 1]]), 64e-5), writes=[r_dcols])
        P.op("dve", lambda e: e.tensor_scalar(out=sb(dcols, 0, 128, 12, [[1, 1]]), in0=col("qg", 0, 1), scalar1=0.125,
                                              scalar2=None, op0=ALU.mult), reads=[r_cols], writes=[r_dcols])
        A.release()
        P.barrier()

        def dcol(j):
            return sb(dcols, 0, 128, j, [[1, 1]])

        gate_t = A.alloc("gate_bc", D, F32)
        r_gate_t = Res("gate")
        make_gate_bc(1, gate_t, r_gate_t)

        def cst(c0, n=128, p0=0, np_=128):
            return sb(consts, p0, np_, c0, [[1, n]])

        carry = A.alloc("carry", 16, F32)
        Zb = A.alloc("Zb", 4 * 128, BF16)
        fKR2 = [A.alloc("fKR", NCH * 256, BF16) for _ in range(2)]
        fB2 = [A.alloc("fB", NCH * 128, BF16) for _ in range(2)]
        fK2 = [A.alloc("fK", NCH * 128, BF16) for _ in range(2)]
        fV2 = [A.alloc("fV", NCH * 128, BF16) for _ in range(2)]
        fKW2 = [A.alloc("fKW", NCH * 128, BF16) for _ in range(2)]
        fBW2 = [A.alloc("fBW", NCH * 128, BF16) for _ in range(2)]
        r_carry, r_Zb = Res("carry"), [Res("Zb%d" % j) for j in range(4)]
        r_f2 = [[Res("%s%d" % (n, q)) for n in ("fKR", "fB", "fK", "fV", "fKW", "fBW")] for q in range(2)]
        P.op("pool", lambda e: e.memset(carry[:], 0.0), writes=[r_carry])
        P.op("pool", lambda e: e.memset(Zb[:], 0.0), writes=r_Zb)
        for q in range(2):
            for t_, r_ in zip((fKR2[q], fB2[q], fK2[q], fV2[q], fKW2[q], fBW2[q]), r_f2[q]):
                P.op("pool", lambda e, t_=t_: e.memset(t_[:], 0.0), writes=[r_])
        tV2 = [A.alloc("tV", NCH * 128, BF16) for _ in range(2)]
        tKW2 = [A.alloc("tKW", NCH * 128, BF16) for _ in range(2)]
        tBW2 = [A.alloc("tBW", NCH * 128, BF16) for _ in range(2)]
        r_t2 = [[Res("%s%d" % (n, q)) for n in ("tV", "tKW", "tBW")] for q in range(2)]
        Wg2 = [A.alloc("Wg", MG, F32) for _ in range(2)]
        Wb2 = [A.alloc("Wbon", MG, F32) for _ in range(2)]
        r_Wg2, r_Wb2 = [Res("Wg0"), Res("Wg1")], [Res("Wbon0"), Res("Wbon1")]
        O2 = [A.alloc("Osc", MG, F32) for _ in range(2)]
        r_O2 = [Res("Osc0"), Res("Osc1")]
        WLt2 = [A.alloc("WLt", 8, F32) for _ in range(2)]
        r_WLt2 = [Res("WLt0"), Res("WLt1")]
        Nn = A.alloc("Nn", NCH * 128, BF16)
        NtArb = A.alloc("NtArb", NCH * 256, BF16)
        AkArk = A.alloc("AkArk", NCH * 256, BF16)
        Xb = A.alloc("Xb", NCH * 384, BF16)
        Esb = A.alloc("Esb", NCH * 128, BF16)
        nGT = A.alloc("nGT", NCH * 128, BF16)
        r_Esb, r_nGT = Res("Esb"), Res("nGT")
        nGTmI = A.alloc("nGTmI", NCH * 128, BF16)
        Xit = A.alloc("Xit", NCH * 256, BF16)
        r_nGTmI, r_Xit = Res("nGTmI"), Res("Xit")
        PP = [A.alloc("PP", NCH * 256, BF16) for _ in range(2)]
        Reff = A.alloc("Reff", NCH * 128, BF16)
        Mc = A.alloc("Mc", NCH * 128, BF16)
        r_Nn, r_NtArb, r_AkArk, r_Xb, r_Reff, r_Mc, r_WLt = [Res(n) for n in (
            "Nn", "NtArb", "AkArk", "Xb", "Reff", "Mc", "WLtX")]
        r_PP = [Res("PP0"), Res("PP1")]
        yT = A.alloc("yT", MG, F32)
        r_yT = Res("yT")
        ymixT = A.alloc("ymixT", 8 * MG, BF16)
        r_ymix = [Res("ymix%d" % i) for i in range(8)]
        qT = A.alloc("qT", 4 * MG, BF16)
        r_qT = [Res("qT%d" % i) for i in range(4)]
        KT = A.alloc("KT", 4 * 1024, BF16)
        r_KT = Res("KT")
        Vr = A.alloc("Vr", 8 * 512, BF16)
        r_Vr = Res("Vr")
        scb = [A.alloc("scb", 640, F32) for _ in range(2)]
        prb = [A.alloc("prb", 640, BF16) for _ in range(2)]
        rec = A.alloc("rec", 128, F32)
        r_scb, r_prb, r_rec = [Res("scb0"), Res("scb1")], [Res("prb0"), Res("prb1")], Res("rec")
        hT = [A.alloc("hT", D, F32) for _ in range(2)]
        zb = [A.alloc("zb", D, BF16) for _ in range(2)]
        stat = A.alloc("stat", 16, F32)
        r_hT, r_zb, r_stat = [Res("hT0"), Res("hT1")], [Res("zb0"), Res("zb1")], [Res("st0"), Res("st1")]
        nT = A.alloc("nT", 8 * MG, BF16)
        r_nT = Res("nT")
        thb = A.alloc("thb", MG, BF16)
        padb = A.alloc("padb", MG, BF16)
        sgdb = A.alloc("sgdb", MG, BF16)
        r_thb, r_padb, r_sgdb = Res("thb"), Res("padb"), Res("sgdb")
        pbuf = [A.alloc("pbuf", MG + 8, F32) for _ in range(2)]
        r_pbuf = [Res("pbuf0"), Res("pbuf1")]
        mixtmp = A.alloc("mixtmp", MG, F32)
        r_mixtmp = Res("mixtmp")
        NW = 13
        Wt = [A.alloc("W%d" % i, MG, F32) if i not in (6, 11, 12) else None for i in range(NW + 3)]
        r_W = [Res("W%d" % i) for i in range(NW + 3)]
        R_, Kx, Vx = NW, NW + 1, NW + 2
        tmpo = [A.alloc("tmpo", 512, F32) for _ in range(2)]
        r_tmpo = [Res("tmpo0"), Res("tmpo1")]

        def W(i, p0=0, np_=128, c0=0, n=MG):
            return sb(Wt[i], p0, np_, c0, [[1, n]])

        def psr(bank, c0, c1):
            return [r_ps[bank]]

        def PS(bank, c0, n, p0=0, np_=128):
            return sb(psum[bank], p0, np_, c0, [[1, n]])

        cnt = {"t": 0, "pb": 0, "att": 0}
        tp_bf = [psum[b][:].bitcast(BF16) for b in range(8)]

        def psbf(bank, c0, n):
            return tp_bf[bank][:, c0:c0 + n]

        def psr_bf(bank, c0, c1):
            return psr(bank, c0 // 2, (c1 + 1) // 2)

        def proj_cols(c0, ncols, bank):
            for kc in range(8):
                P.op("pe", lambda e, kc=kc: e.matmul(
                    PS(bank, 0, MG, 0, ncols), sb(winb, 0, 128, kc * NIN + c0, [[1, ncols]]),
                    sb(nT, 0, 128, kc * MG, [[1, MG]]), start=(kc == 0), stop=(kc == 7)),
                    reads=[r_win, r_nT], writes=psr(bank, 0, MG), inc=(kc == 7))

        def mix(blk, c0, ncols, bank, out_ap, r_out):
            proj_cols(c0, ncols, bank)
            i = cnt["pb"] % 2
            cnt["pb"] += 1
            pb = pbuf[i]
            P.op("act", lambda e: e.activation(out=sb(pb, 0, ncols, 0, [[1, 1]]), in_=sb(carry, 0, ncols, blk, [[1, 1]]),
                                               func=AF.Copy), reads=[r_carry], writes=[r_pbuf[i]])
            P.op("act", lambda e: e.activation(out=sb(pb, 0, ncols, 1, [[1, MG]]), in_=PS(bank, 0, MG, 0, ncols),
                                               func=AF.Copy), reads=psr(bank, 0, MG), writes=[r_pbuf[i]])
            P.op("act", lambda e: e.activation(out=sb(carry, 0, ncols, blk, [[1, 1]]), in_=sb(pb, 0, ncols, MG, [[1, 1]]),
                                               func=AF.Copy), reads=[r_pbuf[i]], writes=[r_carry])
            P.op("dve", lambda e: e.tensor_tensor(out=sb(mixtmp, 0, ncols, 0, [[1, MG]]), in0=sb(pb, 0, ncols, 0, [[1, MG]]),
                                                  in1=sb(pb, 0, ncols, 1, [[1, MG]]), op=ALU.subtract),
                 reads=[r_pbuf[i]], writes=[r_mixtmp])
            P.op("dve", lambda e: e.scalar_tensor_tensor(
                out=out_ap, in0=sb(mixtmp, 0, ncols, 0, [[1, MG]]), scalar=col("mu", blk, 1, 0, ncols),
                in1=sb(pb, 0, ncols, 1, [[1, MG]]), op0=ALU.mult, op1=ALU.add),
                reads=[r_mixtmp, r_pbuf[i], r_cols], writes=[r_out])

        def bsum(bank, src_i):
            P.op("pe", lambda e: e.matmul(PS(bank, 0, MG), cst(C_BD), W(src_i), start=True, stop=True),
                 reads=[r_consts, r_W[src_i]], writes=psr(bank, 0, MG))

        def norm_T(g):
            for st in range(NST):
                i = cnt["t"] % 2
                cnt["t"] += 1
                tok0 = g * MG + st * 128
                P.op("sp", lambda e, i=i, tok0=tok0: e.dma_start(out=hT[i][:], in_=src_d.ap()[tok0:tok0 + 128, :]),
                     writes=[r_hT[i]], dma=True)
                P.op("act", lambda e, i=i: e.activation(out=zb[i][:], in_=hT[i][:], func=AF.Square,
                                                         accum_out=sb(stat, 0, 128, i * 4, [[1, 1]])),
                     reads=[r_hT[i]], writes=[r_zb[i], r_stat[i]])
                P.op("dve", lambda e, i=i: e.tensor_scalar(
                    out=sb(stat, 0, 128, i * 4 + 1, [[1, 1]]), in0=sb(stat, 0, 128, i * 4, [[1, 1]]),
                    scalar1=1.0 / D, scalar2=EPS, op0=ALU.mult, op1=ALU.add), reads=[r_stat[i]], writes=[r_stat[i]])
                P.op("act", lambda e, i=i: e.activation(
                    out=sb(stat, 0, 128, i * 4 + 1, [[1, 1]]), in_=sb(stat, 0, 128, i * 4 + 1, [[1, 1]]),
                    func=AF.Sqrt), reads=[r_stat[i]], writes=[r_stat[i]])
                P.op("dve", lambda e, i=i: e.reciprocal(
                    out=sb(stat, 0, 128, i * 4 + 2, [[1, 1]]), in_=sb(stat, 0, 128, i * 4 + 1, [[1, 1]])),
                    reads=[r_stat[i]], writes=[r_stat[i]])
                P.op("dve", lambda e, i=i: e.tensor_scalar(
                    out=zb[i][:], in0=hT[i][:], scalar1=sb(stat, 0, 128, i * 4 + 2, [[1, 1]]), scalar2=None,
                    op0=ALU.mult), reads=[r_hT[i], r_stat[i]], writes=[r_zb[i]])
                for kc in range(8):
                    P.op("pe", lambda e, i=i, kc=kc: e.transpose(
                        psbf(0, kc * 128, 128), sb(zb[i], 0, 128, kc * 128, [[1, 128]]), ident_bf),
                        reads=[r_zb[i], r_identb], writes=psr_bf(0, 0, 1024), inc=(kc == 7))
                for kc in range(8):
                    P.op("act", lambda e, kc=kc, st=st: e.activation(
                        out=sb(nT, 0, 128, kc * MG + st * 128, [[1, 128]]), in_=psbf(0, kc * 128, 128),
                        func=AF.Identity, scale=Gcol(1, kc), bias=shiftcol(1, kc)),
                        reads=psr_bf(0, 0, 1024) + [r_Gs, r_modT], writes=[r_nT], inc=(kc == 7))

        def QW(i):
            return sb(tmpo[i // 2], 0, 128, (i % 2) * MG, [[1, MG]])

        def qk_norm(c0, gsc, dst_ap, r_dst_, bank, bank2):
            proj_cols(c0, 128, bank)
            P.op("act", lambda e: e.activation(out=QW(0), in_=PS(bank, 0, MG), func=AF.Copy),
                 reads=psr(bank, 0, MG), writes=[r_tmpo[0]])
            P.op("act", lambda e: e.activation(out=QW(1), in_=PS(bank, 0, MG), func=AF.Square),
                 reads=psr(bank, 0, MG), writes=[r_tmpo[0]])
            P.op("pe", lambda e: e.matmul(PS(bank2, 0, MG), cst(C_BD), QW(1), start=True, stop=True),
                 reads=[r_consts, r_tmpo[0]], writes=psr(bank2, 0, MG))
            P.op("dve", lambda e: e.tensor_scalar(out=QW(2), in0=PS(bank2, 0, MG), scalar1=1.0 / 64, scalar2=1e-6,
                                                  op0=ALU.mult, op1=ALU.add), reads=psr(bank2, 0, MG), writes=[r_tmpo[1]])
            P.op("act", lambda e: e.activation(out=QW(2), in_=QW(2), func=AF.Sqrt), reads=[r_tmpo[1]], writes=[r_tmpo[1]])
            P.op("dve", lambda e: e.reciprocal(out=QW(2), in_=QW(2)), reads=[r_tmpo[1]], writes=[r_tmpo[1]])
            P.op("dve", lambda e: e.scalar_tensor_tensor(out=dst_ap, in0=QW(0), scalar=gsc, in1=QW(2), op0=ALU.mult,
                                                         op1=ALU.mult), reads=[r_tmpo[0], r_tmpo[1], r_cols, r_dcols],
                 writes=[r_dst_])

        def attn_proj(g):
            slot = (g * MG) % 1024
            for cb in range(4):
                qk_norm(1792 + cb * 128, dcol(12), sb(qT, 0, 128, cb * MG, [[1, MG]]), r_qT[cb], 1, 2)
                qk_norm(2304 + cb * 128, col("kg", 0, 1), sb(KT, 0, 128, cb * 1024 + slot, [[1, MG]]), r_KT, 1, 2)
            for st in range(NST):
                tile = (g * NST + st) % 8
                for kc in range(8):
                    P.op("pe", lambda e, kc=kc, st=st: e.matmul(
                        PS(1, 0, 512), sb(nT, 0, 128, kc * MG + st * 128, [[1, 128]]),
                        sb(winb, 0, 128, kc * NIN + 2816, [[1, 512]]), start=(kc == 0), stop=(kc == 7)),
                        reads=[r_win, r_nT], writes=psr(1, 0, 512), inc=(kc == 7))
                P.op("act", lambda e, tile=tile: e.activation(out=sb(Vr, 0, 128, tile * 512, [[1, 512]]),
                                                             in_=PS(1, 0, 512), func=AF.Copy),
                     reads=psr(1, 0, 512), writes=[r_Vr])

        def attn(g):
            for st in range(NST):
                T = g * NST + st
                jbs = [jb for jb in range(5) if T - 4 + jb >= 0]
                j0 = jbs[0]
                for cb in range(4):
                    for hh in range(2):
                        head = cb * 2 + hh
                        p0 = hh * 64
                        par = cnt["att"] % 2
                        cnt["att"] += 1
                        bS = 2 + par
                        for jb in jbs:
                            kt = (T - 4 + jb) % 8
                            if jb < 4:
                                o_ap, o_r = PS(bS, jb * 128, 128), psr(bS, jb * 128, jb * 128 + 128)
                            else:
                                o_ap, o_r = PS(4, par * 128, 128), psr(4, par * 128, par * 128 + 128)
                            P.op("pe", lambda e, o_ap=o_ap, kt=kt, cb=cb, p0=p0, st=st: e.matmul(
                                o_ap, sb(KT, p0, 64, cb * 1024 + kt * 128, [[1, 128]]),
                                sb(qT, p0, 64, cb * MG + st * 128, [[1, 128]]), start=True, stop=True),
                                reads=[r_KT, r_qT[cb]], writes=o_r)
                        if j0 < 4:
                            P.op("dve", lambda e, par=par, bS=bS, head=head, j0=j0: e.tensor_tensor(
                                out=sb(scb[par], 0, 128, j0 * 128, [[1, 512 - j0 * 128]]),
                                in0=PS(bS, j0 * 128, 512 - j0 * 128),
                                in1=sb(biasT, 0, 128, head * 640 + j0 * 128, [[1, 512 - j0 * 128]]), op=ALU.add),
                                reads=psr(bS, j0 * 128, 512) + [r_biasT], writes=[r_scb[par]])
                        P.op("dve", lambda e, par=par, head=head: e.tensor_tensor(
                            out=sb(scb[par], 0, 128, 512, [[1, 128]]), in0=PS(4, par * 128, 128),
                            in1=sb(biasT, 0, 128, head * 640 + 512, [[1, 128]]), op=ALU.add),
                            reads=psr(4, par * 128, par * 128 + 128) + [r_biasT], writes=[r_scb[par]])
                        P.op("act", lambda e, par=par, j0=j0: e.activation(
                            out=sb(prb[par], 0, 128, j0 * 128, [[1, 640 - j0 * 128]]),
                            in_=sb(scb[par], 0, 128, j0 * 128, [[1, 640 - j0 * 128]]), func=AF.Exp),
                            reads=[r_scb[par]], writes=[r_prb[par]])
                        for jb in jbs:
                            kt = (T - 4 + jb) % 8
                            P.op("pe", lambda e, jb=jb, kt=kt, cb=cb, hh=hh, par=par, f0=(jb == jbs[0]), f1=(jb == jbs[-1]): e.matmul(
                                PS(5, hh * 128, 128), sb(Vr, 0, 128, kt * 512 + cb * 128, [[1, 128]]),
                                sb(prb[par], 0, 128, jb * 128, [[1, 128]]), start=f0, stop=f1),
                                reads=[r_Vr, r_prb[par]], writes=psr(5, hh * 128, hh * 128 + 128), inc=(jb == jbs[-1]))
                        for jb in jbs:
                            P.op("pe", lambda e, jb=jb, hh=hh, par=par, f0=(jb == jbs[0]), f1=(jb == jbs[-1]): e.matmul(
                                PS(5, 256 + hh * 128, 128), onesb[:],
                                sb(prb[par], 0, 128, jb * 128, [[1, 128]]), start=f0, stop=f1),
                                reads=[r_onesb, r_prb[par]], writes=psr(5, 256 + hh * 128, 256 + hh * 128 + 128),
                                inc=(jb == jbs[-1]))
                        P.op("dve", lambda e, hh=hh, p0=p0: e.reciprocal(
                            out=sb(rec, p0, 64, 0, [[1, 128]]), in_=PS(5, 256 + hh * 128, 128, p0, 64)),
                            reads=psr(5, 256 + hh * 128, 256 + hh * 128 + 128), writes=[r_rec])
                        P.op("dve", lambda e, hh=hh, p0=p0, cb=cb, st=st: e.tensor_tensor(
                            out=sb(ymixT, p0, 64, (4 + cb) * MG + st * 128, [[1, 128]]), in0=PS(5, hh * 128, 128, p0, 64),
                            in1=sb(rec, p0, 64, 0, [[1, 128]]), op=ALU.mult),
                            reads=psr(5, hh * 128, hh * 128 + 128) + [r_rec], writes=[r_ymix[4 + cb]])

        def lora_inputs():
            mix(12, 1536, 64, 1, sb(Wt[0], 0, 64, 0, [[1, MG]]), r_W[0])
            P.op("act", lambda e: e.activation(out=sb(thb, 0, 64, 0, [[1, MG]]), in_=sb(Wt[0], 0, 64, 0, [[1, MG]]),
                                               func=AF.Tanh), reads=[r_W[0]], writes=[r_thb])
            mix(13, 1600, 64, 2, sb(Wt[1], 0, 64, 0, [[1, MG]]), r_W[1])
            P.op("act", lambda e: e.activation(out=sb(padb, 0, 64, 0, [[1, MG]]), in_=sb(Wt[1], 0, 64, 0, [[1, MG]]),
                                               func=AF.Copy), reads=[r_W[1]], writes=[r_padb])
            mix(14, 1664, 128, 1, W(2), r_W[2])
            P.op("act", lambda e: e.activation(out=W(2), in_=W(2), func=AF.Tanh, scale=0.5), reads=[r_W[2]],
                 writes=[r_W[2]])
            P.op("dve", lambda e: e.tensor_scalar(out=sgdb[:], in0=W(2), scalar1=0.5, scalar2=0.5, op0=ALU.mult,
                                                  op1=ALU.add), reads=[r_W[2]], writes=[r_sgdb])

        def v3(t, p0, np_, off, cs, n=64):
            return sb(t, p0, np_, off, [[cs, NCH], [1, n]])

        def rwkv_prep(j):
            q_ = j % 2
            fKR, fB, fK, fV, fKW, fBW = fKR2[q_], fB2[q_], fK2[q_], fV2[q_], fKW2[q_], fBW2[q_]
            r_fKR, r_fB, r_fK, r_fV, r_fKW, r_fBW = r_f2[q_]
            tV, tKW, tBW = tV2[q_], tKW2[q_], tBW2[q_]
            r_tV, r_tKW, r_tBW = r_t2[q_]
            WLt, r_WLt = WLt2[q_], r_WLt2[q_]
            Wg, r_Wg, Wbon, r_Wbon = Wg2[q_], r_Wg2[q_], Wb2[q_], r_Wb2[q_]
            mix(j, j * 128, 128, 1, W(R_), r_W[R_])
            mix(4 + j, 512 + j * 128, 128, 2, W(Kx), r_W[Kx])
            mix(8 + j, 1024 + j * 128, 128, 1, W(Vx), r_W[Vx])
            P.op("pe", lambda e: e.matmul(PS(2, 0, MG), sb(wdub, 0, 64, j * 128, [[1, 128]]),
                                          sb(thb, 0, 64, 0, [[1, MG]]), start=True, stop=True),
                 reads=[r_lora, r_thb], writes=psr(2, 0, MG))
            P.op("act", lambda e: e.activation(out=W(0), in_=PS(2, 0, MG), func=AF.Tanh, scale=0.5, bias=dcol(j)),
                 reads=psr(2, 0, MG) + [r_dcols], writes=[r_W[0]])
            P.op("dve", lambda e: e.tensor_scalar(out=W(0), in0=W(0), scalar1=0.5, scalar2=0.5, op0=ALU.mult,
                                                  op1=ALU.add), reads=[r_W[0]], writes=[r_W[0]])
            P.op("dve", lambda e: e.tensor_tensor_scan(out=W(1), data0=cst(C_MRES, MG), data1=W(0), initial=0.0,
                                                       op0=ALU.mult, op1=ALU.add), reads=[r_W[0], r_consts],
                 writes=[r_W[1]])
            P.op("act", lambda e: e.activation(out=W(2), in_=W(1), func=AF.Exp, scale=-C0), reads=[r_W[1]],
                 writes=[r_W[2]])
            P.op("act", lambda e: e.activation(out=W(3), in_=W(1), func=AF.Exp, scale=C0), reads=[r_W[1]],
                 writes=[r_W[3]])
            P.op("dve", lambda e: e.tensor_tensor(out=v3(Wt[4], 0, 128, 0, 64), in0=sb(Wt[1], 0, 128, 63, [[64, NCH], [0, 64]]),
                                                  in1=v3(Wt[1], 0, 128, 0, 64), op=ALU.subtract), reads=[r_W[1]],
                 writes=[r_W[4]])
            P.op("act", lambda e: e.activation(out=W(4), in_=W(4), func=AF.Exp, scale=-C0), reads=[r_W[4]],
                 writes=[r_W[4]])
            P.op("act", lambda e: e.activation(out=sb(WLt, 0, 128, 0, [[1, NCH]]), in_=sb(Wt[1], 0, 128, 63, [[64, NCH]]),
                                               func=AF.Exp, scale=-C0), reads=[r_W[1]], writes=[r_WLt])
            P.op("pe", lambda e: e.matmul(PS(1, 0, MG), sb(waub, 0, 64, j * 128, [[1, 128]]),
                                          sb(padb, 0, 64, 0, [[1, MG]]), start=True, stop=True),
                 reads=[r_lora, r_padb], writes=psr(1, 0, MG))
            P.op("act", lambda e: e.activation(out=W(5), in_=PS(1, 0, MG), func=AF.Tanh, scale=0.5, bias=dcol(4 + j)),
                 reads=psr(1, 0, MG) + [r_dcols], writes=[r_W[5]])
            P.op("dve", lambda e: e.tensor_scalar(out=W(5), in0=W(5), scalar1=0.5, scalar2=0.5, op0=ALU.mult,
                                                  op1=ALU.add), reads=[r_W[5]], writes=[r_W[5]])
            P.op("pe", lambda e: e.matmul(PS(2, 0, MG), sb(wgub, 0, 128, j * 128, [[1, 128]]), sgdb[:],
                                          start=True, stop=True), reads=[r_lora, r_sgdb], writes=psr(2, 0, MG))
            P.op("act", lambda e: e.activation(out=Wg[:], in_=PS(2, 0, MG), func=AF.Copy), reads=psr(2, 0, MG),
                 writes=[r_Wg])
            P.op("act", lambda e: e.activation(out=W(7), in_=W(Kx), func=AF.Copy, scale=col("k_k", j)),
                 reads=[r_W[Kx], r_cols], writes=[r_W[7]])
            P.op("pool", lambda e: e.tensor_tensor(out=W(8), in0=W(7), in1=W(7), op=ALU.mult), reads=[r_W[7]],
                 writes=[r_W[8]])
            bsum(1, 8)
            P.op("dve", lambda e: e.tensor_scalar(out=W(8), in0=PS(1, 0, MG), scalar1=1e-24, scalar2=None, op0=ALU.max),
                 reads=psr(1, 0, MG), writes=[r_W[8]])
            P.op("act", lambda e: e.activation(out=W(8), in_=W(8), func=AF.Sqrt), reads=[r_W[8]], writes=[r_W[8]])
            P.op("dve", lambda e: e.reciprocal(out=W(8), in_=W(8)), reads=[r_W[8]], writes=[r_W[8]])
            P.op("pool", lambda e: e.tensor_tensor(out=W(7), in0=W(7), in1=W(8), op=ALU.mult), reads=[r_W[7], r_W[8]],
                 writes=[r_W[7]])
            P.op("dve", lambda e: e.tensor_scalar(out=W(0), in0=W(5), scalar1=col("k_a", j), scalar2=dcol(8 + j),
                                                  op0=ALU.mult, op1=ALU.add), reads=[r_W[5], r_cols, r_dcols],
                 writes=[r_W[0]])
            P.op("pool", lambda e: e.tensor_tensor(out=W(9), in0=W(Kx), in1=W(0), op=ALU.mult), reads=[r_W[Kx], r_W[0]],
                 writes=[r_W[9]])
            P.op("pool", lambda e: e.tensor_tensor(out=W(10), in0=W(7), in1=W(5), op=ALU.mult), reads=[r_W[7], r_W[5]],
                 writes=[r_W[10]])
            P.op("dve", lambda e: e.scalar_tensor_tensor(out=mixtmp[:], in0=W(R_), scalar=col("r_k", j), in1=W(9),
                                                         op0=ALU.mult, op1=ALU.mult), reads=[r_W[R_], r_W[9], r_cols],
                 writes=[r_mixtmp])
            P.op("pe", lambda e: e.matmul(PS(2, 0, MG), cst(C_BD), mixtmp[:], start=True, stop=True),
                 reads=[r_consts, r_mixtmp], writes=psr(2, 0, MG))
            P.op("dve", lambda e: e.tensor_tensor(out=Wbon[:], in0=PS(2, 0, MG), in1=W(Vx), op=ALU.mult),
                 reads=psr(2, 0, MG) + [r_W[Vx]], writes=[r_Wbon])
            k_ = 0
            for hh in range(2):
                p0 = hh * 64
                specs = [
                    (fKR, 256, 128 + hh * 64, r_fKR, R_, 2),
                    (fB, 128, hh * 64, r_fB, 10, 3),
                    (fK, 128, hh * 64, r_fK, 9, 3),
                    (fKW, 128, hh * 64, r_fKW, 9, 4),
                    (fBW, 128, hh * 64, r_fBW, 10, 4),
                ]
                for (dst, cs, off, rd, a_i, b_i) in specs:
                    eng = ("dve", "pool")[k_ % 2]
                    k_ += 1
                    P.op(eng, lambda e, dst=dst, cs=cs, off=off, a_i=a_i, b_i=b_i, p0=p0: e.tensor_tensor(
                        out=v3(dst, p0, 64, off, cs), in0=v3(Wt[a_i], p0, 64, 0, 64), in1=v3(Wt[b_i], p0, 64, 0, 64),
                        op=ALU.mult), reads=[r_W[a_i], r_W[b_i]], writes=[rd])
                P.op("pool", lambda e, p0=p0, hh=hh: e.tensor_copy(out=v3(fV, p0, 64, hh * 64, 128),
                                                                   in_=v3(Wt[Vx], p0, 64, 0, 64)),
                     reads=[r_W[Vx]], writes=[r_fV])
                P.op("dve", lambda e, p0=p0, hh=hh: e.tensor_tensor(
                    out=v3(fKR, p0, 64, hh * 64 + 1, 256, 63), in0=v3(Wt[7], p0, 64, 1, 64, 63),
                    in1=v3(Wt[2], p0, 64, 0, 64, 63), op=ALU.mult), reads=[r_W[7], r_W[2]], writes=[r_fKR])
                P.op("pool", lambda e, p0=p0, hh=hh: e.tensor_copy(
                    out=v3(fKR, p0, 64, hh * 64, 256, 1), in_=v3(Wt[7], p0, 64, 0, 64, 1)),
                    reads=[r_W[7]], writes=[r_fKR])
            for (src, rs, dst, rd, bank, c0) in ((fV, r_fV, tV, r_tV, 1, 0), (fKW, r_fKW, tKW, r_tKW, 1, 512),
                                                 (fBW, r_fBW, tBW, r_tBW, 2, 0)):
                for c in range(NCH):
                    P.op("pe", lambda e, src=src, c=c, bank=bank, c0=c0: e.transpose(
                        psbf(bank, c0 + c * 128, 128), sb(src, 0, 128, c * 128, [[1, 128]]), ident_bf),
                        reads=[rs, r_identb], writes=psr_bf(bank, c0, c0 + 512), inc=(c == NCH - 1))
                P.op("act", lambda e, dst=dst, bank=bank, c0=c0: e.activation(
                    out=dst[:], in_=psbf(bank, c0, NCH * 128), func=AF.Copy),
                    reads=psr_bf(bank, c0, c0 + 512), writes=[rd])

        def rwkv_chunks(j):
            q_ = j % 2
            fKR, fB, fK, fV, fKW, fBW = fKR2[q_], fB2[q_], fK2[q_], fV2[q_], fKW2[q_], fBW2[q_]
            r_fKR, r_fB, r_fK, r_fV, r_fKW, r_fBW = r_f2[q_]
            tV, tKW, tBW = tV2[q_], tKW2[q_], tBW2[q_]
            r_tV, r_tKW, r_tBW = r_t2[q_]
            WLt, r_WLt = WLt2[q_], r_WLt2[q_]
            Wg, r_Wg, Wbon, r_Wbon = Wg2[q_], r_Wg2[q_], Wb2[q_], r_Wb2[q_]
            def fKA(c):
                return sb(fKR, 0, 128, c * 256, [[1, 128]])

            def fR(c):
                return sb(fKR, 0, 128, c * 256 + 128, [[1, 128]])

            def blk(t, c, w=128, o=0):
                return sb(t, 0, 128, c * w + o, [[1, 128]])

            for c in range(NCH):
                P.op("pe", lambda e, c=c: e.matmul(PS(3, c * 128, 128), fKA(c), blk(fB, c), start=True, stop=True),
                     reads=[r_fKR, r_fB], writes=psr(3, c * 128, c * 128 + 128))
            P.op("dve", lambda e: e.tensor_tensor(
                out=sb(Nn, 0, 128, 0, [[128, NCH], [1, 128]]), in0=sb(psum[3], 0, 128, 0, [[128, NCH], [1, 128]]),
                in1=sb(consts, 0, 128, C_NDL, [[0, NCH], [1, 128]]), op=ALU.mult),
                reads=psr(3, 0, 512) + [r_consts], writes=[r_Nn])
            P.op("dve", lambda e: e.tensor_tensor(
                out=sb(Esb, 0, 128, 0, [[128, NCH], [1, 128]]), in0=sb(psum[3], 0, 128, 0, [[128, NCH], [1, 128]]),
                in1=sb(consts, 0, 128, C_EM, [[0, NCH], [1, 128]]), op=ALU.mult),
                reads=psr(3, 0, 512) + [r_consts], writes=[r_Esb])
            for (lh, rl, dst, rd, b0, cm) in ((fB, r_fB, NtArb, r_NtArb, 4, C_NDU), (fK, r_fK, AkArk, r_AkArk, 6, C_NSU)):
                for c in range(NCH):
                    P.op("pe", lambda e, c=c, lh=lh, b0=b0: e.matmul(
                        PS(b0 + c // 2, (c % 2) * 256, 256), blk(lh, c), sb(fKR, 0, 128, c * 256, [[1, 256]]),
                        start=True, stop=True), reads=[rl, r_fKR],
                        writes=psr(b0 + c // 2, (c % 2) * 256, (c % 2) * 256 + 256))
                for hb in range(NCH // 2):
                    P.op("dve", lambda e, hb=hb, dst=dst, b0=b0, cm=cm: e.tensor_tensor(
                        out=sb(dst, 0, 128, hb * 512, [[256, 2], [1, 256]]),
                        in0=sb(psum[b0 + hb], 0, 128, 0, [[256, 2], [1, 256]]),
                        in1=sb(consts, 0, 128, cm, [[0, 2], [1, 256]]), op=ALU.mult),
                        reads=psr(b0 + hb, 0, 512) + [r_consts], writes=[rd])

            def XP(c, o, n):
                return PS(3 + c, o, n)

            def XR(c):
                return [r_ps[3 + c]]

            def Rb(c, o, n):
                return sb(Xb, 0, 128, c * 384 + o, [[1, n]])

            for c in range(NCH):
                P.op("pe", lambda e, c=c: e.matmul(XP(c, 0, 128), blk(AkArk, c, 256), blk(tV, c), start=True,
                                                   stop=True, skip_group_check=True),
                     reads=[r_AkArk, r_tV], writes=XR(c))
                P.op("pe", lambda e, c=c: e.matmul(XP(c, 128, 128), fKA(c), ident_bf, start=False, stop=True,
                                                   skip_group_check=True),
                     reads=[r_fKR, r_identb], writes=XR(c))
                P.op("pe", lambda e, c=c: e.matmul(XP(c, 256, 128), ident_bf, blk(Esb, c), start=False, stop=True,
                                                   skip_group_check=True),
                     reads=[r_Esb, r_identb], writes=XR(c))
            Pc = [blk(NtArb, c, 256) for c in range(NCH)]
            PTc = [blk(Nn, c) for c in range(NCH)]
            rP, rPT = [r_NtArb], [r_Nn]

            def pbank(c):
                return (7, 2)[c // 2]

            for lvl in range(4):
                for c in range(NCH):
                    P.op("act", lambda e, c=c: e.activation(out=Rb(c, 0, 384), in_=XP(c, 0, 384), func=AF.Copy),
                         reads=XR(c), writes=[r_Xb])
                if lvl >= 1:
                    pp = PP[lvl % 2]
                    for c in range(NCH):
                        P.op("pe", lambda e, c=c, Pc=Pc, PTc=PTc: e.matmul(
                            PS(pbank(c), (c % 2) * 256, 128), PTc[c], Pc[c], start=True, stop=True),
                            reads=rP + rPT, writes=[r_ps[pbank(c)]])
                        P.op("pe", lambda e, c=c, Pc=Pc, PTc=PTc: e.matmul(
                            PS(pbank(c), (c % 2) * 256 + 128, 128), Pc[c], PTc[c], start=True, stop=True),
                            reads=rP + rPT, writes=[r_ps[pbank(c)]])
                    for hb in range(NCH // 2):
                        P.op("dve", lambda e, hb=hb, pp=pp: e.tensor_copy(out=sb(pp, 0, 128, hb * 512, [[1, 512]]),
                                                                         in_=PS((7, 2)[hb], 0, 512)),
                             reads=[r_ps[(7, 2)[hb]]], writes=[r_PP[lvl % 2]])
                    Pc = [blk(pp, c, 256) for c in range(NCH)]
                    PTc = [blk(pp, c, 256, 128) for c in range(NCH)]
                    rP = rPT = [r_PP[lvl % 2]]
                for c in range(NCH):
                    P.op("pe", lambda e, c=c, Pc=Pc: e.matmul(XP(c, 0, 384), Pc[c], Rb(c, 0, 384),
                                                              start=False, stop=True, skip_group_check=True),
                         reads=rP + [r_Xb], writes=XR(c))
            for c in range(NCH):
                P.op("act", lambda e, c=c: e.activation(out=Rb(c, 0, 384), in_=XP(c, 0, 384), func=AF.Copy),
                     reads=XR(c), writes=[r_Xb])
            for c in range(NCH):
                P.op("pe", lambda e, c=c: e.transpose(psbf(1, c * 128, 128), Rb(c, 256, 128), ident_bf),
                     reads=[r_Xb, r_identb], writes=[r_ps[1]], inc=(c == NCH - 1))
            P.op("act", lambda e: e.activation(out=nGT[:], in_=psbf(1, 0, NCH * 128), func=AF.Copy, scale=-1.0),
                 reads=[r_ps[1]], writes=[r_nGT])
            P.op("dve", lambda e: e.tensor_tensor(
                out=sb(nGTmI, 0, 128, 0, [[128, NCH], [1, 128]]), in0=sb(nGT, 0, 128, 0, [[128, NCH], [1, 128]]),
                in1=sb(consts, 0, 128, 0, [[0, NCH], [1, 128]]), op=ALU.subtract),
                reads=[r_nGT, r_consts], writes=[r_nGTmI])
            for c in range(NCH):
                P.op("pe", lambda e, c=c: e.matmul(XP(c, 0, 256), blk(nGT, c), Rb(c, 0, 256), start=False, stop=True,
                                                   skip_group_check=True), reads=[r_nGT, r_Xb], writes=XR(c))
            for it in range(2):
                for c in range(NCH):
                    P.op("act", lambda e, c=c: e.activation(out=sb(Xit, 0, 128, c * 256, [[1, 256]]), in_=XP(c, 0, 256),
                                                            func=AF.Copy), reads=XR(c), writes=[r_Xit])
                for c in range(NCH):
                    P.op("pe", lambda e, c=c: e.matmul(XP(c, 0, 256), blk(nGTmI, c), sb(Xit, 0, 128, c * 256, [[1, 256]]),
                                                       start=False, stop=True, skip_group_check=True),
                         reads=[r_nGTmI, r_Xit], writes=XR(c))
                    P.op("pe", lambda e, c=c: e.matmul(XP(c, 0, 256), ident_bf, Rb(c, 0, 256),
                                                       start=False, stop=True, skip_group_check=True),
                         reads=[r_identb, r_Xb], writes=XR(c))
            for c in range(NCH):
                P.op("act", lambda e, c=c: e.activation(out=sb(Xit, 0, 128, c * 256, [[1, 256]]), in_=XP(c, 0, 256),
                                                        func=AF.Copy), reads=XR(c), writes=[r_Xit])

            def Ul(c):
                return sb(Xit, 0, 128, c * 256, [[1, 128]])

            def Qc(c):
                return sb(Xit, 0, 128, c * 256 + 128, [[1, 128]])

            def ArbT(c):
                return sb(NtArb, 0, 128, c * 256 + 128, [[1, 128]])

            def ArkT(c):
                return sb(AkArk, 0, 128, c * 256 + 128, [[1, 128]])

            for c in range(NCH):
                P.op("pe", lambda e, c=c: e.matmul(PS(7, c * 128, 128), Qc(c), ArbT(c), start=True, stop=True),
                     reads=[r_Xit, r_NtArb], writes=psr(7, c * 128, c * 128 + 128))
            P.op("dve", lambda e: e.tensor_tensor(
                out=sb(Reff, 0, 128, 0, [[128, NCH], [1, 128]]), in0=sb(fKR, 0, 128, 128, [[256, NCH], [1, 128]]),
                in1=sb(psum[7], 0, 128, 0, [[128, NCH], [1, 128]]), op=ALU.subtract),
                reads=psr(7, 0, 512) + [r_fKR], writes=[r_Reff])
            for c in range(NCH):
                P.op("pe", lambda e, c=c: e.matmul(PS(2, c * 128, 128), Qc(c), blk(tBW, c), start=True, stop=True),
                     reads=[r_Xit, r_tBW], writes=psr(2, c * 128, c * 128 + 128))
            for c in range(NCH):
                P.op("dve", lambda e, c=c: e.scalar_tensor_tensor(
                    out=blk(Mc, c), in0=ident, scalar=sb(WLt, 0, 128, c, [[1, 1]]), in1=PS(2, c * 128, 128),
                    op0=ALU.mult, op1=ALU.subtract), reads=psr(2, c * 128, c * 128 + 128) + [r_WLt, r_consts],
                    writes=[r_Mc])
            Zj = sb(Zb, 0, 128, j * 128, [[1, 128]])
            for c in range(NCH):
                yo, yr = PS(3, c * 128, 128), psr(3, c * 128, c * 128 + 128)
                P.op("pe", lambda e, c=c, yo=yo: e.matmul(yo, Ul(c), ArbT(c), start=True, stop=False),
                     reads=[r_Xit, r_NtArb], writes=yr, inc=False)
                P.op("pe", lambda e, c=c, yo=yo: e.matmul(yo, blk(tV, c), ArkT(c), start=False, stop=False),
                     reads=[r_tV, r_AkArk], writes=yr, inc=False)
                P.op("pe", lambda e, c=c, yo=yo: e.matmul(yo, Zj, blk(Reff, c), start=False, stop=True),
                     reads=[r_Zb[j], r_Reff], writes=yr)
                zo, zr = PS(4, (c % 2) * 128, 128), psr(4, (c % 2) * 128, (c % 2) * 128 + 128)
                P.op("pe", lambda e, c=c, zo=zo: e.matmul(zo, blk(tBW, c), Ul(c), start=True, stop=False),
                     reads=[r_tBW, r_Xit], writes=zr, inc=False)
                P.op("pe", lambda e, c=c, zo=zo: e.matmul(zo, blk(tKW, c), blk(tV, c), start=False, stop=False),
                     reads=[r_tKW, r_tV], writes=zr, inc=False)
                P.op("pe", lambda e, c=c, zo=zo: e.matmul(zo, blk(Mc, c), Zj, start=False, stop=True),
                     reads=[r_Mc, r_Zb[j]], writes=zr)
                P.op("act", lambda e, zo=zo: e.activation(out=Zj, in_=zo, func=AF.Copy), reads=zr, writes=[r_Zb[j]])
                for hh in range(2):
                    p0 = hh * 64
                    P.op("dve", lambda e, c=c, p0=p0, hh=hh: e.tensor_copy(
                        out=sb(yT, p0, 64, c * 64, [[1, 64]]), in_=PS(3, c * 128 + hh * 64, 64, p0, 64)),
                        reads=yr, writes=[r_yT])

        def rwkv_out(j):
            q_ = j % 2
            fKR, fB, fK, fV, fKW, fBW = fKR2[q_], fB2[q_], fK2[q_], fV2[q_], fKW2[q_], fBW2[q_]
            r_fKR, r_fB, r_fK, r_fV, r_fKW, r_fBW = r_f2[q_]
            tV, tKW, tBW = tV2[q_], tKW2[q_], tBW2[q_]
            r_tV, r_tKW, r_tBW = r_t2[q_]
            WLt, r_WLt = WLt2[q_], r_WLt2[q_]
            Wg, r_Wg, Wbon, r_Wbon = Wg2[q_], r_Wg2[q_], Wb2[q_], r_Wb2[q_]
            O0, O1 = O2[0][:], O2[1][:]
            P.op("pe", lambda e: e.matmul(PS(1, 0, MG), cst(C_BD), yT[:], start=True, stop=True),
                 reads=[r_consts, r_yT], writes=psr(1, 0, MG))
            P.op("act", lambda e: e.activation(out=O1, in_=yT[:], func=AF.Square), reads=[r_yT], writes=[r_O2[1]])
            P.op("pe", lambda e: e.matmul(PS(2, 0, MG), cst(C_BD), O1, start=True, stop=True),
                 reads=[r_consts, r_O2[1]], writes=psr(2, 0, MG))
            P.op("dve", lambda e: e.tensor_scalar(out=O0, in0=PS(1, 0, MG), scalar1=1.0 / 64, scalar2=None,
                                                  op0=ALU.mult), reads=psr(1, 0, MG), writes=[r_O2[0]])
            P.op("dve", lambda e: e.tensor_tensor(out=O1, in0=O0, in1=O0, op=ALU.mult), reads=[r_O2[0]],
                 writes=[r_O2[1]])
            P.op("dve", lambda e: e.scalar_tensor_tensor(out=O1, in0=PS(2, 0, MG), scalar=1.0 / 64, in1=O1,
                                                         op0=ALU.mult, op1=ALU.subtract),
                 reads=psr(2, 0, MG) + [r_O2[1]], writes=[r_O2[1]])
            P.op("act", lambda e: e.activation(out=O1, in_=O1, func=AF.Sqrt, bias=sb(dcols, 0, 128, 13, [[1, 1]])),
                 reads=[r_O2[1], r_dcols], writes=[r_O2[1]])
            P.op("dve", lambda e: e.reciprocal(out=O1, in_=O1), reads=[r_O2[1]], writes=[r_O2[1]])
            P.op("dve", lambda e: e.tensor_tensor(out=yT[:], in0=yT[:], in1=O0, op=ALU.subtract),
                 reads=[r_yT, r_O2[0]], writes=[r_yT])
            P.op("dve", lambda e: e.tensor_tensor(out=yT[:], in0=yT[:], in1=O1, op=ALU.mult),
                 reads=[r_yT, r_O2[1]], writes=[r_yT])
            P.op("act", lambda e: e.activation(out=yT[:], in_=yT[:], func=AF.Identity, scale=col("lnx_g", j),
                                               bias=col("lnx_b", j)), reads=[r_yT, r_cols], writes=[r_yT])
            P.op("dve", lambda e: e.tensor_tensor(out=yT[:], in0=yT[:], in1=Wbon[:], op=ALU.add),
                 reads=[r_yT, r_Wbon], writes=[r_yT])
            P.op("dve", lambda e: e.tensor_tensor(out=sb(ymixT, 0, 128, j * MG, [[1, MG]]), in0=yT[:], in1=Wg[:],
                                                  op=ALU.mult), reads=[r_yT, r_Wg], writes=[r_ymix[j]])

        def out_proj(g):
            for st in range(NST):
                i = cnt["t"] % 2
                cnt["t"] += 1
                tok0 = g * MG + st * 128
                P.op("sp", lambda e, i=i, tok0=tok0: e.dma_start(out=hT[i][:], in_=src_d.ap()[tok0:tok0 + 128, :]),
                     writes=[r_hT[i]], dma=True)
                for half in range(2):
                    bank = 6 + half
                    for kc in range(8):
                        P.op("pe", lambda e, kc=kc, st=st, half=half, bank=bank: e.matmul(
                            PS(bank, 0, 512), sb(ymixT, 0, 128, kc * MG + st * 128, [[1, 128]]),
                            sb(woutb, 0, 128, kc * D + half * 512, [[1, 512]]), start=(kc == 0), stop=(kc == 7)),
                            reads=[r_ymix[kc], r_wout], writes=psr(bank, 0, 512), inc=(kc == 7))
                    P.op("dve", lambda e, half=half, bank=bank: e.tensor_tensor(
                        out=tmpo[half][:], in0=PS(bank, 0, 512), in1=sb(gate_t, 0, 128, half * 512, [[1, 512]]),
                        op=ALU.mult), reads=psr(bank, 0, 512) + [r_gate_t], writes=[r_tmpo[half]])
                    P.op("pool", lambda e, i=i, half=half: e.tensor_tensor(
                        out=sb(hT[i], 0, 128, half * 512, [[1, 512]]), in0=sb(hT[i], 0, 128, half * 512, [[1, 512]]),
                        in1=tmpo[half][:], op=ALU.add), reads=[r_tmpo[half], r_hT[i]], writes=[r_hT[i]])
                P.op("sp", lambda e, i=i, tok0=tok0: e.dma_start(out=dst_d.ap()[tok0:tok0 + 128, :], in_=hT[i][:]),
                     reads=[r_hT[i]], writes=[r_dst], dma=True)

        if "no_attn" in flags or "no_rwkv" in flags:
            P.op("pool", lambda e: e.memset(ymixT[:], 0.0), writes=r_ymix)
        for g in range(NGm):
            norm_T(g)
            if "no_attn" not in flags:
                attn_proj(g)
                attn(g)
            if "no_rwkv" not in flags:
                lora_inputs()
                for j in range(4):
                    rwkv_prep(j)
                    rwkv_chunks(j)
                    rwkv_out(j)
            out_proj(g)
        A.release()
        P.barrier()

    ident_bf_t = A.alloc("identb", 128, BF16)
    ident_bf = ident_bf_t[:]
    r_identb = Res("identb")
    P.op("dve", lambda e: e.tensor_copy(out=ident_bf, in_=ident), reads=[r_consts], writes=[r_identb])
    r_dst = Res("dst")
    epsc = A.alloc("epsc", 4, F32)
    r_epsc = Res("epsc")
    P.op("pool", lambda e: e.memset(sb(epsc, 0, 128, 0, [[1, 1]]), EPS), writes=[r_epsc])

    if upto == "ffn1":
        ffn_phase(0, x_d, out_d, f1w1_d, f1w3_d, f1w2_d)
    elif upto == "mixer":
        ffn_phase(0, x_d, h1_d, f1w1_d, f1w3_d, f1w2_d)
        mixer_phase(h1_d, out_d)
    else:
        ffn_phase(0, x_d, h1_d, f1w1_d, f1w3_d, f1w2_d)
        mixer_phase(h1_d, h2_d)
        ffn_phase(2, h2_d, out_d, f2w1_d, f2w3_d, f2w2_d)

    P.emit(reorder=("no_reorder" not in flags))
    P.names = A.names
    return nc, P


def _kc_layout(w):
    Kd, N = w.shape
    return np.ascontiguousarray(w.reshape(Kd // 128, 128, N).transpose(1, 0, 2).reshape(128, (Kd // 128) * N))


def _colvec(v):
    v = np.asarray(v, np.float32).reshape(-1)
    return np.ascontiguousarray(v.reshape(-1, 128).T)


def make_consts():
    c = np.zeros((128, NCONST), np.float32)
    c[:, 0:128] = np.eye(128, dtype=np.float32)
    c[:, 128:256] = 1.0
    p = np.arange(128)
    same = (p[:, None] // 64) == (p[None, :] // 64)
    row, colm = p[:, None] % 64, p[None, :] % 64
    same16 = (p[:, None] // 16) == (p[None, :] // 16)
    c[:, C_NDL:C_NDL + 128] = -(same16 & (row > colm)).astype(np.float32)
    c[:, C_NDU:C_NDU + 128] = -(same16 & (row < colm)).astype(np.float32)
    c[:, C_NSU:C_NSU + 128] = -(same & (row < colm)).astype(np.float32)
    c[:, C_UI:C_UI + 128] = (same & (row <= colm)).astype(np.float32)
    c[:, C_UI2:C_UI2 + 128] = (same & (row <= colm)).astype(np.float32)
    c[:, C_EM:C_EM + 128] = (same & ((row // 16) > (colm // 16))).astype(np.float32)
    c[:, C_BD:C_BD + 128] = same.astype(np.float32)
    c[:, C_MRES:C_MRES + MG] = (np.arange(MG) % 64 != 0).astype(np.float32)[None, :]
    return c


def make_bias_table(rel_bias):
    p = np.arange(128)[:, None, None]
    jb = np.arange(5)[None, :, None]
    qi = np.arange(128)[None, None, :]
    kpos = (jb - 4) * 128 + p
    dist = qi - kpos
    dch = qi // 64 - np.floor_divide(kpos, 64)
    valid = (dch >= 0) & (dch <= 8)
    idx = np.clip(dist, -256, 256) + 256
    rb = np.asarray(rel_bias, np.float32)
    tab = rb[:, idx]
    tab = np.where(valid[None], tab, np.float32(NEG)).astype(np.float32)
    return np.ascontiguousarray(tab.transpose(1, 0, 2, 3).reshape(128, 8 * 640))


def make_core_inputs(b, inp):
    cols = np.zeros((128, NCOL), np.float32)

    def put(name, v):
        o, w = COLS[name]
        cols[:, o:o + w] = v

    put("c", _colvec(inp["c"][b]))
    put("b_ada", _colvec(inp["b_ada"][0]))
    put("n1g", _colvec(inp["norm1_g"][0]))
    put("n2g", _colvec(inp["norm2_g"][0]))
    put("n3g", _colvec(inp["norm3_g"][0]))
    mu = np.asarray(inp["mu_shift"][0], np.float32)
    mucols = np.zeros((128, 15), np.float32)
    mucols[:, 0:12] = _colvec(mu[0:1536])
    mucols[0:64, 12] = mu[1536:1600]
    mucols[0:64, 13] = mu[1600:1664]
    mucols[:, 14] = mu[1664:1792]
    put("mu", mucols)
    for nm, key in (("w0", "w0"), ("a0", "a0"), ("k_k", "k_k"), ("k_a", "k_a"), ("r_k", "r_k"), ("lnx_g", "lnx_g"),
                    ("lnx_b", "lnx_b")):
        put(nm, _colvec(inp[key][0]))
    put("qg", np.tile(np.asarray(inp["q_norm_g"][0], np.float32), 2)[:, None])
    put("kg", np.tile(np.asarray(inp["k_norm_g"][0], np.float32), 2)[:, None])
    m = {
        "x": np.ascontiguousarray(inp["x"][b]),
        "wada": np.ascontiguousarray(inp["w_ada"][0].reshape(8, 128, 9 * D).transpose(1, 0, 2)),
        "cols": cols,
        "consts": make_consts(),
        "f1w1": _kc_layout(inp["ffn1_w1"][0]),
        "f1w3": _kc_layout(inp["ffn1_w3"][0]),
        "f1w2": _kc_layout(inp["ffn1_w2"][0]),
        "f2w1": _kc_layout(inp["ffn2_w1"][0]),
        "f2w3": _kc_layout(inp["ffn2_w3"][0]),
        "f2w2": _kc_layout(inp["ffn2_w2"][0]),
        "win": _kc_layout(inp["w_in"][0]),
        "wout": _kc_layout(inp["w_out"][0]),
        "wdu": np.ascontiguousarray(inp["w_decay_up"][0]),
        "wau": np.ascontiguousarray(inp["w_a_up"][0]),
        "wgu": np.ascontiguousarray(inp["w_g_up"][0]),
        "biasT": make_bias_table(inp["rel_bias"][0]),
    }
    return m


def kernel(**inp):
    inp = {k: np.asarray(v) for k, v in inp.items()}
    B, S, _ = inp["x"].shape
    nc, P = build(S)
    in_maps = [make_core_inputs(b % B, inp) for b in range(8)]
    res = run_bass_kernel_spmd(nc, in_maps, core_ids=list(range(8)))
    out = np.stack([res.results[b]["out"] for b in range(B)], axis=0)
    return out.astype(np.float32)
```
